# Optimizing a Trainium2 kernel written in Bass

```python
import jax, jax.numpy as jnp
from jax import lax
import numpy as np

D_MODEL = 2048
BATCH = 4
SEQ = 2048
DEPTH = 2
DEC_BATCH = 128
DEC_SEQ = 1
PAST_LEN = 16384
PAGE_SIZE = 128

MIX_W = D_MODEL
HG_W = MIX_W // 2
HG_HEADS = 8
HG_VDIM = HG_W // HG_HEADS
HG_EXPAND = 128
HG_FDIM = HG_HEADS * HG_EXPAND
SC_W = MIX_W // 4
SC_WIDTH = 3
CC_W = MIX_W - HG_W - SC_W
CC_WIDTH = 31
D_FF = 4 * D_MODEL
CHUNK = 64
EPS = 1e-6
F_FLOOR = 1e-30
IN_SPLITS = (HG_FDIM, HG_FDIM, HG_W, HG_W, SC_W, SC_W, SC_W, CC_W, CC_W)
IN_COLS = sum(IN_SPLITS)

kernel_name = "hymba_style_hgrn2_shortconv_conformer_decoder_step"


def rmsnorm(x, g):
    xf = x.astype(jnp.float32)
    y = xf * lax.rsqrt(jnp.mean(xf * xf, axis=-1, keepdims=True) + EPS)
    return (y * g.astype(jnp.float32)).astype(x.dtype)


def layernorm(x, g, b):
    xf = x.astype(jnp.float32)
    mu = jnp.mean(xf, axis=-1, keepdims=True)
    var = jnp.mean(jnp.square(xf - mu), axis=-1, keepdims=True)
    y = (xf - mu) * lax.rsqrt(var + EPS)
    return (y * g.astype(jnp.float32) + b.astype(jnp.float32)).astype(x.dtype)


def causal_dwconv(u, buf, w):
    width = w.shape[0]
    full = jnp.concatenate([buf.astype(u.dtype), u], axis=1)
    out = lax.conv_general_dilated(full, w[:, None, :].astype(u.dtype), window_strides=(1,),
                                   padding='VALID', dimension_numbers=('NWC', 'WIO', 'NWC'),
                                   feature_group_count=u.shape[-1])
    return out, full[:, full.shape[1] - (width - 1):]


def hgrn2_chunked(q, k, v, log_f, s0):
    n, l = q.shape[0], q.shape[1]
    c = min(CHUNK, l)
    pad = (-l) % c
    padw = ((0, 0), (0, pad), (0, 0), (0, 0))
    q, k, v, log_f = (jnp.pad(a, padw) for a in (q, k, v, log_f))
    nc = (l + pad) // c

    def to_chunks(a):
        return a.reshape(n, nc, c, a.shape[2], a.shape[3]).transpose(1, 0, 3, 2, 4)

    causal = jnp.tril(jnp.ones((c, c), dtype=bool))[:, :, None]

    def step(s, inp):
        qc, kc, vc, gc = inp
        b = jnp.cumsum(gc, axis=2)
        o_inter = jnp.einsum('nhck,nhkv->nhcv', qc * jnp.exp(b), s)
        diff = b[:, :, :, None, :] - b[:, :, None, :, :]
        decay = jnp.where(causal, jnp.exp(jnp.minimum(diff, 0.0)), 0.0)
        att = jnp.einsum('nhtk,nhtsk,nhsk->nhts', qc, decay, kc)
        o = o_inter + jnp.einsum('nhts,nhsv->nhtv', att, vc)
        b_last = b[:, :, -1:, :]
        s = jnp.exp(b_last[:, :, 0, :])[..., None] * s + jnp.einsum(
            'nhsk,nhsv->nhkv', kc * jnp.exp(b_last - b), vc)
        return s, o

    s_fin, o = lax.scan(step, s0, (to_chunks(q), to_chunks(k), to_chunks(v), to_chunks(log_f)))
    o = o.transpose(1, 0, 3, 2, 4).reshape(n, nc * c, q.shape[2], v.shape[3])[:, :l]
    return o, s_fin


def trunk_layer(x, st_h, st_s, st_c, lb, g_mix, w_in, hg_g, sc_w, cc_w, cc_b, cc_lg, cc_lb,
                w_out, g_mlp, w_up, w_down):
    n, l, _ = x.shape
    f32 = jnp.float32
    h = rmsnorm(x, g_mix)
    z = h @ w_in.astype(h.dtype)
    idx = list(np.cumsum(IN_SPLITS)[:-1])
    q, f_pre, i_v, og, sc_b, sc_c, sc_h, cc_v, cc_g = jnp.split(z, idx, axis=-1)

    q = jax.nn.silu(q.astype(f32)).reshape(n, l, HG_HEADS, HG_EXPAND)
    f_pre = f_pre.astype(f32).reshape(n, l, HG_HEADS, HG_EXPAND)
    lbh = lb.astype(f32).reshape(HG_HEADS, HG_EXPAND)
    sig = jax.nn.sigmoid(f_pre)
    f = lbh + (1.0 - lbh) * sig
    log_f = jnp.log(jnp.maximum(f, F_FLOOR))
    k = (1.0 - lbh) * (1.0 - sig)
    v = i_v.astype(f32).reshape(n, l, HG_HEADS, HG_VDIM)
    o, new_h = hgrn2_chunked(q, k, v, log_f, st_h.astype(f32))
    o = o * lax.rsqrt(jnp.mean(o * o, axis=-1, keepdims=True) + EPS)
    o = o * hg_g.astype(f32).reshape(HG_HEADS, HG_VDIM)
    y_h = (o.reshape(n, l, HG_W) * jax.nn.silu(og.astype(f32))).astype(x.dtype)

    u = sc_c * sc_h
    conv_s, new_s = causal_dwconv(u, st_s, sc_w)
    y_s = sc_b * conv_s

    a = cc_v * jax.nn.sigmoid(cc_g)
    conv_c, new_c = causal_dwconv(a, st_c, cc_w)
    conv_c = conv_c + cc_b.astype(conv_c.dtype)
    y_c = jax.nn.silu(layernorm(conv_c, cc_lg, cc_lb))

    mix = jnp.concatenate([y_h, y_s.astype(x.dtype), y_c.astype(x.dtype)], axis=-1)
    x = x + mix @ w_out.astype(x.dtype)

    h2 = rmsnorm(x, g_mlp)
    x = x + jnp.square(jax.nn.relu(h2 @ w_up.astype(h2.dtype))) @ w_down.astype(h2.dtype)
    return x, new_h, new_s, new_c


def run_trunk(x, st_h, st_s, st_c, lb_all, g_mix, w_in, hgrn_norm_g, sconv_w, cconv_w, cconv_b,
              cconv_ln_g, cconv_ln_b, w_out, g_mlp, w_up, w_down, g_final):
    nh, ns, nc = [], [], []
    for li in range(DEPTH):
        x, sh, ss, sc = trunk_layer(x, st_h[li], st_s[li], st_c[li], lb_all[li], g_mix[li], w_in[li],
                                    hgrn_norm_g[li], sconv_w[li], cconv_w[li], cconv_b[li],
                                    cconv_ln_g[li], cconv_ln_b[li], w_out[li], g_mlp[li],
                                    w_up[li], w_down[li])
        nh.append(sh)
        ns.append(ss)
        nc.append(sc)
    return rmsnorm(x, g_final), jnp.stack(nh), jnp.stack(ns), jnp.stack(nc)


def setup_inputs(seed: int = 0) -> dict:
    key = jax.random.key(seed)
    ks = jax.random.split(key, 20)

    def nrm(k, shape, s):
        return jax.random.normal(k, shape, jnp.float32) * s

    return {
        'x_prompt': nrm(ks[0], (BATCH, SEQ, D_MODEL), 1.0),
        'x_sample': nrm(ks[1], (DEC_BATCH, DEC_SEQ, D_MODEL), 1.0),
        'state_hgrn': nrm(ks[2], (DEPTH, DEC_BATCH, HG_HEADS, HG_EXPAND, HG_VDIM), 0.5),
        'state_sconv': nrm(ks[3], (DEPTH, DEC_BATCH, SC_WIDTH - 1, SC_W), 1.0),
        'state_cconv': nrm(ks[4], (DEPTH, DEC_BATCH, CC_WIDTH - 1, CC_W), 0.7),
        'g_mix': 1.0 + nrm(ks[5], (DEPTH, D_MODEL), 0.02),
        'w_in': nrm(ks[6], (DEPTH, D_MODEL, IN_COLS), D_MODEL ** -0.5),
        'hgrn_lb': nrm(ks[7], (DEPTH, HG_FDIM), 1.0),
        'hgrn_norm_g': 1.0 + nrm(ks[8], (DEPTH, HG_W), 0.02),
        'sconv_w': nrm(ks[9], (DEPTH, SC_WIDTH, SC_W), SC_WIDTH ** -0.5),
        'cconv_w': nrm(ks[10], (DEPTH, CC_WIDTH, CC_W), CC_WIDTH ** -0.5),
        'cconv_b': nrm(ks[11], (DEPTH, CC_W), 0.02),
        'cconv_ln_g': 1.0 + nrm(ks[12], (DEPTH, CC_W), 0.02),
        'cconv_ln_b': nrm(ks[13], (DEPTH, CC_W), 0.02),
        'w_out': nrm(ks[14], (DEPTH, MIX_W, D_MODEL), MIX_W ** -0.5),
        'g_mlp': 1.0 + nrm(ks[15], (DEPTH, D_MODEL), 0.02),
        'w_up': nrm(ks[16], (DEPTH, D_MODEL, D_FF), D_MODEL ** -0.5),
        'w_down': nrm(ks[17], (DEPTH, D_FF, D_MODEL), D_FF ** -0.5),
        'g_final': 1.0 + nrm(ks[18], (D_MODEL,), 0.02),
    }


def reference(x_prompt, x_sample, state_hgrn, state_sconv, state_cconv, g_mix, w_in, hgrn_lb,
              hgrn_norm_g, sconv_w, cconv_w, cconv_b, cconv_ln_g, cconv_ln_b, w_out, g_mlp,
              w_up, w_down, g_final):
    p = jax.nn.softmax(hgrn_lb.astype(jnp.float32), axis=0)
    lb_all = jnp.cumsum(p, axis=0) - p[0:1]

    zh = jnp.zeros((DEPTH, x_prompt.shape[0], HG_HEADS, HG_EXPAND, HG_VDIM), jnp.float32)
    zs = jnp.zeros((DEPTH, x_prompt.shape[0], SC_WIDTH - 1, SC_W), x_prompt.dtype)
    zc = jnp.zeros((DEPTH, x_prompt.shape[0], CC_WIDTH - 1, CC_W), x_prompt.dtype)
    y_prompt, ph, ps, pc = run_trunk(x_prompt, zh, zs, zc, lb_all, g_mix, w_in, hgrn_norm_g,
                                     sconv_w, cconv_w, cconv_b, cconv_ln_g, cconv_ln_b, w_out,
                                     g_mlp, w_up, w_down, g_final)
    y_sample, sh, ss, sc = run_trunk(x_sample, state_hgrn, state_sconv, state_cconv, lb_all, g_mix,
                                     w_in, hgrn_norm_g, sconv_w, cconv_w, cconv_b, cconv_ln_g,
                                     cconv_ln_b, w_out, g_mlp, w_up, w_down, g_final)
    return (y_prompt, y_sample, ph, ps, pc, sh, ss, sc)
```

```python
import numpy as np
import concourse.bass as bass
import concourse.mybir as mybir
from concourse.bass_utils import run_bass_kernel_spmd

F32 = mybir.dt.float32
BF16 = mybir.dt.bfloat16
AF = mybir.ActivationFunctionType
ALU = mybir.AluOpType
AX = mybir.AxisListType

import os
NCORES = 8
DBG = int(os.environ.get("KDBG", "99"))


class _Stop(Exception):
    pass

D = 2048
T = 1024
NS = 16
TT = T + NS
KC = 16
DEPTH = 2
EPS = 1e-6
NH = 8
TILES = [(0, 352), (352, 352), (704, 336)]
NBLK = 49
WBLK = 8192

ROLES = []
for i in (0, 1):
    ROLES += [("ccv", 2 * i), ("ccg", 2 * i), ("ccv", 2 * i + 1), ("ccg", 2 * i + 1)]
_sc = []
for i in range(4):
    _sc += [("scc", i), ("sch", i), ("scb", i)]
ROLES += _sc
for h in range(NH):
    ROLES += [("f", h), ("q", h), ("v", h), ("og", h)]
assert len(ROLES) == 52
_COL0 = {"q": 0, "f": 1024, "v": 2048, "og": 3072, "scb": 4096, "scc": 4608, "sch": 5120, "ccv": 5632, "ccg": 6144}

PV = {}
_o = 0
for _n, _w in [("g", 5 * 16), ("lb", 2 * 8), ("hg", 2 * 8), ("scw", 2 * 4 * 3), ("ccw", 2 * 4 * 31),
               ("ccb", 2 * 4), ("clg", 2 * 4), ("clb", 2 * 4), ("flag", 1)]:
    PV[_n] = (_o, _w)
    _o += _w
NPV = _o
CV = {}
_o = 0
for _n, _w in [("ident", 128), ("causal", 64), ("smask", 1024)]:
    CV[_n] = (_o, _w)
    _o += _w
NCV = _o


class Sched:
    def __init__(self):
        self.ops = {e: [] for e in ("pe", "act", "dve", "pool", "sp")}
        self.streams = {}
        self.last_w = {}
        self.readers = {}
        self.known = {e: {} for e in self.ops}

    def op(self, eng, fn, reads=(), writes=(), dma=None, nowait_self=False):
        need = {}

        def add(ev):
            if ev is None:
                return
            k, i = ev
            if need.get(k, -1) < i:
                need[k] = i
        for r in reads:
            add(self.last_w.get(r))
        for w in writes:
            add(self.last_w.get(w))
            for ev in list(self.readers.get(w, {}).items()):
                add(ev)
        waits = []
        for k, i in need.items():
            if nowait_self and k == eng:
                continue
            if k in self.streams:
                i = max(i, self.streams[k] - 1)
            if self.known[eng].get(k, -1) >= i:
                continue
            self.known[eng][k] = i
            waits.append((k, i))
        rec = {"fn": fn, "waits": waits, "dma": dma, "sig": False}
        self.ops[eng].append(rec)
        if dma is not None:
            self.streams.setdefault(dma, 0)
            ev = (dma, self.streams[dma])
            self.streams[dma] += 1
        else:
            ev = (eng, len(self.ops[eng]) - 1)
        for r in reads:
            lst = self.readers.setdefault(r, {})
            if lst.get(ev[0], -1) < ev[1]:
                lst[ev[0]] = ev[1]
        for w in writes:
            self.last_w[w] = ev
            self.readers[w] = {}
        return ev

    def finalize(self):
        for e, lst in self.ops.items():
            for rec in lst:
                for (k, i) in rec["waits"]:
                    if k not in self.streams:
                        self.ops[k][i]["sig"] = True
        self.sigcount = {}
        for e, lst in self.ops.items():
            c = 0
            arr = []
            for rec in lst:
                if rec["sig"]:
                    c += 1
                arr.append(c)
            self.sigcount[e] = arr


def build_program():
    nc = bass.Bass("TRN2", target_bir_lowering=False)
    S = Sched()

    def din(name, shape, dt=F32):
        return nc.dram_tensor(name, shape, dt, kind="ExternalInput").ap()

    def dout(name, shape):
        return nc.dram_tensor(name, shape, F32, kind="ExternalOutput").ap()

    xT_d = din("xT", [128, KC, TT])
    NW = DEPTH * NBLK if DBG in (5, 6, 99) else 13
    wall = din("wall", [NW, 128, WBLK])
    pv_d = din("pv", [128, NPV])
    cv_d = din("cv", [128, NCV])
    sh_d = din("sh", [DEPTH, NH, 128, NS * 128])
    ss_d = din("ss", [DEPTH, 128, 4 * NS * 2])
    sc_d = din("sc", [DEPTH, 128, 4 * NS * 30])
    yT_o = dout("yT", [128, KC, TT])
    hp_o = dout("hp", [DEPTH, 128, NH * 128])
    sp_o = dout("sp", [DEPTH, 128, 8])
    cp_o = dout("cp", [DEPTH, 128, 120])
    hs_o = dout("hs", [DEPTH, NH, 128, NS * 128])
    ss_o = dout("sso", [DEPTH, 128, 4 * NS * 2])
    sc_o = dout("sco", [DEPTH, 128, 4 * NS * 30])
    xs_d = nc.dram_tensor("xspill", [128, KC, TT], F32).ap()
    PAYW = NH * 128 + 120 + 8
    pay_in = [nc.dram_tensor(f"pay_in{l}", [128, PAYW], F32) for l in range(DEPTH)]
    pay_out = [nc.dram_tensor(f"pay_out{l}", [256, PAYW], F32) for l in range(DEPTH)]

    def sb(name, shape, dt=F32):
        return nc.alloc_sbuf_tensor(name, shape, dt).ap()

    A = sb("A", [128, KC * TT])
    xT = A.rearrange("p (c t) -> p c t", c=KC)
    hT = sb("hT", [128, KC, TT], BF16)
    mixT = sb("mixT", [128, KC, TT], BF16)
    W = [sb(f"W{i}", [128, WBLK], BF16) for i in range(2)]
    MS = sb("MS", [128, 4352])
    PVt = sb("PVt", [128, NPV])
    CVt = sb("CVt", [128, NCV])
    identb = sb("identb", [128, 128], BF16)
    onesb = sb("onesb", [128, 128], BF16)
    causb = sb("causb", [128, 64], BF16)
    Gs = sb("Gs", [128, 5 * 16])
    LBt = sb("LBt", [128, 4 * 16])
    VT = sb("VT", [128, TT], BF16)
    VTOK2 = [sb(f"VTOK{i}", [128, 8 * 128], BF16) for i in range(2)]
    EBL2 = [sb(f"EBL{i}", [128, 16]) for i in range(2)]
    SMP2 = [sb(f"SMP{i}", [128, 4 * NS]) for i in range(2)]
    EBE = sb("EBE", [128, NH])
    SSm = sb("SSm", [128, 4 * NS * 3])
    FNC = sb("FNC", [128, 16])
    KEEP = sb("KEEP", [128, 4 * 30 + 4 * 2 + 4 * 2])
    FIX = sb("FIX", [128, 512])

    def carve(base, n, dt=F32):
        v = A[:, base:base + (n if dt == F32 else n // 2)]
        return v if dt == F32 else v.bitcast(BF16)
    _a = 0

    def take(n_f32):
        nonlocal _a
        r = (_a, n_f32)
        _a += n_f32
        return r
    r_T0, r_QS, r_FG, r_KK = take(TT), take(TT), take(TT), take(TT)
    r_BD, r_EE = take(T), take(T)
    r_QT, r_KT, r_KTOK = take(T // 2), take(T // 2), take(T // 2)
    r_QT2, r_KT2, r_KTOK2 = take(T // 2), take(T // 2), take(T // 2)
    r_OL = take(NH * T // 2)
    r_SST = take(NS * 128)
    assert _a <= KC * TT, _a
    T0 = A[:, r_T0[0]:r_T0[0] + TT]
    QS = A[:, r_QS[0]:r_QS[0] + TT]
    FG = A[:, r_FG[0]:r_FG[0] + TT]
    KK = A[:, r_KK[0]:r_KK[0] + TT]
    BD = A[:, r_BD[0]:r_BD[0] + T]
    EE = A[:, r_EE[0]:r_EE[0] + T]
    QTt2 = [A[:, r[0]:r[0] + T // 2].bitcast(BF16) for r in (r_QT, r_QT2)]
    KTt2 = [A[:, r[0]:r[0] + T // 2].bitcast(BF16) for r in (r_KT, r_KT2)]
    KTOK2 = [A[:, r[0]:r[0] + T // 2].bitcast(BF16) for r in (r_KTOK, r_KTOK2)]
    OL = A[:, r_OL[0]:r_OL[0] + NH * T // 2].bitcast(BF16).rearrange("p (h t) -> p h t", h=NH)
    Sst = A[:, r_SST[0]:r_SST[0] + NS * 128]
    APAD = A[:, 0:4 * 1054].rearrange("p (i t) -> p i t", i=4)
    CO = A[:, 4216:4216 + 4 * TT].rearrange("p (i t) -> p i t", i=4)
    CT1 = A[:, 8376:8376 + TT]
    CT2 = A[:, 9416:9416 + TT]
    CT3 = A[:, 10456:10456 + TT]
    UPAD = A[:, 11496:11496 + 1026]
    CMU = A[:, 11496:11496 + TT]
    CRS = A[:, 12536:12536 + TT]
    SMX = A[:, 12536:12536 + 512]
    CXB = A[:, 13576:13576 + TT // 2].bitcast(BF16)
    CXQ = A[:, 14096:14096 + TT // 2].bitcast(BF16)
    SCm = A[:, 14616:14616 + 4 * NS * 31]
    PAY = A[:, 0:PAYW]
    PIN = A[:, PAYW:2 * PAYW]
    SINB = A[:, 2 * PAYW:2 * PAYW + 64].bitcast(BF16)
    TSQ = A[:, 2400:2400 + TT // 2].bitcast(BF16)
    TRS = A[:, 2920:2920 + TT]
    Rr = MS[:, 0:TT]
    SQ = [MS[:, 1040 + i * 520:1040 + (i + 1) * 520].bitcast(BF16) for i in range(2)]
    SOG = MS[:, 0:4160].bitcast(BF16).rearrange("p (h t) -> p h t", h=NH)
    XO = [MS[:, i * TT:(i + 1) * TT] for i in range(2)]
    HID = [MS[:, i * 2080:(i + 1) * 2080].bitcast(BF16).rearrange("p (c t) -> p c t", c=4) for i in range(2)]

    def pvs(name, a, b=None):
        o, w = PV[name]
        return PVt[:, o + a:o + (a + 1 if b is None else b)]

    PS = [nc.alloc_psum_tensor(f"ps{i}", [128, 512], F32).ap() for i in range(8)]

    def slot_regs(s):
        return [PS[3 * s][:, 0:352], PS[3 * s + 1][:, 0:352], PS[3 * s + 2][:, 0:336]]

    def pieces(regs):
        return [(regs[0], 0, 352), (regs[1], 352, 704), (regs[2][:, 0:320], 704, 1024), (regs[2][:, 320:336], 1024, 1040)]
    B6, B7 = PS[6], PS[7]
    PTR = PS[7].bitcast(BF16)

    MSTATE = {"owner": None}

    def ms_fence(owner):
        if MSTATE["owner"] != owner:
            S.op("dve", lambda e: e.memset(FNC[:, 0:1], 0.0), reads=[], writes=["MS", "SMXf"])
            MSTATE["owner"] = owner

    wctr = {"n": 0}

    def load_w(l, b):
        i = wctr["n"] % 2
        wctr["n"] += 1
        src = wall[l * NBLK + b]
        S.op("pool", lambda e, i=i, src=src: e.dma_start(out=W[i], in_=src), reads=[], writes=[f"W{i}"], dma=f"w{i}")
        return i

    def mm_chunk(slot, lhs_fn, rhs_fn, nk, reads, extra_w=(), hook=None):
        regs = slot_regs(slot)

        def mk(k0, k1):
            def fn(e):
                ins = None
                for k in range(k0, k1):
                    for ti, (t0, tw) in enumerate(TILES):
                        ins = e.matmul(regs[ti][:, 0:tw], lhsT=lhs_fn(k), rhs=rhs_fn(k, t0, tw),
                                       start=(k == 0), stop=(k == nk - 1))
                return ins
            return fn
        if hook is None:
            S.op("pe", mk(0, nk), reads=reads, writes=[f"slot{slot}"] + list(extra_w))
        else:
            S.op("pe", mk(0, nk // 2), reads=reads, writes=[f"slot{slot}"] + list(extra_w))
            hook()
            S.op("pe", mk(nk // 2, nk), reads=reads, writes=[f"slot{slot}"] + list(extra_w), nowait_self=True)
            hook()

    def ew3(eng, slot, fn3, reads, writes):
        regs = slot_regs(slot)

        def fn(e):
            ins = None
            for ti, (t0, tw) in enumerate(TILES):
                ins = fn3(e, regs[ti][:, 0:tw], t0, tw)
            return ins
        S.op(eng, fn, reads=[f"slot{slot}"] + list(reads), writes=list(writes) + [f"slot{slot}"])

    slotctr = {"n": 0}

    def next_slot():
        s = slotctr["n"] % 2
        slotctr["n"] += 1
        return s

    S.op("sp", lambda e: e.dma_start(out=xT, in_=xT_d), writes=["xT"], dma="ld_x")
    S.op("sp", lambda e: e.dma_start(out=PVt, in_=pv_d), writes=["PV"], dma="ld_p")
    S.op("sp", lambda e: e.dma_start(out=CVt, in_=cv_d), writes=["CV"], dma="ld_c")
    S.op("dve", lambda e: e.memset(onesb, 1.0), writes=["ones"])
    o_id, o_ca, o_sm = CV["ident"][0], CV["causal"][0], CV["smask"][0]
    identf = CVt[:, o_id:o_id + 128]
    smask = CVt[:, o_sm:o_sm + 1024]
    S.op("dve", lambda e: e.tensor_copy(out=identb, in_=identf), reads=["CV"], writes=["identb"])
    S.op("dve", lambda e: e.tensor_copy(out=causb, in_=CVt[:, o_ca:o_ca + 64]), reads=["CV"], writes=["causb"])
    S.op("dve", lambda e: e.tensor_scalar(out=Gs, in0=pvs("g", 0, 80), scalar1=float(np.sqrt(D)), scalar2=None,
                                          op0=ALU.mult), reads=["PV"], writes=["Gs"])
    lb0, lb1 = pvs("lb", 0, 8), pvs("lb", 8, 16)
    sc0, sc1, sc2 = LBt[:, 48:56], LBt[:, 56:64], LBt[:, 0:8]

    def lbsetup():
        S.op("dve", lambda e: e.tensor_tensor(out=sc2, in0=lb0, in1=lb1, op=ALU.max), reads=["PV"], writes=["LB"])
        S.op("dve", lambda e: e.tensor_tensor(out=sc0, in0=lb0, in1=sc2, op=ALU.subtract), reads=["LB", "PV"], writes=["LB"])
        S.op("dve", lambda e: e.tensor_tensor(out=sc1, in0=lb1, in1=sc2, op=ALU.subtract), reads=["LB", "PV"], writes=["LB"])
        S.op("act", lambda e: e.activation(out=sc0, in_=sc0, func=AF.Exp), reads=["LB"], writes=["LB"])
        S.op("act", lambda e: e.activation(out=sc1, in_=sc1, func=AF.Exp), reads=["LB"], writes=["LB"])
        S.op("dve", lambda e: e.tensor_tensor(out=sc2, in0=sc0, in1=sc1, op=ALU.add), reads=["LB"], writes=["LB"])
        S.op("dve", lambda e: e.reciprocal(out=sc2, in_=sc2), reads=["LB"], writes=["LB"])
        S.op("dve", lambda e: e.tensor_tensor(out=sc0, in0=sc0, in1=sc2, op=ALU.mult), reads=["LB"], writes=["LB"])
        S.op("dve", lambda e: e.tensor_tensor(out=sc1, in0=sc1, in1=sc2, op=ALU.mult), reads=["LB"], writes=["LB"])
        S.op("dve", lambda e: e.tensor_tensor(out=sc2, in0=sc0, in1=sc1, op=ALU.add), reads=["LB"], writes=["LB"])
        S.op("dve", lambda e: e.tensor_tensor(out=LBt[:, 8:16], in0=sc2, in1=sc0, op=ALU.subtract), reads=["LB"], writes=["LB"])
        S.op("dve", lambda e: e.tensor_tensor(out=LBt[:, 0:8], in0=sc0, in1=sc0, op=ALU.subtract), reads=["LB"], writes=["LB"])
        S.op("dve", lambda e: e.tensor_scalar(out=LBt[:, 16:32], in0=LBt[:, 0:16], scalar1=-1.0, scalar2=1.0,
                                              op0=ALU.mult, op1=ALU.add), reads=["LB"], writes=["LB"])
        S.op("dve", lambda e: e.tensor_scalar(out=LBt[:, 32:48], in0=LBt[:, 0:16], scalar1=-1.0, scalar2=None,
                                              op0=ALU.add), reads=["LB"], writes=["LB"])
    lbsetup()

    def chain(eng, fns, reads, writes):
        for f in fns:
            S.op(eng, f, reads=reads, writes=writes)

    def act_sigmoid_from_exp(buf, res, extra_reads=()):
        chain("act", [lambda e: e.activation(out=buf, in_=buf, func=AF.Ln, bias=1.0),
                      lambda e: e.activation(out=buf, in_=buf, func=AF.Exp, scale=-1.0)], [res] + list(extra_reads), [res])

    def act_rsqrt_inplace(buf, res, addc, extra_reads=()):
        chain("act", [lambda e: e.activation(out=buf, in_=buf, func=AF.Ln, bias=float(addc)),
                      lambda e: e.activation(out=buf, in_=buf, func=AF.Exp, scale=-0.5)], [res] + list(extra_reads), [res])

    def rmsnorm(gidx, dst_fn, dst_res):
        ms_fence("norm")
        slot = next_slot()
        regs = slot_regs(slot)
        for dc in range(KC):
            q = SQ[dc % 2]
            S.op("act", lambda e, dc=dc, q=q: e.activation(out=q, in_=xT[:, dc, :], func=AF.Square),
                 reads=["xT", "MS"], writes=[f"SQ{dc % 2}"])

            def fn(e, dc=dc, q=q):
                ins = None
                for ti, (t0, tw) in enumerate(TILES):
                    ins = e.matmul(regs[ti][:, 0:tw], lhsT=onesb, rhs=q[:, t0:t0 + tw], start=(dc == 0), stop=(dc == KC - 1))
                return ins
            S.op("pe", fn, reads=[f"SQ{dc % 2}", "ones"], writes=[f"slot{slot}"])
        ew3("act", slot, lambda e, ps, t0, tw: e.activation(out=Rr[:, t0:t0 + tw], in_=ps, func=AF.Ln, bias=float(D * EPS)),
            reads=["MS"], writes=["Rr"])
        S.op("act", lambda e: e.activation(out=Rr, in_=Rr, func=AF.Exp, scale=-0.5), reads=["Rr"], writes=["Rr"])
        for dc in range(KC):
            S.op("dve", lambda e, dc=dc: e.scalar_tensor_tensor(out=dst_fn(dc), in0=xT[:, dc, :],
                                                                scalar=Gs[:, gidx * 16 + dc:gidx * 16 + dc + 1], in1=Rr,
                                                                op0=ALU.mult, op1=ALU.mult),
                 reads=["xT", "Rr", "Gs"], writes=[dst_res])

    SLE = sb("SLE", [128, NH * 128])
    SL = sb("SL", [128, 128])
    SDB = [sb(f"SDB{i}", [128, 128], BF16) for i in range(2)]
    KVT = sb("KVT", [32, 256])
    AN = sb("AN", [32, 4 * 128])
    ATS = sb("ATS", [128, 128], BF16)
    BLt = sb("BLt", [128, 16])
    PFX = sb("PFX", [128, 16])
    PFE = sb("PFE", [128, 16])
    ONE16 = sb("ONE16", [128, 16])
    OTS = sb("OTS", [128, NH * NS])
    PST = sb("PST", [128, 128])
    HGs = sb("HGs", [128, 16])
    S.op("dve", lambda e: e.memset(ONE16, 1.0), writes=["ONE16"])
    S.op("dve", lambda e: e.tensor_scalar(out=HGs, in0=pvs("hg", 0, 16), scalar1=float(np.sqrt(128.0)), scalar2=None,
                                          op0=ALU.mult), reads=["PV"], writes=["HGs"])

    seq = []
    for l in range(DEPTH):
        order = list(range(13)) + list(range(13, 17))
        ml = [17, 18]
        for g in range(16):
            ml.append(33 + g)
            if g + 2 < 16:
                ml.append(17 + g + 2)
        order += ml
        assert len(order) == NBLK and len(set(order)) == NBLK
        seq += [(l, b) for b in order]
    wp = {"p": 0}

    def get_w(l, b):
        p = wp["p"]
        assert seq[p] == (l, b), (seq[p], l, b)
        if p == 0:
            load_w(*seq[0])
        if p + 1 < len(seq) and seq[p + 1][0] * NBLK + seq[p + 1][1] < NW:
            load_w(*seq[p + 1])
        wp["p"] += 1
        return p % 2

    def fenceA():
        S.op("dve", lambda e: e.memset(FNC[:, 2:3], 0.0), reads=[], writes=["Afree", "SMXf3"])

    def ln_silu(l, n0, n1, src_fn, dst_fn, tag, use_slots):
        pass

    def conformer_ln(l, AF_R):
        sa, sb_ = next_slot(), next_slot()
        ra, rb = slot_regs(sa), slot_regs(sb_)
        for i in range(4):
            S.op("act", lambda e, i=i: e.activation(out=CXB, in_=CO[:, i, :], func=AF.Copy),
                 reads=[f"CO{i}", f"CO{i}s"] + AF_R, writes=["CXB"])
            S.op("act", lambda e, i=i: e.activation(out=CXQ, in_=CO[:, i, :], func=AF.Square),
                 reads=[f"CO{i}", f"CO{i}s"] + AF_R, writes=["CXQ"])

            def fn(e, i=i):
                ins = None
                for ti, (t0, tw) in enumerate(TILES):
                    e.matmul(ra[ti][:, 0:tw], lhsT=onesb, rhs=CXB[:, t0:t0 + tw], start=(i == 0), stop=(i == 3))
                    ins = e.matmul(rb[ti][:, 0:tw], lhsT=onesb, rhs=CXQ[:, t0:t0 + tw], start=(i == 0), stop=(i == 3))
                return ins
            S.op("pe", fn, reads=["CXB", "CXQ", "ones"], writes=[f"slot{sa}", f"slot{sb_}"])
        ew3("dve", sa, lambda e, ps, t0, tw: e.tensor_scalar(out=CMU[:, t0:t0 + tw], in0=ps, scalar1=1.0 / 512, scalar2=None, op0=ALU.mult),
            reads=AF_R, writes=["CMU", "UPAD", "UPADr", f"slot{sa}"])
        ew3("dve", sb_, lambda e, ps, t0, tw: e.tensor_scalar(out=CRS[:, t0:t0 + tw], in0=ps, scalar1=1.0 / 512, scalar2=None, op0=ALU.mult),
            reads=AF_R, writes=["CRS", "SMX", f"slot{sb_}"])

        chain("dve", [lambda e: e.tensor_tensor(out=CT1, in0=CMU, in1=CMU, op=ALU.mult),
                      lambda e: e.tensor_tensor(out=CRS, in0=CRS, in1=CT1, op=ALU.subtract)],
              ["CMU", "CRS"] + AF_R, ["CRS", "CT1"])
        act_rsqrt_inplace(CRS, "CRS", EPS)
        for i in range(4):
            o_g = PV["clg"][0] + l * 4 + i
            o_b = PV["clb"][0] + l * 4 + i

            chain("dve", [lambda e, i=i: e.tensor_tensor(out=CT2, in0=CO[:, i, :], in1=CMU, op=ALU.subtract),
                          lambda e: e.tensor_tensor(out=CT2, in0=CT2, in1=CRS, op=ALU.mult),
                          lambda e, o_g=o_g, o_b=o_b: e.tensor_scalar(out=CT2, in0=CT2, scalar1=PVt[:, o_g:o_g + 1],
                                                                      scalar2=PVt[:, o_b:o_b + 1], op0=ALU.mult, op1=ALU.add)],
                  [f"CO{i}", f"CO{i}s", "CMU", "CRS", "PV"] + AF_R, ["CT2"])
            S.op("act", lambda e: e.activation(out=CT3, in_=CT2, func=AF.Exp, scale=-1.0), reads=["CT2"], writes=["CT3"])
            act_sigmoid_from_exp(CT3, "CT3")

            def f2(e, i=i):
                return e.tensor_tensor(out=mixT[:, 12 + i, :], in0=CT2, in1=CT3, op=ALU.mult)
            S.op("dve", f2, reads=["CT2", "CT3"], writes=["CT3", "CT2", f"mix{12 + i}"])
        S.op("dve", lambda e: e.tensor_copy(out=PST[:, 0:120].rearrange("p (i r) -> p i r", i=4), in_=APAD[:, :, 1024:1054]),
             reads=["APAD"] + AF_R, writes=["PSTa"])

    STEPS = []

    def pop_steps(n):
        for _ in range(n):
            if STEPS:
                STEPS.pop(0)()

    ELQ = []

    def pop_el(n):
        for _ in range(n):
            if ELQ:
                ELQ.pop(0)()

    def head_q(l, h, sq, AF_R):
        R = AF_R
        hp = h % 2
        ew3("act", sq, lambda e, ps, t0, tw: e.activation(out=T0[:, t0:t0 + tw], in_=ps, func=AF.Exp, scale=-1.0), reads=R, writes=["T0"])
        act_sigmoid_from_exp(T0, "T0")
        ew3("dve", sq, lambda e, ps, t0, tw: e.tensor_tensor(out=QS[:, t0:t0 + tw], in0=ps, in1=T0[:, t0:t0 + tw], op=ALU.mult),
            reads=["T0"] + R, writes=["QS", f"slot{sq}"])
        S.op("dve", lambda e: e.tensor_copy(out=SMP2[hp][:, 2 * NS:3 * NS], in_=QS[:, T:TT]), reads=["QS"], writes=[f"SMPq{hp}"])
        QTt, KTt = QTt2[hp], KTt2[hp]

        def QJ():
            S.op("dve", lambda e: e.tensor_tensor(out=mixT[:, h, 0:T], in0=QS[:, 0:T], in1=EE, op=ALU.mult), reads=["QS", "EE"], writes=[f"mix{h}"])
            S.op("act", lambda e: e.activation(out=EE, in_=BD, func=AF.Exp), reads=["BD", f"mix{h}"], writes=["EE"])
            S.op("dve", lambda e: e.scalar_tensor_tensor(out=QTt, in0=EE, scalar=1e35, in1=QS[:, 0:T], op0=ALU.min, op1=ALU.mult),
                 reads=["EE", "QS"] + R, writes=[f"QT{hp}"])
        ELQ.append(QJ)

    def head_f(l, h, sf, AF_R, LB, OML, NOML):
        R = AF_R
        hp = h % 2
        QTt, KTt, EBL = QTt2[hp], KTt2[hp], EBL2[hp]
        ew3("act", sf, lambda e, ps, t0, tw: e.activation(out=FG[:, t0:t0 + tw], in_=ps, func=AF.Exp, scale=-1.0), reads=R, writes=["FG", f"slot{sf}"])
        BD3 = BD.rearrange("p (c t) -> p c t", t=64)
        EE3 = EE.rearrange("p (c t) -> p c t", t=64)

        def F1():
            act_sigmoid_from_exp(FG, "FG")
            chain("dve", [lambda e: e.tensor_scalar(out=KK, in0=FG, scalar1=NOML[:, h:h + 1], scalar2=OML[:, h:h + 1], op0=ALU.mult, op1=ALU.add),
                          lambda e: e.tensor_scalar(out=FG, in0=FG, scalar1=OML[:, h:h + 1], scalar2=LB[:, h:h + 1], op0=ALU.mult, op1=ALU.add),
                          lambda e: e.tensor_scalar(out=FG, in0=FG, scalar1=1e-30, scalar2=None, op0=ALU.max)],
                  ["FG", "LB"] + R, ["FG", "KK"])

            def fcp(e):
                e.tensor_copy(out=SMP2[hp][:, 0:NS], in_=FG[:, T:TT])
                return e.tensor_copy(out=SMP2[hp][:, 3 * NS:4 * NS], in_=KK[:, T:TT])
            S.op("dve", fcp, reads=["FG", "KK"], writes=[f"SMPf{hp}"])

        def F2():
            S.op("act", lambda e: e.activation(out=FG[:, 0:T], in_=FG[:, 0:T], func=AF.Ln), reads=["FG", f"SMPf{hp}"], writes=["FG"])
            S.op("dve", lambda e: e.tensor_tensor_scan(out=BD, data0=smask, data1=FG[:, 0:T], initial=0.0, op0=ALU.mult, op1=ALU.add),
                 reads=["FG", "CV"] + R, writes=["BD"])

        def F3():
            chain("dve", [lambda e: e.tensor_copy(out=BLt, in_=BD3[:, :, 63]),
                          lambda e: e.tensor_tensor_scan(out=PFX, data0=ONE16, data1=BLt, initial=0.0, op0=ALU.mult, op1=ALU.add),
                          lambda e: e.tensor_tensor(out=PFE, in0=PFX, in1=BLt, op=ALU.subtract),
                          lambda e: e.tensor_tensor(out=EE3, in0=BD3, in1=PFE.unsqueeze(2).to_broadcast([128, 16, 64]), op=ALU.add),
                          lambda e: e.tensor_tensor(out=BD3, in0=BD3, in1=BLt.unsqueeze(2).to_broadcast([128, 16, 64]), op=ALU.subtract)],
                  ["BD", "ONE16"] + R, ["BD", "EE", "BLt", "PFX"])

        def F4():
            def fe1(e):
                e.activation(out=EBL, in_=BLt, func=AF.Exp)
                e.activation(out=EBE[:, h:h + 1], in_=PFX[:, 15:16], func=AF.Exp)
                return e.activation(out=EE, in_=EE, func=AF.Exp)
            S.op("act", fe1, reads=["EE", "BLt", "PFX"], writes=["EE", f"EBL{hp}", f"EBE{h}"])
            S.op("act", lambda e: e.activation(out=FG[:, 0:T], in_=BD, func=AF.Exp, scale=-1.0), reads=["BD", "FG"], writes=["FG"])
            S.op("dve", lambda e: e.tensor_tensor(out=KTt, in0=FG[:, 0:T], in1=KK[:, 0:T], op=ALU.mult), reads=["FG", "KK"] + R, writes=[f"KT{hp}"])
        ELQ.extend([F1, F2, F3, F4])

    def head_v(l, h, sv, AF_R):
        R = AF_R
        hp = h % 2
        vr = slot_regs(sv)

        def fvv(e):
            ins = None
            for (ps, a0, a1) in pieces(vr):
                if a0 < T:
                    ins = e.activation(out=VT[:, a0:a1], in_=ps, func=AF.Copy)
                else:
                    ins = e.activation(out=SMP2[hp][:, NS:2 * NS], in_=ps, func=AF.Copy)
            return ins
        S.op("act", fvv, reads=[f"slot{sv}"], writes=["VT", f"SMPv{hp}", f"slot{sv}"])

    def head_og(l, h, so, AF_R):
        R = AF_R
        hp = h % 2
        pop_el(len(ELQ))
        QTt, KTt, KTOK, VTOK, EBL, SMP = QTt2[hp], KTt2[hp], KTOK2[hp], VTOK2[hp], EBL2[hp], SMP2[hp]
        Fs, VSs, QSs, KKs = SMP[:, 0:NS], SMP[:, NS:2 * NS], SMP[:, 2 * NS:3 * NS], SMP[:, 3 * NS:4 * NS]
        rQT, rKT, rKTOK, rVTOK, rEBL = f"QT{hp}", f"KT{hp}", f"KTOK{hp}", f"VTOK{hp}", f"EBL{hp}"
        rSMP = [f"SMPq{hp}", f"SMPf{hp}", f"SMPv{hp}"]
        ew3("act", so, lambda e, ps, t0, tw: e.activation(out=T0[:, t0:t0 + tw], in_=ps, func=AF.Exp, scale=-1.0), reads=["QS"] + R, writes=["T0"])
        act_sigmoid_from_exp(T0, "T0")
        ew3("dve", so, lambda e, ps, t0, tw: e.tensor_tensor(out=SOG[:, h, t0:t0 + tw], in0=ps, in1=T0[:, t0:t0 + tw], op=ALU.mult),
            reads=["T0", "MS"], writes=[f"SOG{h}", f"slot{so}"])
        def tr_step(which):
            src, dst, sr, dr = ((KTt, KTOK, rKT, rKTOK), (VT, VTOK, "VT", rVTOK))[which]

            def ft(e):
                ins = None
                for j in range(8):
                    ins = e.transpose(out=PTR[:, j * 128:(j + 1) * 128], in_=src[:, j * 128:(j + 1) * 128], identity=identb)
                return ins
            S.op("pe", ft, reads=[sr, "identb"], writes=["B7"])
            S.op("act", lambda e: e.activation(out=dst, in_=PTR, func=AF.Copy), reads=R, writes=[dr, "B7"])
        tr_step(1)
        STEPS.append(lambda: tr_step(0))

        def o_mm(g):
            for p in range(2):
                def fo(e, g=g, p=p):
                    c = 2 * g + p
                    oreg = B6[:, (c % 8) * 64:(c % 8) * 64 + 64]
                    ins = e.matmul(oreg, lhsT=VTOK[64 * p:64 * p + 64, g * 128:(g + 1) * 128], rhs=ATS[64 * p:64 * p + 64, 64 * p:64 * p + 64],
                                   start=True, stop=(c == 0))
                    if c > 0:
                        ins = e.matmul(oreg, lhsT=SDB[p], rhs=QTt[:, 64 * c:64 * c + 64], start=False, stop=True)
                    return ins
                S.op("pe", fo, reads=[rVTOK, "ATS", "SDB0", "SDB1", rQT], writes=["B6"])
            if g % 4 == 3:
                half = g // 4
                S.op("act", lambda e, half=half: e.activation(out=OL[:, h, 512 * half:512 * half + 512], in_=B6, func=AF.Copy),
                     reads=R, writes=[f"OL{h}", "B6"])

        def scan_step(g):
            if g > 0:
                o_mm(g - 1)
            for p in range(2):
                def fAU(e, g=g, p=p):
                    c = 2 * g + p
                    e.matmul(B7[:, 64 * p:64 * p + 64], lhsT=KTt[:, g * 128:(g + 1) * 128], rhs=QTt[:, 64 * c:64 * c + 64], start=True, stop=True)
                    return e.matmul(B7[:, 128 + 128 * p:256 + 128 * p], lhsT=KTOK[64 * p:64 * p + 64, g * 128:(g + 1) * 128],
                                    rhs=VTOK[64 * p:64 * p + 64, g * 128:(g + 1) * 128], start=True, stop=True)
                S.op("pe", fAU, reads=[rKT, rQT, rKTOK, rVTOK], writes=["B7"])
            S.op("dve", lambda e: e.tensor_tensor(out=ATS.rearrange("p (a t) -> p a t", a=2), in0=B7[:, 0:128].rearrange("p (a t) -> p a t", a=2),
                                                  in1=causb.unsqueeze(1).to_broadcast([128, 2, 64]), op=ALU.mult),
                 reads=["causb"], writes=["ATS", "B7"])
            for p in range(2):
                c = 2 * g + p
                if c > 0:
                    S.op("dve", lambda e, c=c, p=p: e.tensor_scalar(out=SDB[p], in0=SL, scalar1=EBL[:, c:c + 1], scalar2=None, op0=ALU.mult),
                         reads=["SL", rEBL], writes=[f"SDB{p}"])
                    S.op("dve", lambda e, c=c, p=p: e.scalar_tensor_tensor(out=SL, in0=SL, scalar=EBL[:, c:c + 1], in1=B7[:, 128 + 128 * p:256 + 128 * p],
                                                                            op0=ALU.mult, op1=ALU.add),
                         reads=["SL", rEBL], writes=["SL", "B7"])
                else:
                    S.op("dve", lambda e, p=p: e.tensor_copy(out=SL, in_=B7[:, 128 + 128 * p:256 + 128 * p]), reads=[], writes=["SL", "B7"])

        def scan_end():
            o_mm(7)
            S.op("dve", lambda e: e.tensor_copy(out=SLE[:, h * 128:(h + 1) * 128], in_=SL), reads=["SL"], writes=[f"SLE{h}"])
            S.op("sp", lambda e: e.dma_start(out=Sst, in_=sh_d[l, h]), reads=R, writes=["Sst"], dma="ld_sh")

            def ftr(e):
                e.matmul(B7[0:NS, 0:128], lhsT=KKs, rhs=identf, start=True, stop=True)
                return e.matmul(B7[0:NS, 128:256], lhsT=VSs, rhs=identf, start=True, stop=True)
            S.op("pe", ftr, reads=rSMP + ["CV"], writes=["B7"])
            S.op("dve", lambda e: e.tensor_copy(out=KVT[0:NS, :], in_=B7[0:NS, 0:256]), reads=[], writes=["KVT", "B7"])

        def samp_step(q4):
            bank, bres = (B6, "B6") if q4 % 2 == 0 else (B7, "B7")
            S.op("dve", lambda e: e.tensor_tensor(out=AN[0:NS, :].rearrange("p (n k) -> p n k", n=4),
                                                  in0=KVT[0:NS, 0:128].unsqueeze(1).to_broadcast([NS, 4, 128]),
                                                  in1=identf[0:NS, 4 * q4:4 * q4 + 4].unsqueeze(2).to_broadcast([NS, 4, 128]), op=ALU.mult),
                 reads=["KVT", "CV"], writes=["AN"])

            def fkv(e):
                ins = None
                for jn in range(4):
                    ins = e.matmul(bank[:, 128 * jn:128 * jn + 128], lhsT=AN[0:NS, jn * 128:(jn + 1) * 128], rhs=KVT[0:NS, 128:256],
                                   start=True, stop=True)
                return ins
            S.op("pe", fkv, reads=["AN", "KVT"], writes=[bres])

            def fsu(e):
                ins = None
                for jn in range(4):
                    n = 4 * q4 + jn
                    ins = e.scalar_tensor_tensor(out=Sst[:, n * 128:(n + 1) * 128], in0=Sst[:, n * 128:(n + 1) * 128], scalar=Fs[:, n:n + 1],
                                                 in1=bank[:, 128 * jn:128 * jn + 128], op0=ALU.mult, op1=ALU.add)
                return ins
            S.op("dve", fsu, reads=rSMP, writes=["Sst", bres])

        def samp_end():
            def fos(e):
                ins = None
                for n in range(NS):
                    ins = e.matmul(B6[:, n:n + 1], lhsT=Sst[:, n * 128:(n + 1) * 128], rhs=QSs[:, n:n + 1], start=True, stop=True)
                return ins
            S.op("pe", fos, reads=["Sst"] + rSMP, writes=["B6"])
            S.op("dve", lambda e: e.tensor_copy(out=OTS[:, h * NS:(h + 1) * NS], in_=B6[:, 0:NS]), reads=[], writes=[f"OTS{h}", "B6"])
            store("sp", hs_o[l, h], Sst, ["Sst"] + R, "st_hs")
        for g in range(8):
            STEPS.append(lambda g=g: scan_step(g))
        STEPS.append(scan_end)
        for q4 in range(4):
            STEPS.append(lambda q4=q4: samp_step(q4))
        STEPS.append(samp_end)

    def tail_and_rest(l, AF_R, xold_src):
        R = AF_R
        store("sp", ss_o[l].rearrange("p (i q) -> p i q", i=4), SSm.rearrange("p (i q) -> p i q", i=4)[:, :, NS:3 * NS], ["SSm"], "st_ss")
        S.op("dve", lambda e: e.tensor_copy(out=PST[:, 120:128], in_=FIX[:, 0:8]), reads=[f"FIXu{i}" for i in range(4)], writes=["PSTu"])
        fenceA()
        R2 = ["Afree"]

        def fpay(e):
            e.tensor_copy(out=PAY[:, 0:1024], in_=SLE)
            return e.tensor_copy(out=PAY[:, 1024:1152], in_=PST)
        S.op("dve", fpay, reads=[f"SLE{h}" for h in range(NH)] + ["PSTa", "PSTu"] + R2, writes=["PAY"])
        S.op("sp", lambda e: e.dma_start(out=pay_in[l].ap(), in_=PAY), reads=["PAY"] + R2, writes=["payin"], dma=f"st_pay{l}")
        S.op("pool", lambda e: e.collective_compute("AllGather", ALU.bypass, replica_groups=[[0, 1], [2, 3], [4, 5], [6, 7]],
                                                    ins=[pay_in[l].ap().opt()], outs=[pay_out[l].ap().opt()]),
             reads=["payin"], writes=["payout"], dma=f"cc{l}")
        S.op("sp", lambda e: e.dma_start(out=PIN, in_=pay_out[l].ap()[0:128, :]), reads=["payout"] + R2, writes=["PIN"], dma=f"ld_pin{l}")
        o_fl = PV["flag"][0]
        S.op("dve", lambda e: e.tensor_scalar(out=PIN, in0=PIN, scalar1=PVt[:, o_fl:o_fl + 1], scalar2=None, op0=ALU.mult),
             reads=["PIN", "PV"], writes=["PIN"])
        store("sp", sp_o[l], PST[:, 120:128], ["PSTu"], "st_sp")
        store("sp", cp_o[l], PST[:, 0:120], ["PSTa"], "st_cp")
        for h in range(NH):
            S.op("dve", lambda e, h=h: e.tensor_copy(out=SINB, in_=PIN[:, h * 128:(h + 1) * 128]), reads=["PIN"], writes=["SINB"])

            for half in range(2):
                S.op("pe", lambda e, h=h, half=half: e.matmul(B6, lhsT=SINB, rhs=mixT[:, h, 512 * half:512 * half + 512], start=True, stop=True),
                     reads=["SINB", f"mix{h}"], writes=["B6"])
                S.op("dve", lambda e, h=h, half=half: e.tensor_tensor(out=OL[:, h, 512 * half:512 * half + 512], in0=B6,
                                                                     in1=OL[:, h, 512 * half:512 * half + 512], op=ALU.add),
                     reads=[f"OL{h}"], writes=[f"OL{h}", "B6"])
            S.op("dve", lambda e, h=h: e.scalar_tensor_tensor(out=SLE[:, h * 128:(h + 1) * 128], in0=PIN[:, h * 128:(h + 1) * 128],
                                                              scalar=EBE[:, h:h + 1], in1=SLE[:, h * 128:(h + 1) * 128], op0=ALU.mult, op1=ALU.add),
                 reads=["PIN", f"EBE{h}", "PAY"], writes=[f"SLE{h}"])

            def fsq(e, h=h):
                e.activation(out=TSQ[:, 0:T], in_=OL[:, h, :], func=AF.Square)
                return e.activation(out=TSQ[:, T:TT], in_=OTS[:, h * NS:(h + 1) * NS], func=AF.Square)
            S.op("act", fsq, reads=[f"OL{h}", f"OTS{h}"] + R2, writes=["TSQ"])
            slot = next_slot()
            mm_chunk(slot, lambda k: onesb, lambda k, t0, tw: TSQ[:, t0:t0 + tw], 1, reads=["TSQ", "ones"])
            ew3("act", slot, lambda e, ps, t0, tw: e.activation(out=TRS[:, t0:t0 + tw], in_=ps, func=AF.Ln, bias=float(128 * EPS)),
                reads=R2, writes=["TRS", f"slot{slot}"])
            S.op("act", lambda e: e.activation(out=TRS, in_=TRS, func=AF.Exp, scale=-0.5), reads=["TRS"], writes=["TRS"])

            def fn_a(e, h=h):
                e.tensor_tensor(out=OL[:, h, :], in0=OL[:, h, :], in1=TRS[:, 0:T], op=ALU.mult)
                return e.tensor_tensor(out=OTS[:, h * NS:(h + 1) * NS], in0=OTS[:, h * NS:(h + 1) * NS], in1=TRS[:, T:TT], op=ALU.mult)

            def fn_b(e, h=h):
                e.scalar_tensor_tensor(out=mixT[:, h, 0:T], in0=OL[:, h, :], scalar=HGs[:, l * 8 + h:l * 8 + h + 1], in1=SOG[:, h, 0:T],
                                       op0=ALU.mult, op1=ALU.mult)
                return e.scalar_tensor_tensor(out=mixT[:, h, T:TT], in0=OTS[:, h * NS:(h + 1) * NS], scalar=HGs[:, l * 8 + h:l * 8 + h + 1],
                                              in1=SOG[:, h, T:TT], op0=ALU.mult, op1=ALU.mult)
            chain("dve", [fn_a, fn_b], ["TRS", f"OL{h}", f"OTS{h}", f"SOG{h}", "HGs"], [f"OL{h}", f"OTS{h}", f"mix{h}"])
        store("sp", hp_o[l], SLE, [f"SLE{h}" for h in range(NH)], "st_hp")
        HB = FIX[:, 0:60]
        C30 = FIX[:, 64:64 + 120].rearrange("p (i t) -> p i t", i=4)
        XB30 = FIX[:, 192:192 + 60].bitcast(BF16)
        XQ30 = FIX[:, 256:256 + 60].bitcast(BF16)
        M30, R30, Y30, E30 = FIX[:, 320:350], FIX[:, 352:382], FIX[:, 384:414], FIX[:, 416:446]
        for i in range(4):
            o_w = PV["ccw"][0] + (l * 4 + i) * 31

            def fx0(e, i=i):
                e.memset(HB[:, 30:60], 0.0)
                e.tensor_copy(out=HB[:, 0:30], in_=PIN[:, 1024 + 30 * i:1024 + 30 * i + 30])
                return e.tensor_copy(out=C30[:, i, :], in_=KEEP[:, 30 * i:30 * i + 30])
            fl_ = [fx0] + [lambda e, i=i, j=j, o_w=o_w: e.scalar_tensor_tensor(out=C30[:, i, :], in0=HB[:, j:j + 30], scalar=PVt[:, o_w + j:o_w + j + 1],
                                                                             in1=C30[:, i, :], op0=ALU.mult, op1=ALU.add) for j in range(30)]
            chain("dve", fl_, ["PIN", f"KEEPc{i}", "PV", "PSTu"], ["HB", f"C30_{i}"])
        S.op("act", lambda e: e.activation(out=XB30, in_=FIX[:, 64:184], func=AF.Copy), reads=[f"C30_{i}" for i in range(4)], writes=["XB30"])
        S.op("act", lambda e: e.activation(out=XQ30, in_=FIX[:, 64:184], func=AF.Square), reads=[f"C30_{i}" for i in range(4)], writes=["XQ30"])

        def fl(e):
            ins = None
            for i in range(4):
                e.matmul(PS[6][:, 64:94], lhsT=onesb, rhs=XB30[:, 30 * i:30 * i + 30], start=(i == 0), stop=(i == 3))
            for i in range(4):
                ins = e.matmul(PS[6][:, 128:158], lhsT=onesb, rhs=XQ30[:, 30 * i:30 * i + 30], start=(i == 0), stop=(i == 3))
            return ins
        S.op("pe", fl, reads=["XB30", "XQ30", "ones"], writes=["B6"])

        def fl2a(e):
            e.tensor_scalar(out=M30, in0=PS[6][:, 64:94], scalar1=1.0 / 512, scalar2=None, op0=ALU.mult)
            return e.tensor_scalar(out=R30, in0=PS[6][:, 128:158], scalar1=1.0 / 512, scalar2=None, op0=ALU.mult)
        chain("dve", [fl2a, lambda e: e.tensor_tensor(out=Y30, in0=M30, in1=M30, op=ALU.mult),
                      lambda e: e.tensor_tensor(out=R30, in0=R30, in1=Y30, op=ALU.subtract)],
              ["Y30"], ["M30", "B6", "Y30"])
        act_rsqrt_inplace(R30, "M30", EPS)
        for i in range(4):
            o_g = PV["clg"][0] + l * 4 + i
            o_b = PV["clb"][0] + l * 4 + i

            chain("dve", [lambda e, i=i: e.tensor_tensor(out=Y30, in0=C30[:, i, :], in1=M30, op=ALU.subtract),
                          lambda e: e.tensor_tensor(out=Y30, in0=Y30, in1=R30, op=ALU.mult),
                          lambda e, o_g=o_g, o_b=o_b: e.tensor_scalar(out=Y30, in0=Y30, scalar1=PVt[:, o_g:o_g + 1], scalar2=PVt[:, o_b:o_b + 1],
                                                                      op0=ALU.mult, op1=ALU.add)],
                  ["M30", f"C30_{i}", "PV"], ["Y30"])
            S.op("act", lambda e: e.activation(out=E30, in_=Y30, func=AF.Exp, scale=-1.0), reads=["Y30"], writes=["E30"])
            act_sigmoid_from_exp(E30, "E30")

            def f2(e, i=i):
                return e.tensor_tensor(out=mixT[:, 12 + i, 0:30], in0=Y30, in1=E30, op=ALU.mult)
            S.op("dve", f2, reads=["Y30", "E30"], writes=["E30", "Y30", f"mix{12 + i}"])
            o_w = PV["scw"][0] + (l * 4 + i) * 3
            h0 = PIN[:, 1144 + 2 * i:1144 + 2 * i + 1]
            h1 = PIN[:, 1144 + 2 * i + 1:1144 + 2 * i + 2]
            k0 = KEEP[:, 120 + 2 * i:121 + 2 * i]
            k1 = KEEP[:, 121 + 2 * i:122 + 2 * i]
            sc2 = FIX[:, 448 + 2 * i:450 + 2 * i]

            w0, w1 = PVt[:, o_w:o_w + 1], PVt[:, o_w + 1:o_w + 2]

            def f3a(e, h0=h0, h1=h1, k0=k0, k1=k1, sc2=sc2, w0=w0):
                e.scalar_tensor_tensor(out=sc2[:, 0:1], in0=h0, scalar=w0, in1=k0, op0=ALU.mult, op1=ALU.add)
                return e.scalar_tensor_tensor(out=sc2[:, 1:2], in0=h1, scalar=w0, in1=k1, op0=ALU.mult, op1=ALU.add)
            chain("dve", [f3a,
                          lambda e, h1=h1, sc2=sc2, w1=w1: e.scalar_tensor_tensor(out=sc2[:, 0:1], in0=h1, scalar=w1, in1=sc2[:, 0:1],
                                                                                  op0=ALU.mult, op1=ALU.add),
                          lambda e, i=i, sc2=sc2: e.tensor_tensor(out=mixT[:, 8 + i, 0:2], in0=sc2, in1=KEEP[:, 128 + 2 * i:130 + 2 * i], op=ALU.mult)],
                  ["PIN", f"KEEPs{i}", f"KEEPb{i}", "PV", "PSTu"], [f"mix{8 + i}", f"sc2_{i}"])
        if DBG == 4:
            return
        S.op("dve", lambda e: e.memset(FNC[:, 4:5], 0.0), reads=[], writes=["Afree", "xT", "MS", "SMXf5"])
        MSTATE["owner"] = "xold"
        mixres = [f"mix{m}" for m in range(16)]
        for b in range(4):
            wi = get_w(l, 13 + b)
            Wb = W[wi].rearrange("p (c k m) -> p c k m", c=4, k=KC)
            for ci in range(4):
                dc = 4 * b + ci
                slot = next_slot()
                xo = XO[dc % 2]
                S.op("sp", lambda e, dc=dc, xo=xo: e.dma_start(out=xo, in_=xold_src[:, dc, :]), reads=["xspill", "MS"], writes=[f"XO{dc % 2}"],
                     dma=f"ld_xo{dc % 2}")
                mm_chunk(slot, lambda k, Wb=Wb, ci=ci: Wb[:, ci, k, :], lambda k, t0, tw: mixT[:, k, t0:t0 + tw], KC, reads=mixres + [f"W{wi}"])
                ew3("dve", slot, lambda e, ps, t0, tw, dc=dc, xo=xo: e.tensor_tensor(out=xT[:, dc, t0:t0 + tw], in0=ps, in1=xo[:, t0:t0 + tw], op=ALU.add),
                    reads=[f"XO{dc % 2}", "Afree"], writes=[f"x{dc}", f"slot{slot}"])
        xres = [f"x{dc}" for dc in range(16)]
        S.op("dve", lambda e: e.memset(FNC[:, 5:6], 0.0), reads=xres, writes=["xT", "SMXf6"])
        rmsnorm(2 + l, lambda dc: hT[:, dc, :], "hT")
        S.op("dve", lambda e: e.memset(FNC[:, 6:7], 0.0), reads=[], writes=["MS", "SMXf7"])
        MSTATE["owner"] = "hid"

        def up(g):
            wi = get_w(l, 17 + g)
            Wb = W[wi].rearrange("p (c k m) -> p c k m", c=4, k=KC)
            hid = HID[g % 2]
            for ci in range(4):
                slot = next_slot()
                mm_chunk(slot, lambda k, Wb=Wb, ci=ci: Wb[:, ci, k, :], lambda k, t0, tw: hT[:, k, t0:t0 + tw], KC, reads=["hT", f"W{wi}"])
                ew3("act", slot, lambda e, ps, t0, tw, ci=ci, hid=hid: e.activation(out=hid[:, ci, t0:t0 + tw], in_=ps, func=AF.Relu),
                    reads=["MS"], writes=[f"HID{g % 2}"])
                ew3("dve", slot, lambda e, ps, t0, tw, ci=ci, hid=hid: e.tensor_tensor(out=hid[:, ci, t0:t0 + tw], in0=ps, in1=hid[:, ci, t0:t0 + tw],
                                                                                     op=ALU.mult),
                    reads=["MS", f"HID{g % 2}"], writes=[f"HID{g % 2}", f"slot{slot}"])

        def down(g):
            wi = get_w(l, 33 + g)
            Wd = W[wi].rearrange("p (d f m) -> p d f m", d=16, f=4)
            hid = HID[g % 2]
            for dc in range(16):
                slot = next_slot()
                mm_chunk(slot, lambda k, Wd=Wd, dc=dc: Wd[:, dc, k, :], lambda k, t0, tw, hid=hid: hid[:, k, t0:t0 + tw], 4,
                         reads=[f"HID{g % 2}", f"W{wi}"])
                ew3("dve", slot, lambda e, ps, t0, tw, dc=dc: e.tensor_tensor(out=xT[:, dc, t0:t0 + tw], in0=ps, in1=xT[:, dc, t0:t0 + tw], op=ALU.add),
                    reads=["xT"], writes=[f"x{dc}", f"slot{slot}"])
        up(0)
        up(1)
        for g in range(16):
            down(g)
            if g + 2 < 16:
                up(g + 2)
        S.op("dve", lambda e: e.memset(FNC[:, 7:8], 0.0), reads=xres, writes=["xT", "SMXf8"])

    out_streams = []
    _CACHE["build"] = (nc, S, out_streams)
    dumpctr = {"n": 0}

    def dump(name, ap, reads):
        shape = list(ap.shape)
        d = nc.dram_tensor("dbg_" + name, shape, F32, kind="ExternalOutput").ap()
        dumpctr["n"] += 1
        if ap.dtype == F32:
            store("sp", d, ap, reads, f"st_dbg{dumpctr['n']}")
        else:
            S.op("pool", lambda e: e.dma_start(out=d, in_=ap), reads=reads, writes=[], dma=f"st_dbg{dumpctr['n']}")
            out_streams.append(f"st_dbg{dumpctr['n']}")

    def store(eng, out, in_, reads, stream):
        S.op(eng, lambda e: e.dma_start(out=out, in_=in_), reads=reads, writes=[], dma=stream)
        if stream not in out_streams:
            out_streams.append(stream)

    for l in range(DEPTH if DBG in (5, 99) else 1):
        LB = LBt[:, 8 * l:8 * l + 8]
        OML = LBt[:, 16 + 8 * l:16 + 8 * l + 8]
        NOML = LBt[:, 32 + 8 * l:32 + 8 * l + 8]
        rmsnorm(l, lambda dc: hT[:, dc, :], "hT")
        if DBG == 1:
            dump("hT", hT, ["hT"])
            dump("LBt", LBt, ["LB"])
            dump("Gs", Gs, ["Gs"])
            raise _Stop()
        if l == 1:
            S.op("sp", lambda e: e.dma_start(out=xs_d, in_=xT), reads=["xT"], writes=["xspill"], dma="st_xs")
        xold_src = xT_d if l == 0 else xs_d
        S.op("dve", lambda e: e.memset(FNC[:, 1:2], 0.0), reads=[], writes=["xT", "Afree", "SMXf2"])
        AF_R = ["Afree"]

        S.op("sp", lambda e, l=l: e.dma_start(out=SSm.rearrange("p (i q) -> p i q", i=4)[:, :, 0:2 * NS],
                                              in_=ss_d[l].rearrange("p (i q) -> p i q", i=4)),
             writes=["SSm"], dma="ld_ss")
        S.op("sp", lambda e, l=l: e.dma_start(out=SCm.rearrange("p (i q) -> p i q", i=4)[:, :, 0:30 * NS],
                                              in_=sc_d[l].rearrange("p (i q) -> p i q", i=4)),
             reads=AF_R, writes=["SCm"], dma="ld_sc")
        SS3 = SSm.rearrange("p (i r n) -> p i r n", i=4, r=3)
        SC31 = SCm.rearrange("p (i r n) -> p i r n", i=4, r=31)

        S.op("dve", lambda e: e.memset(APAD[:, :, 0:30], 0.0), reads=AF_R, writes=["APAD"])
        S.op("dve", lambda e: e.memset(UPAD[:, 0:2], 0.0), reads=AF_R, writes=["UPAD"])

        pend = {}
        for b in range(13):
            wcur = get_w(l, b)
            Wb = W[wcur].rearrange("p (c k m) -> p c k m", c=4, k=KC)
            for ci in range(4):
                role, idx = ROLES[4 * b + ci]
                slot = next_slot()
                mm_chunk(slot, lambda k, Wb=Wb, ci=ci: Wb[:, ci, k, :], lambda k, t0, tw: hT[:, k, t0:t0 + tw], KC,
                         reads=["hT", f"W{wcur}"], hook=((lambda: (pop_steps(2), pop_el(1))) if b >= 5 else None))
                if role == "ccg":
                    ew3("act", slot, lambda e, ps, t0, tw: e.activation(out=CT1[:, t0:t0 + tw], in_=ps, func=AF.Exp, scale=-1.0),
                        reads=AF_R, writes=["CT1"])
                    act_sigmoid_from_exp(CT1, "CT1")
                    vslot = pend.pop(("ccv", idx))
                    i = idx
                    vregs = slot_regs(vslot)

                    def fa(e, i=i, vregs=vregs):
                        ins = None
                        for (ps, a0, a1) in pieces(vregs):
                            dst = APAD[:, i, 30 + a0:30 + a1] if a0 < T else SC31[:, i, 30, :]
                            ins = e.tensor_tensor(out=dst, in0=ps, in1=CT1[:, a0:a1], op=ALU.mult)
                        return ins
                    S.op("dve", fa, reads=[f"slot{vslot}", "CT1"] + AF_R, writes=["APAD", "SCm", f"slot{vslot}"])
                    o_w = PV["ccw"][0] + (l * 4 + i) * 31
                    o_b = PV["ccb"][0] + l * 4 + i
                    ceng = "dve"

                    fcl = [lambda e, i=i, o_w=o_w, o_b=o_b: e.tensor_scalar(out=CO[:, i, 0:T], in0=APAD[:, i, 30:30 + T],
                                                                            scalar1=PVt[:, o_w + 30:o_w + 31], scalar2=PVt[:, o_b:o_b + 1],
                                                                            op0=ALU.mult, op1=ALU.add)]
                    fcl += [lambda e, i=i, j=j, o_w=o_w: e.scalar_tensor_tensor(out=CO[:, i, 0:T], in0=APAD[:, i, j:j + T],
                                                                                scalar=PVt[:, o_w + j:o_w + j + 1], in1=CO[:, i, 0:T],
                                                                                op0=ALU.mult, op1=ALU.add) for j in range(30)]
                    chain(ceng, fcl, ["APAD", "PV"] + AF_R, [f"CO{i}"])
                    wv = PVt[:, o_w:o_w + 31]

                    chain("dve", [lambda e, i=i, wv=wv: e.tensor_tensor(out=SMX[:, 0:NS * 31].rearrange("p (r n) -> p r n", r=31), in0=SC31[:, i, :, :],
                                                                        in1=wv.unsqueeze(2).to_broadcast([128, 31, NS]), op=ALU.mult),
                                  lambda e, i=i: e.tensor_reduce(out=CO[:, i, T:TT], in_=SMX[:, 0:NS * 31].rearrange("p (r n) -> p n r", r=31),
                                                                 axis=AX.X, op=ALU.add),
                                  lambda e, i=i, o_b=o_b: e.tensor_scalar(out=CO[:, i, T:TT], in0=CO[:, i, T:TT], scalar1=PVt[:, o_b:o_b + 1],
                                                                          scalar2=None, op0=ALU.add)],
                          ["SCm", "PV"] + AF_R, [f"CO{i}s", "SMX"])
                    S.op("dve", lambda e, i=i: e.tensor_copy(out=KEEP[:, 30 * i:30 * i + 30], in_=CO[:, i, 0:30]),
                         reads=[f"CO{i}"], writes=[f"KEEPc{i}"])
                elif role == "ccv":
                    pend[("ccv", idx)] = slot
                elif role == "scc":
                    ew3("act", slot, lambda e, ps, t0, tw: e.activation(out=CT2[:, t0:t0 + tw], in_=ps, func=AF.Copy),
                        reads=AF_R, writes=["CT2"])
                elif role == "sch":
                    i = idx
                    regs = slot_regs(slot)

                    def fu(e, i=i, regs=regs):
                        ins = None
                        for (ps, a0, a1) in pieces(regs):
                            dst = UPAD[:, 2 + a0:2 + a1] if a0 < T else SS3[:, i, 2, :]
                            ins = e.tensor_tensor(out=dst, in0=ps, in1=CT2[:, a0:a1], op=ALU.mult)
                        return ins
                    S.op("dve", fu, reads=[f"slot{slot}", "CT2"] + AF_R, writes=["UPAD", "SSm", f"slot{slot}"])
                    o_w = PV["scw"][0] + (l * 4 + i) * 3

                    def fcs0(e, i=i, o_w=o_w):
                        e.tensor_scalar(out=CT3[:, 0:T], in0=UPAD[:, 2:2 + T], scalar1=PVt[:, o_w + 2:o_w + 3], scalar2=None, op0=ALU.mult)
                        e.tensor_tensor(out=SMX[:, 0:NS * 3].rearrange("p (r n) -> p r n", r=3), in0=SS3[:, i, :, :],
                                        in1=PVt[:, o_w:o_w + 3].unsqueeze(2).to_broadcast([128, 3, NS]), op=ALU.mult)
                        return e.tensor_copy(out=FIX[:, 2 * i:2 * i + 2], in_=UPAD[:, 1024:1026])

                    def fcs1(e, i=i, o_w=o_w):
                        e.tensor_reduce(out=CT3[:, T:TT], in_=SMX[:, 0:NS * 3].rearrange("p (r n) -> p n r", r=3), axis=AX.X, op=ALU.add)
                        return e.scalar_tensor_tensor(out=CT3[:, 0:T], in0=UPAD[:, 1:1 + T], scalar=PVt[:, o_w + 1:o_w + 2], in1=CT3[:, 0:T],
                                                      op0=ALU.mult, op1=ALU.add)
                    chain("dve", [fcs0, fcs1,
                                  lambda e, o_w=o_w: e.scalar_tensor_tensor(out=CT3[:, 0:T], in0=UPAD[:, 0:T], scalar=PVt[:, o_w:o_w + 1], in1=CT3[:, 0:T],
                                                                            op0=ALU.mult, op1=ALU.add),
                                  lambda e, i=i: e.tensor_copy(out=KEEP[:, 120 + 2 * i:122 + 2 * i], in_=CT3[:, 0:2])],
                          ["UPAD", "SSm", "PV"] + AF_R, ["CT3", "SMX", f"KEEPs{i}", f"FIXu{i}", "UPADr"])
                elif role == "scb":
                    i = idx
                    regs = slot_regs(slot)

                    def fy(e, i=i, regs=regs):
                        e.tensor_copy(out=KEEP[:, 128 + 2 * i:130 + 2 * i], in_=regs[0][:, 0:2])
                        ins = None
                        for (ps, a0, a1) in pieces(regs):
                            ins = e.tensor_tensor(out=mixT[:, 8 + i, a0:a1], in0=ps, in1=CT3[:, a0:a1], op=ALU.mult)
                        return ins
                    S.op("dve", fy, reads=[f"slot{slot}", "CT3"], writes=[f"mix{8 + i}", f"KEEPb{i}", f"slot{slot}", "UPAD"])
                elif role == "q":
                    if idx == 0:
                        ms_fence("sog")
                    head_q(l, idx, slot, AF_R)
                elif role == "f":
                    head_f(l, idx, slot, AF_R, LB, OML, NOML)
                elif role == "v":
                    head_v(l, idx, slot, AF_R)
                elif role == "og":
                    head_og(l, idx, slot, AF_R)
            if b == 4:
                conformer_ln(l, AF_R)
                store("sp", sc_o[l].rearrange("p (i q) -> p i q", i=4), SCm.rearrange("p (i q) -> p i q", i=4)[:, :, NS:31 * NS], ["SCm"], "st_sc")
                S.op("dve", lambda e: e.memset(FNC[:, 9:10], 0.0), reads=[], writes=["Afree", "SCm", "SSmX", "FNC9"])
                if DBG == 2:
                    dump("mixT", mixT[:, 8:16, :], [f"mix{m}" for m in range(8, 16)])
                    dump("PST", PST[:, 0:120], ["PSTa"])
                    dump("KEEP", KEEP, [f"KEEPc{i}" for i in range(4)] + [f"KEEPs{i}" for i in range(4)] + [f"KEEPb{i}" for i in range(4)])
                    dump("FIX", FIX[:, 0:8], [f"FIXu{i}" for i in range(4)])
                    store("sp", ss_o[l].rearrange("p (i q) -> p i q", i=4), SSm.rearrange("p (i q) -> p i q", i=4)[:, :, NS:3 * NS], ["SSm"], "st_ss")
                    raise _Stop()
        pop_steps(len(STEPS))
        if DBG == 3:
            dump("OL", OL, [f"OL{h}" for h in range(NH)])
            dump("SLE", SLE, [f"SLE{h}" for h in range(NH)])
            dump("OTS", OTS, [f"OTS{h}" for h in range(NH)])
            dump("QH", mixT[:, 0:8, 0:T], [f"mix{h}" for h in range(NH)])
            dump("SOG", SOG, [f"SOG{h}" for h in range(NH)])
            raise _Stop()
        tail_and_rest(l, AF_R, xold_src)
        if DBG == 6 or DBG == 4:
            raise _Stop()

    rmsnorm(4, lambda dc: xT[:, dc, :], "xT")
    store("sp", yT_o, xT, ["xT"], "st_y")
    return nc, S, out_streams


def emit(nc, S, out_streams):
    S.finalize()
    import contextlib
    with contextlib.ExitStack() as st:
        sems = {}
        for e in S.ops:
            sems[e] = st.enter_context(nc.semaphore(f"s_{e}"))
        for k in S.streams:
            sems[k] = st.enter_context(nc.semaphore(f"d_{k}"))
        block = st.enter_context(nc.Block())

        def wval(k, i):
            if k in S.streams:
                return (i + 1) if k.startswith("cc") else 16 * (i + 1)
            return S.sigcount[k][i]

        def run(engname, eng):
            for rec in S.ops[engname]:
                for (k, i) in rec["waits"]:
                    eng.wait_ge(sems[k], wval(k, i))
                ins = rec["fn"](eng)
                if rec["dma"] is not None:
                    ins.then_inc(sems[rec["dma"]], 1 if rec["dma"].startswith("cc") else 16)
                elif rec["sig"]:
                    ins.then_inc(sems[engname], 1)
            if engname == "sp":
                for k in out_streams:
                    eng.wait_ge(sems[k], 16 * S.streams[k])

        @block.sync
        def _(e):
            run("sp", e)

        @block.scalar
        def _(e):
            run("act", e)

        @block.vector
        def _(e):
            run("dve", e)

        @block.gpsimd
        def _(e):
            run("pool", e)

        @block.tensor
        def _(e):
            run("pe", e)


_CACHE = {}


def _pack_weights(w_in, w_out, w_up, w_down):
    wall = np.empty((DEPTH * NBLK, 128, WBLK), np.float32)
    cols = np.concatenate([np.arange(_COL0[r] + 128 * i, _COL0[r] + 128 * i + 128) for (r, i) in ROLES])

    def lhs_blocks(Wm, nblk):
        ncols = Wm.shape[1]
        t = Wm.reshape(KC, 128, ncols // 512, 4, 128)
        return np.ascontiguousarray(t.transpose(2, 1, 3, 0, 4)).reshape(ncols // 512, 128, WBLK)
    for l in range(DEPTH):
        base = l * NBLK
        wall[base:base + 13] = lhs_blocks(w_in[l][:, cols], 13)
        wall[base + 13:base + 17] = lhs_blocks(w_out[l], 4)
        wall[base + 17:base + 33] = lhs_blocks(w_up[l], 16)
        t = w_down[l].reshape(16, 4, 128, 16, 128)
        wall[base + 33:base + 49] = np.ascontiguousarray(t.transpose(0, 2, 3, 1, 4)).reshape(16, 128, WBLK)
    return wall


def kernel(x_prompt, x_sample, state_hgrn, state_sconv, state_cconv, g_mix, w_in, hgrn_lb, hgrn_norm_g, sconv_w,
           cconv_w, cconv_b, cconv_ln_g, cconv_ln_b, w_out, g_mlp, w_up, w_down, g_final):
    f = lambda a: np.asarray(a, dtype=np.float32)
    x_prompt, x_sample = f(x_prompt), f(x_sample)
    wall = _pack_weights(f(w_in), f(w_out), f(w_up), f(w_down))
    pv = np.zeros((128, NPV), np.float32)

    def put(name, arr):
        o, w = PV[name]
        pv[:, o:o + w] = arr.reshape(128, w)
    g5 = np.stack([f(g_mix)[0], f(g_mix)[1], f(g_mlp)[0], f(g_mlp)[1], f(g_final)])
    put("g", g5.reshape(5, 16, 128).transpose(2, 0, 1))
    put("lb", f(hgrn_lb).reshape(2, 8, 128).transpose(2, 0, 1))
    put("hg", f(hgrn_norm_g).reshape(2, 8, 128).transpose(2, 0, 1))
    put("scw", f(sconv_w).reshape(2, 3, 4, 128).transpose(3, 0, 2, 1))
    put("ccw", f(cconv_w).reshape(2, 31, 4, 128).transpose(3, 0, 2, 1))
    put("ccb", f(cconv_b).reshape(2, 4, 128).transpose(2, 0, 1))
    put("clg", f(cconv_ln_g).reshape(2, 4, 128).transpose(2, 0, 1))
    put("clb", f(cconv_ln_b).reshape(2, 4, 128).transpose(2, 0, 1))
    cv = np.zeros((128, NCV), np.float32)
    cv[:, CV["ident"][0]:CV["ident"][0] + 128] = np.eye(128, dtype=np.float32)
    caus = (np.arange(64)[:, None] <= np.arange(64)[None, :]).astype(np.float32)
    cv[:, CV["causal"][0]:CV["causal"][0] + 64] = np.concatenate([caus, caus], 0)
    sm = np.ones(1024, np.float32)
    sm[::64] = 0.0
    cv[:, CV["smask"][0]:CV["smask"][0] + 1024] = sm[None, :]

    sh, ssv, scv = f(state_hgrn), f(state_sconv), f(state_cconv)
    in_maps = []
    for c in range(NCORES):
        j, half = c // 2, c % 2
        xc = np.concatenate([x_prompt[j, half * T:(half + 1) * T], x_sample[NS * c:NS * (c + 1), 0]], 0)
        xTc = np.ascontiguousarray(xc.reshape(TT, KC, 128).transpose(2, 1, 0))
        pvc = pv.copy()
        pvc[:, PV["flag"][0]] = float(half)
        shc = np.ascontiguousarray(sh[:, NS * c:NS * (c + 1)].transpose(0, 2, 3, 1, 4)).reshape(DEPTH, NH, 128, NS * 128)
        ssc = np.ascontiguousarray(ssv[:, NS * c:NS * (c + 1)].reshape(DEPTH, NS, 2, 4, 128).transpose(0, 4, 3, 2, 1)).reshape(DEPTH, 128, 4 * NS * 2)
        scc = np.ascontiguousarray(scv[:, NS * c:NS * (c + 1)].reshape(DEPTH, NS, 30, 4, 128).transpose(0, 4, 3, 2, 1)).reshape(DEPTH, 128, 4 * NS * 30)
        in_maps.append({"xT": xTc, "wall": wall, "pv": pvc, "cv": cv, "sh": shc, "ss": ssc, "sc": scc})
    if DBG not in (5, 6, 99):
        wall = wall[:13]
        for m in in_maps:
            m["wall"] = wall
    if "nc" not in _CACHE:
        try:
            build_program()
        except _Stop:
            pass
        nc, S, outs = _CACHE["build"]
        emit(nc, S, outs)
        _CACHE["nc"] = nc
    nc = _CACHE["nc"]
    res = run_bass_kernel_spmd(nc, in_maps, core_ids=list(range(NCORES)))
    R = res.results
    _CACHE["results"] = R
    y_prompt = np.empty((4, 2048, D), np.float32)
    y_sample = np.empty((128, 1, D), np.float32)
    hp = np.empty((DEPTH, 4, NH, 128, 128), np.float32)
    sp = np.empty((DEPTH, 4, 2, 512), np.float32)
    cp = np.empty((DEPTH, 4, 30, 512), np.float32)
    hs = np.empty((DEPTH, 128, NH, 128, 128), np.float32)
    sso = np.empty((DEPTH, 128, 2, 512), np.float32)
    sco = np.empty((DEPTH, 128, 30, 512), np.float32)
    for c in range(NCORES):
        j, half = c // 2, c % 2
        yc = R[c]["yT"].transpose(2, 1, 0).reshape(TT, D)
        y_prompt[j, half * T:(half + 1) * T] = yc[:T]
        y_sample[NS * c:NS * (c + 1), 0] = yc[T:]
        hs[:, NS * c:NS * (c + 1)] = R[c]["hs"].reshape(DEPTH, NH, 128, NS, 128).transpose(0, 3, 1, 2, 4)
        sso[:, NS * c:NS * (c + 1)] = R[c]["sso"].reshape(DEPTH, 128, 4, 2, NS).transpose(0, 4, 3, 2, 1).reshape(DEPTH, NS, 2, 512)
        sco[:, NS * c:NS * (c + 1)] = R[c]["sco"].reshape(DEPTH, 128, 4, 30, NS).transpose(0, 4, 3, 2, 1).reshape(DEPTH, NS, 30, 512)
        if half == 1:
            hp[:, j] = R[c]["hp"].reshape(DEPTH, 128, NH, 128).transpose(0, 2, 1, 3)
            sp[:, j] = R[c]["sp"].reshape(DEPTH, 128, 4, 2).transpose(0, 3, 2, 1).reshape(DEPTH, 2, 512)
            cp[:, j] = R[c]["cp"].reshape(DEPTH, 128, 4, 30).transpose(0, 3, 2, 1).reshape(DEPTH, 30, 512)
    return (y_prompt, y_sample, hp, sp, cp, hs, sso, sco)
```

```python
import numpy as np
import concourse.bass as bass
import concourse.mybir as mybir
from concourse.bass_utils import run_bass_kernel_spmd

F32 = mybir.dt.float32
BF16 = mybir.dt.bfloat16
AF = mybir.ActivationFunctionType
ALU = mybir.AluOpType
AX = mybir.AxisListType

import os
NCORES = 8
DBG = int(os.environ.get("KDBG", "99"))


class _Stop(Exception):
    pass

D = 2048
T = 1024
NS = 16
TT = T + NS
KC = 16
DEPTH = 2
EPS = 1e-6
NH = 8
TILES = [(0, 352), (352, 352), (704, 336)]
NBLK = 49
WBLK = 8192

ROLES = []
for i in (0, 1):
    ROLES += [("ccv", 2 * i), ("ccg", 2 * i), ("ccv", 2 * i + 1), ("ccg", 2 * i + 1)]
_sc = []
for i in range(4):
    _sc += [("scc", i), ("sch", i), ("scb", i)]
ROLES += _sc
for h in range(NH):
    ROLES += [("f", h), ("q", h), ("v", h), ("og", h)]
assert len(ROLES) == 52
_COL0 = {"q": 0, "f": 1024, "v": 2048, "og": 3072, "scb": 4096, "scc": 4608, "sch": 5120, "ccv": 5632, "ccg": 6144}

PV = {}
_o = 0
for _n, _w in [("g", 5 * 16), ("lb", 2 * 8), ("hg", 2 * 8), ("scw", 2 * 4 * 3), ("ccw", 2 * 4 * 31),
               ("ccb", 2 * 4), ("clg", 2 * 4), ("clb", 2 * 4), ("flag", 1)]:
    PV[_n] = (_o, _w)
    _o += _w
NPV = _o
CV = {}
_o = 0
for _n, _w in [("ident", 128), ("causal", 64), ("smask", 1024)]:
    CV[_n] = (_o, _w)
    _o += _w
NCV = _o


class Sched:
    def __init__(self):
        self.ops = {e: [] for e in ("pe", "act", "dve", "pool", "sp")}
        self.streams = {}
        self.last_w = {}
        self.readers = {}
        self.known = {e: {} for e in self.ops}

    def op(self, eng, fn, reads=(), writes=(), dma=None, nowait_self=False):
        need = {}

        def add(ev):
            if ev is None:
                return
            k, i = ev
            if need.get(k, -1) < i:
                need[k] = i
        for r in reads:
            add(self.last_w.get(r))
        for w in writes:
            add(self.last_w.get(w))
            for ev in list(self.readers.get(w, {}).items()):
                add(ev)
        waits = []
        for k, i in need.items():
            if nowait_self and k == eng:
                continue
            if k in self.streams:
                i = max(i, self.streams[k] - 1)
            if self.known[eng].get(k, -1) >= i:
                continue
            self.known[eng][k] = i
            waits.append((k, i))
        rec = {"fn": fn, "waits": waits, "dma": dma, "sig": False}
        self.ops[eng].append(rec)
        if dma is not None:
            self.streams.setdefault(dma, 0)
            ev = (dma, self.streams[dma])
            self.streams[dma] += 1
        else:
            ev = (eng, len(self.ops[eng]) - 1)
        for r in reads:
            lst = self.readers.setdefault(r, {})
            if lst.get(ev[0], -1) < ev[1]:
                lst[ev[0]] = ev[1]
        for w in writes:
            self.last_w[w] = ev
            self.readers[w] = {}
        return ev

    def finalize(self):
        for e, lst in self.ops.items():
            for rec in lst:
                for (k, i) in rec["waits"]:
                    if k not in self.streams:
                        self.ops[k][i]["sig"] = True
        self.sigcount = {}
        for e, lst in self.ops.items():
            c = 0
            arr = []
            for rec in lst:
                if rec["sig"]:
                    c += 1
                arr.append(c)
            self.sigcount[e] = arr


def build_program():
    nc = bass.Bass("TRN2", target_bir_lowering=False)
    S = Sched()

    def din(name, shape, dt=F32):
        return nc.dram_tensor(name, shape, dt, kind="ExternalInput").ap()

    def dout(name, shape):
        return nc.dram_tensor(name, shape, F32, kind="ExternalOutput").ap()

    xT_d = din("xT", [128, KC, TT])
    NW = DEPTH * NBLK if DBG in (5, 6, 99) else 13
    wall = din("wall", [NW, 128, WBLK])
    pv_d = din("pv", [128, NPV])
    cv_d = din("cv", [128, NCV])
    sh_d = din("sh", [DEPTH, NH, 128, NS * 128])
    ss_d = din("ss", [DEPTH, 128, 4 * NS * 2])
    sc_d = din("sc", [DEPTH, 128, 4 * NS * 30])
    yT_o = dout("yT", [128, KC, TT])
    hp_o = dout("hp", [DEPTH, 128, NH * 128])
    sp_o = dout("sp", [DEPTH, 128, 8])
    cp_o = dout("cp", [DEPTH, 128, 120])
    hs_o = dout("hs", [DEPTH, NH, 128, NS * 128])
    ss_o = dout("sso", [DEPTH, 128, 4 * NS * 2])
    sc_o = dout("sco", [DEPTH, 128, 4 * NS * 30])
    xs_d = nc.dram_tensor("xspill", [128, KC, TT], F32).ap()
    PAYW = NH * 128 + 120 + 8
    pay_in = [nc.dram_tensor(f"pay_in{l}", [128, PAYW], F32) for l in range(DEPTH)]
    pay_out = [nc.dram_tensor(f"pay_out{l}", [256, PAYW], F32) for l in range(DEPTH)]

    def sb(name, shape, dt=F32):
        return nc.alloc_sbuf_tensor(name, shape, dt).ap()

    A = sb("A", [128, KC * TT])
    xT = A.rearrange("p (c t) -> p c t", c=KC)
    hT = sb("hT", [128, KC, TT], BF16)
    mixT = sb("mixT", [128, KC, TT], BF16)
    W = [sb(f"W{i}", [128, WBLK], BF16) for i in range(2)]
    MS = sb("MS", [128, 4352])
    PVt = sb("PVt", [128, NPV])
    CVt = sb("CVt", [128, NCV])
    identb = sb("identb", [128, 128], BF16)
    onesb = sb("onesb", [128, 128], BF16)
    causb = sb("causb", [128, 64], BF16)
    Gs = sb("Gs", [128, 5 * 16])
    LBt = sb("LBt", [128, 4 * 16])
    VT = sb("VT", [128, TT], BF16)
    VTOK2 = [sb(f"VTOK{i}", [128, 8 * 128], BF16) for i in range(2)]
    EBL2 = [sb(f"EBL{i}", [128, 16]) for i in range(2)]
    SMP2 = [sb(f"SMP{i}", [128, 4 * NS]) for i in range(2)]
    EBE = sb("EBE", [128, NH])
    SSm = sb("SSm", [128, 4 * NS * 3])
    FNC = sb("FNC", [128, 16])
    KEEP = sb("KEEP", [128, 4 * 30 + 4 * 2 + 4 * 2])
    FIX = sb("FIX", [128, 512])

    def carve(base, n, dt=F32):
        v = A[:, base:base + (n if dt == F32 else n // 2)]
        return v if dt == F32 else v.bitcast(BF16)
    _a = 0

    def take(n_f32):
        nonlocal _a
        r = (_a, n_f32)
        _a += n_f32
        return r
    r_T0, r_QS, r_FG, r_KK = take(TT), take(TT), take(TT), take(TT)
    r_BD, r_EE = take(T), take(T)
    r_QT, r_KT, r_KTOK = take(T // 2), take(T // 2), take(T // 2)
    r_QT2, r_KT2, r_KTOK2 = take(T // 2), take(T // 2), take(T // 2)
    r_OL = take(NH * T // 2)
    r_SST = take(NS * 128)
    assert _a <= KC * TT, _a
    T0 = A[:, r_T0[0]:r_T0[0] + TT]
    QS = A[:, r_QS[0]:r_QS[0] + TT]
    FG = A[:, r_FG[0]:r_FG[0] + TT]
    KK = A[:, r_KK[0]:r_KK[0] + TT]
    BD = A[:, r_BD[0]:r_BD[0] + T]
    EE = A[:, r_EE[0]:r_EE[0] + T]
    QTt2 = [A[:, r[0]:r[0] + T // 2].bitcast(BF16) for r in (r_QT, r_QT2)]
    KTt2 = [A[:, r[0]:r[0] + T // 2].bitcast(BF16) for r in (r_KT, r_KT2)]
    KTOK2 = [A[:, r[0]:r[0] + T // 2].bitcast(BF16) for r in (r_KTOK, r_KTOK2)]
    OL = A[:, r_OL[0]:r_OL[0] + NH * T // 2].bitcast(BF16).rearrange("p (h t) -> p h t", h=NH)
    Sst = A[:, r_SST[0]:r_SST[0] + NS * 128]
    APAD = A[:, 0:4 * 1054].rearrange("p (i t) -> p i t", i=4)
    CO = A[:, 4216:4216 + 4 * TT].rearrange("p (i t) -> p i t", i=4)
    CT1 = A[:, 8376:8376 + TT]
    CT2 = A[:, 9416:9416 + TT]
    CT3 = A[:, 10456:10456 + TT]
    UPAD = A[:, 11496:11496 + 1026]
    CMU = A[:, 11496:11496 + TT]
    CRS = A[:, 12536:12536 + TT]
    SMX = A[:, 12536:12536 + 512]
    CXB = A[:, 13576:13576 + TT // 2].bitcast(BF16)
    CXQ = A[:, 14096:14096 + TT // 2].bitcast(BF16)
    SCm = A[:, 14616:14616 + 4 * NS * 31]
    PAY = A[:, 0:PAYW]
    PIN = A[:, PAYW:2 * PAYW]
    SINB = A[:, 2 * PAYW:2 * PAYW + 64].bitcast(BF16)
    TSQ2 = [A[:, o:o + TT // 2].bitcast(BF16) for o in (2400, 3960)]
    TRS2 = [A[:, o:o + TT] for o in (2920, 4480)]
    SINB2 = [SINB, A[:, 5520:5520 + 64].bitcast(BF16)]
    Rr = MS[:, 0:TT]
    SQ = [MS[:, 1040 + i * 520:1040 + (i + 1) * 520].bitcast(BF16) for i in range(2)]
    SOG = MS[:, 0:4160].bitcast(BF16).rearrange("p (h t) -> p h t", h=NH)
    XO = [MS[:, i * TT:(i + 1) * TT] for i in range(2)]
    HID = [MS[:, i * 2080:(i + 1) * 2080].bitcast(BF16).rearrange("p (c t) -> p c t", c=4) for i in range(2)]

    def pvs(name, a, b=None):
        o, w = PV[name]
        return PVt[:, o + a:o + (a + 1 if b is None else b)]

    PS = [nc.alloc_psum_tensor(f"ps{i}", [128, 512], F32).ap() for i in range(8)]

    def slot_regs(s):
        return [PS[3 * s][:, 0:352], PS[3 * s + 1][:, 0:352], PS[3 * s + 2][:, 0:336]]

    def pieces(regs):
        return [(regs[0], 0, 352), (regs[1], 352, 704), (regs[2][:, 0:320], 704, 1024), (regs[2][:, 320:336], 1024, 1040)]
    B6, B7 = PS[6], PS[7]
    PTR = PS[7].bitcast(BF16)

    MSTATE = {"owner": None}

    def ms_fence(owner):
        if MSTATE["owner"] != owner:
            S.op("dve", lambda e: e.memset(FNC[:, 0:1], 0.0), reads=[], writes=["MS", "SMXf"])
            MSTATE["owner"] = owner

    wctr = {"n": 0}

    def load_w(l, b):
        i = wctr["n"] % 2
        wctr["n"] += 1
        src = wall[l * NBLK + b]
        S.op("pool", lambda e, i=i, src=src: e.dma_start(out=W[i], in_=src), reads=[], writes=[f"W{i}"], dma=f"w{i}")
        return i

    def mm_chunk(slot, lhs_fn, rhs_fn, nk, reads, extra_w=(), hook=None):
        regs = slot_regs(slot)

        def mk(k0, k1):
            def fn(e):
                ins = None
                for k in range(k0, k1):
                    for ti, (t0, tw) in enumerate(TILES):
                        ins = e.matmul(regs[ti][:, 0:tw], lhsT=lhs_fn(k), rhs=rhs_fn(k, t0, tw),
                                       start=(k == 0), stop=(k == nk - 1))
                return ins
            return fn
        if hook is None:
            S.op("pe", mk(0, nk), reads=reads, writes=[f"slot{slot}"] + list(extra_w))
        else:
            S.op("pe", mk(0, nk // 2), reads=reads, writes=[f"slot{slot}"] + list(extra_w))
            hook()
            S.op("pe", mk(nk // 2, nk), reads=reads, writes=[f"slot{slot}"] + list(extra_w), nowait_self=True)
            hook()

    def ew3(eng, slot, fn3, reads, writes):
        regs = slot_regs(slot)

        def fn(e):
            ins = None
            for ti, (t0, tw) in enumerate(TILES):
                ins = fn3(e, regs[ti][:, 0:tw], t0, tw)
            return ins
        S.op(eng, fn, reads=[f"slot{slot}"] + list(reads), writes=list(writes) + [f"slot{slot}"])

    slotctr = {"n": 0}

    def next_slot():
        s = slotctr["n"] % 2
        slotctr["n"] += 1
        return s

    S.op("sp", lambda e: e.dma_start(out=xT, in_=xT_d), writes=["xT"], dma="ld_x")
    S.op("sp", lambda e: e.dma_start(out=PVt, in_=pv_d), writes=["PV"], dma="ld_p")
    S.op("sp", lambda e: e.dma_start(out=CVt, in_=cv_d), writes=["CV"], dma="ld_c")
    S.op("dve", lambda e: e.memset(onesb, 1.0), writes=["ones"])
    o_id, o_ca, o_sm = CV["ident"][0], CV["causal"][0], CV["smask"][0]
    identf = CVt[:, o_id:o_id + 128]
    smask = CVt[:, o_sm:o_sm + 1024]
    S.op("dve", lambda e: e.tensor_copy(out=identb, in_=identf), reads=["CV"], writes=["identb"])
    S.op("dve", lambda e: e.tensor_copy(out=causb, in_=CVt[:, o_ca:o_ca + 64]), reads=["CV"], writes=["causb"])
    S.op("dve", lambda e: e.tensor_scalar(out=Gs, in0=pvs("g", 0, 80), scalar1=float(np.sqrt(D)), scalar2=None,
                                          op0=ALU.mult), reads=["PV"], writes=["Gs"])
    lb0, lb1 = pvs("lb", 0, 8), pvs("lb", 8, 16)
    sc0, sc1, sc2 = LBt[:, 48:56], LBt[:, 56:64], LBt[:, 0:8]

    def lbsetup():
        S.op("dve", lambda e: e.tensor_tensor(out=sc2, in0=lb0, in1=lb1, op=ALU.max), reads=["PV"], writes=["LB"])
        S.op("dve", lambda e: e.tensor_tensor(out=sc0, in0=lb0, in1=sc2, op=ALU.subtract), reads=["LB", "PV"], writes=["LB"])
        S.op("dve", lambda e: e.tensor_tensor(out=sc1, in0=lb1, in1=sc2, op=ALU.subtract), reads=["LB", "PV"], writes=["LB"])
        S.op("act", lambda e: e.activation(out=sc0, in_=sc0, func=AF.Exp), reads=["LB"], writes=["LB"])
        S.op("act", lambda e: e.activation(out=sc1, in_=sc1, func=AF.Exp), reads=["LB"], writes=["LB"])
        S.op("dve", lambda e: e.tensor_tensor(out=sc2, in0=sc0, in1=sc1, op=ALU.add), reads=["LB"], writes=["LB"])
        S.op("dve", lambda e: e.reciprocal(out=sc2, in_=sc2), reads=["LB"], writes=["LB"])
        S.op("dve", lambda e: e.tensor_tensor(out=sc0, in0=sc0, in1=sc2, op=ALU.mult), reads=["LB"], writes=["LB"])
        S.op("dve", lambda e: e.tensor_tensor(out=sc1, in0=sc1, in1=sc2, op=ALU.mult), reads=["LB"], writes=["LB"])
        S.op("dve", lambda e: e.tensor_tensor(out=sc2, in0=sc0, in1=sc1, op=ALU.add), reads=["LB"], writes=["LB"])
        S.op("dve", lambda e: e.tensor_tensor(out=LBt[:, 8:16], in0=sc2, in1=sc0, op=ALU.subtract), reads=["LB"], writes=["LB"])
        S.op("dve", lambda e: e.tensor_tensor(out=LBt[:, 0:8], in0=sc0, in1=sc0, op=ALU.subtract), reads=["LB"], writes=["LB"])
        S.op("dve", lambda e: e.tensor_scalar(out=LBt[:, 16:32], in0=LBt[:, 0:16], scalar1=-1.0, scalar2=1.0,
                                              op0=ALU.mult, op1=ALU.add), reads=["LB"], writes=["LB"])
        S.op("dve", lambda e: e.tensor_scalar(out=LBt[:, 32:48], in0=LBt[:, 0:16], scalar1=-1.0, scalar2=None,
                                              op0=ALU.add), reads=["LB"], writes=["LB"])
    lbsetup()

    def chain(eng, fns, reads, writes):
        for f in fns:
            S.op(eng, f, reads=reads, writes=writes)

    def act_sigmoid_from_exp(buf, res, extra_reads=()):
        chain("act", [lambda e: e.activation(out=buf, in_=buf, func=AF.Ln, bias=1.0),
                      lambda e: e.activation(out=buf, in_=buf, func=AF.Exp, scale=-1.0)], [res] + list(extra_reads), [res])

    def act_rsqrt_inplace(buf, res, addc, extra_reads=()):
        chain("act", [lambda e: e.activation(out=buf, in_=buf, func=AF.Ln, bias=float(addc)),
                      lambda e: e.activation(out=buf, in_=buf, func=AF.Exp, scale=-0.5)], [res] + list(extra_reads), [res])

    def rmsnorm(gidx, dst_fn, dst_res):
        ms_fence("norm")
        slot = next_slot()
        regs = slot_regs(slot)
        for dc in range(KC):
            q = SQ[dc % 2]
            S.op("act", lambda e, dc=dc, q=q: e.activation(out=q, in_=xT[:, dc, :], func=AF.Square),
                 reads=["xT", "MS"], writes=[f"SQ{dc % 2}"])

            def fn(e, dc=dc, q=q):
                ins = None
                for ti, (t0, tw) in enumerate(TILES):
                    ins = e.matmul(regs[ti][:, 0:tw], lhsT=onesb, rhs=q[:, t0:t0 + tw], start=(dc == 0), stop=(dc == KC - 1))
                return ins
            S.op("pe", fn, reads=[f"SQ{dc % 2}", "ones"], writes=[f"slot{slot}"])
        ew3("act", slot, lambda e, ps, t0, tw: e.activation(out=Rr[:, t0:t0 + tw], in_=ps, func=AF.Ln, bias=float(D * EPS)),
            reads=["MS"], writes=["Rr"])
        S.op("act", lambda e: e.activation(out=Rr, in_=Rr, func=AF.Exp, scale=-0.5), reads=["Rr"], writes=["Rr"])
        for dc in range(KC):
            S.op("dve", lambda e, dc=dc: e.scalar_tensor_tensor(out=dst_fn(dc), in0=xT[:, dc, :],
                                                                scalar=Gs[:, gidx * 16 + dc:gidx * 16 + dc + 1], in1=Rr,
                                                                op0=ALU.mult, op1=ALU.mult),
                 reads=["xT", "Rr", "Gs"], writes=[dst_res])

    SLE = sb("SLE", [128, NH * 128])
    SL = sb("SL", [128, 128])
    SDB = [sb(f"SDB{i}", [128, 128], BF16) for i in range(2)]
    KVT = sb("KVT", [32, 256])
    AN = sb("AN", [32, 4 * 128])
    ATS = sb("ATS", [128, 128], BF16)
    BLt = sb("BLt", [128, 16])
    PFX = sb("PFX", [128, 16])
    PFE = sb("PFE", [128, 16])
    ONE16 = sb("ONE16", [128, 16])
    OTS = sb("OTS", [128, NH * NS])
    PST = sb("PST", [128, 128])
    HGs = sb("HGs", [128, 16])
    S.op("dve", lambda e: e.memset(ONE16, 1.0), writes=["ONE16"])
    S.op("dve", lambda e: e.tensor_scalar(out=HGs, in0=pvs("hg", 0, 16), scalar1=float(np.sqrt(128.0)), scalar2=None,
                                          op0=ALU.mult), reads=["PV"], writes=["HGs"])

    seq = []
    for l in range(DEPTH):
        order = list(range(13)) + list(range(13, 17))
        ml = [17, 18]
        for g in range(16):
            ml.append(33 + g)
            if g + 2 < 16:
                ml.append(17 + g + 2)
        order += ml
        assert len(order) == NBLK and len(set(order)) == NBLK
        seq += [(l, b) for b in order]
    wp = {"p": 0}

    def get_w(l, b):
        p = wp["p"]
        assert seq[p] == (l, b), (seq[p], l, b)
        if p == 0:
            load_w(*seq[0])
        if p + 1 < len(seq) and seq[p + 1][0] * NBLK + seq[p + 1][1] < NW:
            load_w(*seq[p + 1])
        wp["p"] += 1
        return p % 2

    def fenceA():
        S.op("dve", lambda e: e.memset(FNC[:, 2:3], 0.0), reads=[], writes=["Afree", "SMXf3"])

    def ln_silu(l, n0, n1, src_fn, dst_fn, tag, use_slots):
        pass

    def conformer_ln(l, AF_R):
        sa, sb_ = next_slot(), next_slot()
        ra, rb = slot_regs(sa), slot_regs(sb_)
        for i in range(4):
            S.op("act", lambda e, i=i: e.activation(out=CXB, in_=CO[:, i, :], func=AF.Copy),
                 reads=[f"CO{i}", f"CO{i}s"] + AF_R, writes=["CXB"])
            S.op("act", lambda e, i=i: e.activation(out=CXQ, in_=CO[:, i, :], func=AF.Square),
                 reads=[f"CO{i}", f"CO{i}s"] + AF_R, writes=["CXQ"])

            def fn(e, i=i):
                ins = None
                for ti, (t0, tw) in enumerate(TILES):
                    e.matmul(ra[ti][:, 0:tw], lhsT=onesb, rhs=CXB[:, t0:t0 + tw], start=(i == 0), stop=(i == 3))
                    ins = e.matmul(rb[ti][:, 0:tw], lhsT=onesb, rhs=CXQ[:, t0:t0 + tw], start=(i == 0), stop=(i == 3))
                return ins
            S.op("pe", fn, reads=["CXB", "CXQ", "ones"], writes=[f"slot{sa}", f"slot{sb_}"])
        ew3("dve", sa, lambda e, ps, t0, tw: e.tensor_scalar(out=CMU[:, t0:t0 + tw], in0=ps, scalar1=1.0 / 512, scalar2=None, op0=ALU.mult),
            reads=AF_R, writes=["CMU", "UPAD", "UPADr", f"slot{sa}"])
        ew3("dve", sb_, lambda e, ps, t0, tw: e.tensor_scalar(out=CRS[:, t0:t0 + tw], in0=ps, scalar1=1.0 / 512, scalar2=None, op0=ALU.mult),
            reads=AF_R, writes=["CRS", "SMX", f"slot{sb_}"])

        chain("dve", [lambda e: e.tensor_tensor(out=CT1, in0=CMU, in1=CMU, op=ALU.mult),
                      lambda e: e.tensor_tensor(out=CRS, in0=CRS, in1=CT1, op=ALU.subtract)],
              ["CMU", "CRS"] + AF_R, ["CRS", "CT1"])
        act_rsqrt_inplace(CRS, "CRS", EPS)
        for i in range(4):
            o_g = PV["clg"][0] + l * 4 + i
            o_b = PV["clb"][0] + l * 4 + i

            chain("dve", [lambda e, i=i: e.tensor_tensor(out=CT2, in0=CO[:, i, :], in1=CMU, op=ALU.subtract),
                          lambda e: e.tensor_tensor(out=CT2, in0=CT2, in1=CRS, op=ALU.mult),
                          lambda e, o_g=o_g, o_b=o_b: e.tensor_scalar(out=CT2, in0=CT2, scalar1=PVt[:, o_g:o_g + 1],
                                                                      scalar2=PVt[:, o_b:o_b + 1], op0=ALU.mult, op1=ALU.add)],
                  [f"CO{i}", f"CO{i}s", "CMU", "CRS", "PV"] + AF_R, ["CT2"])
            S.op("act", lambda e: e.activation(out=CT3, in_=CT2, func=AF.Exp, scale=-1.0), reads=["CT2"], writes=["CT3"])
            act_sigmoid_from_exp(CT3, "CT3")

            def f2(e, i=i):
                return e.tensor_tensor(out=mixT[:, 12 + i, :], in0=CT2, in1=CT3, op=ALU.mult)
            S.op("dve", f2, reads=["CT2", "CT3"], writes=["CT3", "CT2", f"mix{12 + i}"])
        S.op("dve", lambda e: e.tensor_copy(out=PST[:, 0:120].rearrange("p (i r) -> p i r", i=4), in_=APAD[:, :, 1024:1054]),
             reads=["APAD"] + AF_R, writes=["PSTa"])

    STEPS = []

    def pop_steps(n):
        for _ in range(n):
            if STEPS:
                STEPS.pop(0)()

    def head_q(l, h, sq, AF_R):
        R = AF_R
        hp = h % 2
        ew3("act", sq, lambda e, ps, t0, tw: e.activation(out=T0[:, t0:t0 + tw], in_=ps, func=AF.Exp, scale=-1.0), reads=R, writes=["T0"])
        act_sigmoid_from_exp(T0, "T0")
        ew3("dve", sq, lambda e, ps, t0, tw: e.tensor_tensor(out=QS[:, t0:t0 + tw], in0=ps, in1=T0[:, t0:t0 + tw], op=ALU.mult),
            reads=["T0"] + R, writes=["QS", f"slot{sq}"])
        S.op("dve", lambda e: e.tensor_copy(out=SMP2[hp][:, 2 * NS:3 * NS], in_=QS[:, T:TT]), reads=["QS"], writes=[f"SMPq{hp}"])
        QTt, KTt = QTt2[hp], KTt2[hp]
        S.op("dve", lambda e: e.tensor_tensor(out=mixT[:, h, 0:T], in0=QS[:, 0:T], in1=EE, op=ALU.mult), reads=["QS", "EE"], writes=[f"mix{h}"])
        S.op("act", lambda e: e.activation(out=EE, in_=BD, func=AF.Exp), reads=["BD", f"mix{h}"], writes=["EE"])
        S.op("dve", lambda e: e.scalar_tensor_tensor(out=QTt, in0=EE, scalar=1e35, in1=QS[:, 0:T], op0=ALU.min, op1=ALU.mult),
             reads=["EE", "QS"] + R, writes=[f"QT{hp}"])


    def head_f(l, h, sf, AF_R, LB, OML, NOML):
        R = AF_R
        hp = h % 2
        QTt, KTt, EBL = QTt2[hp], KTt2[hp], EBL2[hp]
        ew3("act", sf, lambda e, ps, t0, tw: e.activation(out=FG[:, t0:t0 + tw], in_=ps, func=AF.Exp, scale=-1.0), reads=R, writes=["FG", f"slot{sf}"])
        act_sigmoid_from_exp(FG, "FG")
        chain("dve", [lambda e: e.tensor_scalar(out=KK, in0=FG, scalar1=NOML[:, h:h + 1], scalar2=OML[:, h:h + 1], op0=ALU.mult, op1=ALU.add),
                      lambda e: e.tensor_scalar(out=FG, in0=FG, scalar1=OML[:, h:h + 1], scalar2=LB[:, h:h + 1], op0=ALU.mult, op1=ALU.add),
                      lambda e: e.tensor_scalar(out=FG, in0=FG, scalar1=1e-30, scalar2=None, op0=ALU.max)],
              ["FG", "LB"] + R, ["FG", "KK"])

        def fcp(e):
            e.tensor_copy(out=SMP2[hp][:, 0:NS], in_=FG[:, T:TT])
            return e.tensor_copy(out=SMP2[hp][:, 3 * NS:4 * NS], in_=KK[:, T:TT])
        S.op("dve", fcp, reads=["FG", "KK"], writes=[f"SMPf{hp}"])
        S.op("act", lambda e: e.activation(out=FG[:, 0:T], in_=FG[:, 0:T], func=AF.Ln), reads=["FG", f"SMPf{hp}"], writes=["FG"])
        S.op("dve", lambda e: e.tensor_tensor_scan(out=BD, data0=smask, data1=FG[:, 0:T], initial=0.0, op0=ALU.mult, op1=ALU.add),
             reads=["FG", "CV"] + R, writes=["BD"])
        BD3 = BD.rearrange("p (c t) -> p c t", t=64)
        EE3 = EE.rearrange("p (c t) -> p c t", t=64)
        chain("dve", [lambda e: e.tensor_copy(out=BLt, in_=BD3[:, :, 63]),
                      lambda e: e.tensor_tensor_scan(out=PFX, data0=ONE16, data1=BLt, initial=0.0, op0=ALU.mult, op1=ALU.add),
                      lambda e: e.tensor_tensor(out=PFE, in0=PFX, in1=BLt, op=ALU.subtract),
                      lambda e: e.tensor_tensor(out=EE3, in0=BD3, in1=PFE.unsqueeze(2).to_broadcast([128, 16, 64]), op=ALU.add),
                      lambda e: e.tensor_tensor(out=BD3, in0=BD3, in1=BLt.unsqueeze(2).to_broadcast([128, 16, 64]), op=ALU.subtract)],
              ["BD", "ONE16"] + R, ["BD", "EE", "BLt", "PFX"])

        def fe1(e):
            e.activation(out=EBL, in_=BLt, func=AF.Exp)
            e.activation(out=EBE[:, h:h + 1], in_=PFX[:, 15:16], func=AF.Exp)
            return e.activation(out=EE, in_=EE, func=AF.Exp)
        S.op("act", fe1, reads=["EE", "BLt", "PFX"], writes=["EE", f"EBL{hp}", f"EBE{h}"])
        S.op("act", lambda e: e.activation(out=FG[:, 0:T], in_=BD, func=AF.Exp, scale=-1.0), reads=["BD", "FG"], writes=["FG"])
        S.op("dve", lambda e: e.tensor_tensor(out=KTt, in0=FG[:, 0:T], in1=KK[:, 0:T], op=ALU.mult), reads=["FG", "KK"] + R, writes=[f"KT{hp}"])

    def head_v(l, h, sv, AF_R):
        R = AF_R
        hp = h % 2
        vr = slot_regs(sv)

        def fvv(e):
            ins = None
            for (ps, a0, a1) in pieces(vr):
                if a0 < T:
                    ins = e.activation(out=VT[:, a0:a1], in_=ps, func=AF.Copy)
                else:
                    ins = e.activation(out=SMP2[hp][:, NS:2 * NS], in_=ps, func=AF.Copy)
            return ins
        S.op("act", fvv, reads=[f"slot{sv}"], writes=["VT", f"SMPv{hp}", f"slot{sv}"])

    def head_og(l, h, so, AF_R):
        R = AF_R
        hp = h % 2
        QTt, KTt, KTOK, VTOK, EBL, SMP = QTt2[hp], KTt2[hp], KTOK2[hp], VTOK2[hp], EBL2[hp], SMP2[hp]
        Fs, VSs, QSs, KKs = SMP[:, 0:NS], SMP[:, NS:2 * NS], SMP[:, 2 * NS:3 * NS], SMP[:, 3 * NS:4 * NS]
        rQT, rKT, rKTOK, rVTOK, rEBL = f"QT{hp}", f"KT{hp}", f"KTOK{hp}", f"VTOK{hp}", f"EBL{hp}"
        rSMP = [f"SMPq{hp}", f"SMPf{hp}", f"SMPv{hp}"]
        ew3("act", so, lambda e, ps, t0, tw: e.activation(out=T0[:, t0:t0 + tw], in_=ps, func=AF.Exp, scale=-1.0), reads=["QS"] + R, writes=["T0"])
        act_sigmoid_from_exp(T0, "T0")
        ew3("dve", so, lambda e, ps, t0, tw: e.tensor_tensor(out=SOG[:, h, t0:t0 + tw], in0=ps, in1=T0[:, t0:t0 + tw], op=ALU.mult),
            reads=["T0", "MS"], writes=[f"SOG{h}", f"slot{so}"])
        def tr_step(which):
            src, dst, sr, dr = ((KTt, KTOK, rKT, rKTOK), (VT, VTOK, "VT", rVTOK))[which]

            def ft(e):
                ins = None
                for j in range(8):
                    ins = e.transpose(out=PTR[:, j * 128:(j + 1) * 128], in_=src[:, j * 128:(j + 1) * 128], identity=identb)
                return ins
            S.op("pe", ft, reads=[sr, "identb"], writes=["B7"])
            S.op("act", lambda e: e.activation(out=dst, in_=PTR, func=AF.Copy), reads=R, writes=[dr, "B7"])
        tr_step(1)
        STEPS.append(lambda: tr_step(0))

        def o_mm(g):
            for p in range(2):
                def fo(e, g=g, p=p):
                    c = 2 * g + p
                    oreg = B6[:, (c % 8) * 64:(c % 8) * 64 + 64]
                    ins = e.matmul(oreg, lhsT=VTOK[64 * p:64 * p + 64, g * 128:(g + 1) * 128], rhs=ATS[64 * p:64 * p + 64, 64 * p:64 * p + 64],
                                   start=True, stop=(c == 0))
                    if c > 0:
                        ins = e.matmul(oreg, lhsT=SDB[p], rhs=QTt[:, 64 * c:64 * c + 64], start=False, stop=True)
                    return ins
                S.op("pe", fo, reads=[rVTOK, "ATS", "SDB0", "SDB1", rQT], writes=["B6"])
            if g % 4 == 3:
                half = g // 4
                S.op("act", lambda e, half=half: e.activation(out=OL[:, h, 512 * half:512 * half + 512], in_=B6, func=AF.Copy),
                     reads=R, writes=[f"OL{h}", "B6"])

        def scan_step(g):
            if g > 0:
                o_mm(g - 1)
            for p in range(2):
                def fAU(e, g=g, p=p):
                    c = 2 * g + p
                    e.matmul(B7[:, 64 * p:64 * p + 64], lhsT=KTt[:, g * 128:(g + 1) * 128], rhs=QTt[:, 64 * c:64 * c + 64], start=True, stop=True)
                    return e.matmul(B7[:, 128 + 128 * p:256 + 128 * p], lhsT=KTOK[64 * p:64 * p + 64, g * 128:(g + 1) * 128],
                                    rhs=VTOK[64 * p:64 * p + 64, g * 128:(g + 1) * 128], start=True, stop=True)
                S.op("pe", fAU, reads=[rKT, rQT, rKTOK, rVTOK], writes=["B7"])
            S.op("dve", lambda e: e.tensor_tensor(out=ATS.rearrange("p (a t) -> p a t", a=2), in0=B7[:, 0:128].rearrange("p (a t) -> p a t", a=2),
                                                  in1=causb.unsqueeze(1).to_broadcast([128, 2, 64]), op=ALU.mult),
                 reads=["causb"], writes=["ATS", "B7"])
            for p in range(2):
                c = 2 * g + p
                if c > 0:
                    S.op("dve", lambda e, c=c, p=p: e.tensor_scalar(out=SDB[p], in0=SL, scalar1=EBL[:, c:c + 1], scalar2=None, op0=ALU.mult),
                         reads=["SL", rEBL], writes=[f"SDB{p}"])
                    S.op("dve", lambda e, c=c, p=p: e.scalar_tensor_tensor(out=SL, in0=SL, scalar=EBL[:, c:c + 1], in1=B7[:, 128 + 128 * p:256 + 128 * p],
                                                                            op0=ALU.mult, op1=ALU.add),
                         reads=["SL", rEBL], writes=["SL", "B7"])
                else:
                    S.op("dve", lambda e, p=p: e.tensor_copy(out=SL, in_=B7[:, 128 + 128 * p:256 + 128 * p]), reads=[], writes=["SL", "B7"])

        def scan_end():
            o_mm(7)
            S.op("dve", lambda e: e.tensor_copy(out=SLE[:, h * 128:(h + 1) * 128], in_=SL), reads=["SL"], writes=[f"SLE{h}"])
            S.op("sp", lambda e: e.dma_start(out=Sst, in_=sh_d[l, h]), reads=R, writes=["Sst"], dma="ld_sh")

            def ftr(e):
                e.matmul(B7[0:NS, 0:128], lhsT=KKs, rhs=identf, start=True, stop=True)
                return e.matmul(B7[0:NS, 128:256], lhsT=VSs, rhs=identf, start=True, stop=True)
            S.op("pe", ftr, reads=rSMP + ["CV"], writes=["B7"])
            S.op("dve", lambda e: e.tensor_copy(out=KVT[0:NS, :], in_=B7[0:NS, 0:256]), reads=[], writes=["KVT", "B7"])

        def samp_step(q4):
            bank, bres = (B6, "B6") if q4 % 2 == 0 else (B7, "B7")
            S.op("dve", lambda e: e.tensor_tensor(out=AN[0:NS, :].rearrange("p (n k) -> p n k", n=4),
                                                  in0=KVT[0:NS, 0:128].unsqueeze(1).to_broadcast([NS, 4, 128]),
                                                  in1=identf[0:NS, 4 * q4:4 * q4 + 4].unsqueeze(2).to_broadcast([NS, 4, 128]), op=ALU.mult),
                 reads=["KVT", "CV"], writes=["AN"])

            def fkv(e):
                ins = None
                for jn in range(4):
                    ins = e.matmul(bank[:, 128 * jn:128 * jn + 128], lhsT=AN[0:NS, jn * 128:(jn + 1) * 128], rhs=KVT[0:NS, 128:256],
                                   start=True, stop=True)
                return ins
            S.op("pe", fkv, reads=["AN", "KVT"], writes=[bres])

            def fsu(e):
                ins = None
                for jn in range(4):
                    n = 4 * q4 + jn
                    ins = e.scalar_tensor_tensor(out=Sst[:, n * 128:(n + 1) * 128], in0=Sst[:, n * 128:(n + 1) * 128], scalar=Fs[:, n:n + 1],
                                                 in1=bank[:, 128 * jn:128 * jn + 128], op0=ALU.mult, op1=ALU.add)
                return ins
            S.op("dve", fsu, reads=rSMP, writes=["Sst", bres])

        def samp_end():
            def fos(e):
                ins = None
                for n in range(NS):
                    ins = e.matmul(B6[:, n:n + 1], lhsT=Sst[:, n * 128:(n + 1) * 128], rhs=QSs[:, n:n + 1], start=True, stop=True)
                return ins
            S.op("pe", fos, reads=["Sst"] + rSMP, writes=["B6"])
            S.op("dve", lambda e: e.tensor_copy(out=OTS[:, h * NS:(h + 1) * NS], in_=B6[:, 0:NS]), reads=[], writes=[f"OTS{h}", "B6"])
            store("sp", hs_o[l, h], Sst, ["Sst"] + R, "st_hs")
        for g in range(8):
            STEPS.append(lambda g=g: scan_step(g))
        STEPS.append(scan_end)
        for q4 in range(4):
            STEPS.append(lambda q4=q4: samp_step(q4))
        STEPS.append(samp_end)

    def tail_and_rest(l, AF_R, xold_src):
        R = AF_R
        store("sp", ss_o[l].rearrange("p (i q) -> p i q", i=4), SSm.rearrange("p (i q) -> p i q", i=4)[:, :, NS:3 * NS], ["SSm"], "st_ss")
        S.op("dve", lambda e: e.tensor_copy(out=PST[:, 120:128], in_=FIX[:, 0:8]), reads=[f"FIXu{i}" for i in range(4)], writes=["PSTu"])
        fenceA()
        R2 = ["Afree"]

        def fpay(e):
            e.tensor_copy(out=PAY[:, 0:1024], in_=SLE)
            return e.tensor_copy(out=PAY[:, 1024:1152], in_=PST)
        S.op("dve", fpay, reads=[f"SLE{h}" for h in range(NH)] + ["PSTa", "PSTu"] + R2, writes=["PAY"])
        S.op("sp", lambda e: e.dma_start(out=pay_in[l].ap(), in_=PAY), reads=["PAY"] + R2, writes=["payin"], dma=f"st_pay{l}")
        S.op("pool", lambda e: e.collective_compute("AllGather", ALU.bypass, replica_groups=[[0, 1], [2, 3], [4, 5], [6, 7]],
                                                    ins=[pay_in[l].ap().opt()], outs=[pay_out[l].ap().opt()]),
             reads=["payin"], writes=["payout"], dma=f"cc{l}")
        S.op("sp", lambda e: e.dma_start(out=PIN, in_=pay_out[l].ap()[0:128, :]), reads=["payout"] + R2, writes=["PIN"], dma=f"ld_pin{l}")
        o_fl = PV["flag"][0]
        S.op("dve", lambda e: e.tensor_scalar(out=PIN, in0=PIN, scalar1=PVt[:, o_fl:o_fl + 1], scalar2=None, op0=ALU.mult),
             reads=["PIN", "PV"], writes=["PIN"])
        store("sp", sp_o[l], PST[:, 120:128], ["PSTu"], "st_sp")
        store("sp", cp_o[l], PST[:, 0:120], ["PSTa"], "st_cp")
        slots_h = {}

        def stA(h):
            hp = h % 2
            S.op("dve", lambda e: e.tensor_copy(out=SINB2[hp], in_=PIN[:, h * 128:(h + 1) * 128]), reads=["PIN"], writes=[f"SINB{hp}"])
            for half in range(2):
                S.op("pe", lambda e, half=half: e.matmul(B6, lhsT=SINB2[hp], rhs=mixT[:, h, 512 * half:512 * half + 512], start=True, stop=True),
                     reads=[f"SINB{hp}", f"mix{h}"], writes=["B6"])
                S.op("dve", lambda e, half=half: e.tensor_tensor(out=OL[:, h, 512 * half:512 * half + 512], in0=B6,
                                                                in1=OL[:, h, 512 * half:512 * half + 512], op=ALU.add),
                     reads=[f"OL{h}"], writes=[f"OL{h}", "B6"])
            S.op("dve", lambda e: e.scalar_tensor_tensor(out=SLE[:, h * 128:(h + 1) * 128], in0=PIN[:, h * 128:(h + 1) * 128],
                                                         scalar=EBE[:, h:h + 1], in1=SLE[:, h * 128:(h + 1) * 128], op0=ALU.mult, op1=ALU.add),
                 reads=["PIN", f"EBE{h}", "PAY"], writes=[f"SLE{h}"])

            def fsq(e):
                e.activation(out=TSQ2[hp][:, 0:T], in_=OL[:, h, :], func=AF.Square)
                return e.activation(out=TSQ2[hp][:, T:TT], in_=OTS[:, h * NS:(h + 1) * NS], func=AF.Square)
            S.op("act", fsq, reads=[f"OL{h}", f"OTS{h}"] + R2, writes=[f"TSQ{hp}"])
            slot = next_slot()
            slots_h[h] = slot
            mm_chunk(slot, lambda k: onesb, lambda k, t0, tw: TSQ2[hp][:, t0:t0 + tw], 1, reads=[f"TSQ{hp}", "ones"])

        def stB(h):
            hp = h % 2
            slot = slots_h[h]
            ew3("act", slot, lambda e, ps, t0, tw: e.activation(out=TRS2[hp][:, t0:t0 + tw], in_=ps, func=AF.Ln, bias=float(128 * EPS)),
                reads=R2, writes=[f"TRS{hp}", f"slot{slot}"])
            S.op("act", lambda e: e.activation(out=TRS2[hp], in_=TRS2[hp], func=AF.Exp, scale=-0.5), reads=[f"TRS{hp}"], writes=[f"TRS{hp}"])

        def stC(h):
            hp = h % 2

            def fn_a(e):
                e.tensor_tensor(out=OL[:, h, :], in0=OL[:, h, :], in1=TRS2[hp][:, 0:T], op=ALU.mult)
                return e.tensor_tensor(out=OTS[:, h * NS:(h + 1) * NS], in0=OTS[:, h * NS:(h + 1) * NS], in1=TRS2[hp][:, T:TT], op=ALU.mult)

            def fn_b(e):
                e.scalar_tensor_tensor(out=mixT[:, h, 0:T], in0=OL[:, h, :], scalar=HGs[:, l * 8 + h:l * 8 + h + 1], in1=SOG[:, h, 0:T],
                                       op0=ALU.mult, op1=ALU.mult)
                return e.scalar_tensor_tensor(out=mixT[:, h, T:TT], in0=OTS[:, h * NS:(h + 1) * NS], scalar=HGs[:, l * 8 + h:l * 8 + h + 1],
                                              in1=SOG[:, h, T:TT], op0=ALU.mult, op1=ALU.mult)
            chain("dve", [fn_a, fn_b], [f"TRS{hp}", f"OL{h}", f"OTS{h}", f"SOG{h}", "HGs"], [f"OL{h}", f"OTS{h}", f"mix{h}"])
        for i in range(NH + 2):
            if i < NH:
                stA(i)
            if 0 <= i - 1 < NH:
                stB(i - 1)
            if 0 <= i - 2 < NH:
                stC(i - 2)
        store("sp", hp_o[l], SLE, [f"SLE{h}" for h in range(NH)], "st_hp")
        HB = FIX[:, 0:60]
        C30 = FIX[:, 64:64 + 120].rearrange("p (i t) -> p i t", i=4)
        XB30 = FIX[:, 192:192 + 60].bitcast(BF16)
        XQ30 = FIX[:, 256:256 + 60].bitcast(BF16)
        M30, R30, Y30, E30 = FIX[:, 320:350], FIX[:, 352:382], FIX[:, 384:414], FIX[:, 416:446]
        for i in range(4):
            o_w = PV["ccw"][0] + (l * 4 + i) * 31

            def fx0(e, i=i):
                e.memset(HB[:, 30:60], 0.0)
                e.tensor_copy(out=HB[:, 0:30], in_=PIN[:, 1024 + 30 * i:1024 + 30 * i + 30])
                return e.tensor_copy(out=C30[:, i, :], in_=KEEP[:, 30 * i:30 * i + 30])
            fl_ = [fx0] + [lambda e, i=i, j=j, o_w=o_w: e.scalar_tensor_tensor(out=C30[:, i, :], in0=HB[:, j:j + 30], scalar=PVt[:, o_w + j:o_w + j + 1],
                                                                             in1=C30[:, i, :], op0=ALU.mult, op1=ALU.add) for j in range(30)]
            chain("dve", fl_, ["PIN", f"KEEPc{i}", "PV", "PSTu"], ["HB", f"C30_{i}"])
        S.op("act", lambda e: e.activation(out=XB30, in_=FIX[:, 64:184], func=AF.Copy), reads=[f"C30_{i}" for i in range(4)], writes=["XB30"])
        S.op("act", lambda e: e.activation(out=XQ30, in_=FIX[:, 64:184], func=AF.Square), reads=[f"C30_{i}" for i in range(4)], writes=["XQ30"])

        def fl(e):
            ins = None
            for i in range(4):
                e.matmul(PS[6][:, 64:94], lhsT=onesb, rhs=XB30[:, 30 * i:30 * i + 30], start=(i == 0), stop=(i == 3))
            for i in range(4):
                ins = e.matmul(PS[6][:, 128:158], lhsT=onesb, rhs=XQ30[:, 30 * i:30 * i + 30], start=(i == 0), stop=(i == 3))
            return ins
        S.op("pe", fl, reads=["XB30", "XQ30", "ones"], writes=["B6"])

        def fl2a(e):
            e.tensor_scalar(out=M30, in0=PS[6][:, 64:94], scalar1=1.0 / 512, scalar2=None, op0=ALU.mult)
            return e.tensor_scalar(out=R30, in0=PS[6][:, 128:158], scalar1=1.0 / 512, scalar2=None, op0=ALU.mult)
        chain("dve", [fl2a, lambda e: e.tensor_tensor(out=Y30, in0=M30, in1=M30, op=ALU.mult),
                      lambda e: e.tensor_tensor(out=R30, in0=R30, in1=Y30, op=ALU.subtract)],
              ["Y30"], ["M30", "B6", "Y30"])
        act_rsqrt_inplace(R30, "M30", EPS)
        for i in range(4):
            o_g = PV["clg"][0] + l * 4 + i
            o_b = PV["clb"][0] + l * 4 + i

            chain("dve", [lambda e, i=i: e.tensor_tensor(out=Y30, in0=C30[:, i, :], in1=M30, op=ALU.subtract),
                          lambda e: e.tensor_tensor(out=Y30, in0=Y30, in1=R30, op=ALU.mult),
                          lambda e, o_g=o_g, o_b=o_b: e.tensor_scalar(out=Y30, in0=Y30, scalar1=PVt[:, o_g:o_g + 1], scalar2=PVt[:, o_b:o_b + 1],
                                                                      op0=ALU.mult, op1=ALU.add)],
                  ["M30", f"C30_{i}", "PV"], ["Y30"])
            S.op("act", lambda e: e.activation(out=E30, in_=Y30, func=AF.Exp, scale=-1.0), reads=["Y30"], writes=["E30"])
            act_sigmoid_from_exp(E30, "E30")

            def f2(e, i=i):
                return e.tensor_tensor(out=mixT[:, 12 + i, 0:30], in0=Y30, in1=E30, op=ALU.mult)
            S.op("dve", f2, reads=["Y30", "E30"], writes=["E30", "Y30", f"mix{12 + i}"])
            o_w = PV["scw"][0] + (l * 4 + i) * 3
            h0 = PIN[:, 1144 + 2 * i:1144 + 2 * i + 1]
            h1 = PIN[:, 1144 + 2 * i + 1:1144 + 2 * i + 2]
            k0 = KEEP[:, 120 + 2 * i:121 + 2 * i]
            k1 = KEEP[:, 121 + 2 * i:122 + 2 * i]
            sc2 = FIX[:, 448 + 2 * i:450 + 2 * i]

            w0, w1 = PVt[:, o_w:o_w + 1], PVt[:, o_w + 1:o_w + 2]

            def f3a(e, h0=h0, h1=h1, k0=k0, k1=k1, sc2=sc2, w0=w0):
                e.scalar_tensor_tensor(out=sc2[:, 0:1], in0=h0, scalar=w0, in1=k0, op0=ALU.mult, op1=ALU.add)
                return e.scalar_tensor_tensor(out=sc2[:, 1:2], in0=h1, scalar=w0, in1=k1, op0=ALU.mult, op1=ALU.add)
            chain("dve", [f3a,
                          lambda e, h1=h1, sc2=sc2, w1=w1: e.scalar_tensor_tensor(out=sc2[:, 0:1], in0=h1, scalar=w1, in1=sc2[:, 0:1],
                                                                                  op0=ALU.mult, op1=ALU.add),
                          lambda e, i=i, sc2=sc2: e.tensor_tensor(out=mixT[:, 8 + i, 0:2], in0=sc2, in1=KEEP[:, 128 + 2 * i:130 + 2 * i], op=ALU.mult)],
                  ["PIN", f"KEEPs{i}", f"KEEPb{i}", "PV", "PSTu"], [f"mix{8 + i}", f"sc2_{i}"])
        if DBG == 4:
            return
        S.op("dve", lambda e: e.memset(FNC[:, 4:5], 0.0), reads=[], writes=["Afree", "xT", "MS", "SMXf5"])
        MSTATE["owner"] = "xold"
        mixres = [f"mix{m}" for m in range(16)]
        for b in range(4):
            wi = get_w(l, 13 + b)
            Wb = W[wi].rearrange("p (c k m) -> p c k m", c=4, k=KC)
            for ci in range(4):
                dc = 4 * b + ci
                slot = next_slot()
                xo = XO[dc % 2]
                S.op("sp", lambda e, dc=dc, xo=xo: e.dma_start(out=xo, in_=xold_src[:, dc, :]), reads=["xspill", "MS"], writes=[f"XO{dc % 2}"],
                     dma=f"ld_xo{dc % 2}")
                mm_chunk(slot, lambda k, Wb=Wb, ci=ci: Wb[:, ci, k, :], lambda k, t0, tw: mixT[:, k, t0:t0 + tw], KC, reads=mixres + [f"W{wi}"])
                ew3("dve", slot, lambda e, ps, t0, tw, dc=dc, xo=xo: e.tensor_tensor(out=xT[:, dc, t0:t0 + tw], in0=ps, in1=xo[:, t0:t0 + tw], op=ALU.add),
                    reads=[f"XO{dc % 2}", "Afree"], writes=[f"x{dc}", f"slot{slot}"])
        xres = [f"x{dc}" for dc in range(16)]
        S.op("dve", lambda e: e.memset(FNC[:, 5:6], 0.0), reads=xres, writes=["xT", "SMXf6"])
        rmsnorm(2 + l, lambda dc: hT[:, dc, :], "hT")
        S.op("dve", lambda e: e.memset(FNC[:, 6:7], 0.0), reads=[], writes=["MS", "SMXf7"])
        MSTATE["owner"] = "hid"

        def up(g):
            wi = get_w(l, 17 + g)
            Wb = W[wi].rearrange("p (c k m) -> p c k m", c=4, k=KC)
            hid = HID[g % 2]
            for ci in range(4):
                slot = next_slot()
                mm_chunk(slot, lambda k, Wb=Wb, ci=ci: Wb[:, ci, k, :], lambda k, t0, tw: hT[:, k, t0:t0 + tw], KC, reads=["hT", f"W{wi}"])
                ew3("act", slot, lambda e, ps, t0, tw, ci=ci, hid=hid: e.activation(out=hid[:, ci, t0:t0 + tw], in_=ps, func=AF.Relu),
                    reads=["MS"], writes=[f"HID{g % 2}"])
                ew3("dve", slot, lambda e, ps, t0, tw, ci=ci, hid=hid: e.tensor_tensor(out=hid[:, ci, t0:t0 + tw], in0=ps, in1=hid[:, ci, t0:t0 + tw],
                                                                                     op=ALU.mult),
                    reads=["MS", f"HID{g % 2}"], writes=[f"HID{g % 2}", f"slot{slot}"])

        def down(g):
            wi = get_w(l, 33 + g)
            Wd = W[wi].rearrange("p (d f m) -> p d f m", d=16, f=4)
            hid = HID[g % 2]
            for dc in range(16):
                slot = next_slot()
                mm_chunk(slot, lambda k, Wd=Wd, dc=dc: Wd[:, dc, k, :], lambda k, t0, tw, hid=hid: hid[:, k, t0:t0 + tw], 4,
                         reads=[f"HID{g % 2}", f"W{wi}"])
                ew3("dve", slot, lambda e, ps, t0, tw, dc=dc: e.tensor_tensor(out=xT[:, dc, t0:t0 + tw], in0=ps, in1=xT[:, dc, t0:t0 + tw], op=ALU.add),
                    reads=["xT"], writes=[f"x{dc}", f"slot{slot}"])
        up(0)
        up(1)
        for g in range(16):
            down(g)
            if g + 2 < 16:
                up(g + 2)
        S.op("dve", lambda e: e.memset(FNC[:, 7:8], 0.0), reads=xres, writes=["xT", "SMXf8"])

    out_streams = []
    _CACHE["build"] = (nc, S, out_streams)
    dumpctr = {"n": 0}

    def dump(name, ap, reads):
        shape = list(ap.shape)
        d = nc.dram_tensor("dbg_" + name, shape, F32, kind="ExternalOutput").ap()
        dumpctr["n"] += 1
        if ap.dtype == F32:
            store("sp", d, ap, reads, f"st_dbg{dumpctr['n']}")
        else:
            S.op("pool", lambda e: e.dma_start(out=d, in_=ap), reads=reads, writes=[], dma=f"st_dbg{dumpctr['n']}")
            out_streams.append(f"st_dbg{dumpctr['n']}")

    def store(eng, out, in_, reads, stream):
        S.op(eng, lambda e: e.dma_start(out=out, in_=in_), reads=reads, writes=[], dma=stream)
        if stream not in out_streams:
            out_streams.append(stream)

    for l in range(DEPTH if DBG in (5, 99) else 1):
        LB = LBt[:, 8 * l:8 * l + 8]
        OML = LBt[:, 16 + 8 * l:16 + 8 * l + 8]
        NOML = LBt[:, 32 + 8 * l:32 + 8 * l + 8]
        rmsnorm(l, lambda dc: hT[:, dc, :], "hT")
        if DBG == 1:
            dump("hT", hT, ["hT"])
            dump("LBt", LBt, ["LB"])
            dump("Gs", Gs, ["Gs"])
            raise _Stop()
        if l == 1:
            S.op("sp", lambda e: e.dma_start(out=xs_d, in_=xT), reads=["xT"], writes=["xspill"], dma="st_xs")
        xold_src = xT_d if l == 0 else xs_d
        S.op("dve", lambda e: e.memset(FNC[:, 1:2], 0.0), reads=[], writes=["xT", "Afree", "SMXf2"])
        AF_R = ["Afree"]

        S.op("sp", lambda e, l=l: e.dma_start(out=SSm.rearrange("p (i q) -> p i q", i=4)[:, :, 0:2 * NS],
                                              in_=ss_d[l].rearrange("p (i q) -> p i q", i=4)),
             writes=["SSm"], dma="ld_ss")
        S.op("sp", lambda e, l=l: e.dma_start(out=SCm.rearrange("p (i q) -> p i q", i=4)[:, :, 0:30 * NS],
                                              in_=sc_d[l].rearrange("p (i q) -> p i q", i=4)),
             reads=AF_R, writes=["SCm"], dma="ld_sc")
        SS3 = SSm.rearrange("p (i r n) -> p i r n", i=4, r=3)
        SC31 = SCm.rearrange("p (i r n) -> p i r n", i=4, r=31)

        S.op("dve", lambda e: e.memset(APAD[:, :, 0:30], 0.0), reads=AF_R, writes=["APAD"])
        S.op("dve", lambda e: e.memset(UPAD[:, 0:2], 0.0), reads=AF_R, writes=["UPAD"])

        pend = {}
        for b in range(13):
            wcur = get_w(l, b)
            Wb = W[wcur].rearrange("p (c k m) -> p c k m", c=4, k=KC)
            for ci in range(4):
                role, idx = ROLES[4 * b + ci]
                slot = next_slot()
                mm_chunk(slot, lambda k, Wb=Wb, ci=ci: Wb[:, ci, k, :], lambda k, t0, tw: hT[:, k, t0:t0 + tw], KC,
                         reads=["hT", f"W{wcur}"], hook=((lambda: pop_steps(2)) if b >= 5 else None))
                if role == "ccg":
                    ew3("act", slot, lambda e, ps, t0, tw: e.activation(out=CT1[:, t0:t0 + tw], in_=ps, func=AF.Exp, scale=-1.0),
                        reads=AF_R, writes=["CT1"])
                    act_sigmoid_from_exp(CT1, "CT1")
                    vslot = pend.pop(("ccv", idx))
                    i = idx
                    vregs = slot_regs(vslot)

                    def fa(e, i=i, vregs=vregs):
                        ins = None
                        for (ps, a0, a1) in pieces(vregs):
                            dst = APAD[:, i, 30 + a0:30 + a1] if a0 < T else SC31[:, i, 30, :]
                            ins = e.tensor_tensor(out=dst, in0=ps, in1=CT1[:, a0:a1], op=ALU.mult)
                        return ins
                    S.op("dve", fa, reads=[f"slot{vslot}", "CT1"] + AF_R, writes=["APAD", "SCm", f"slot{vslot}"])
                    o_w = PV["ccw"][0] + (l * 4 + i) * 31
                    o_b = PV["ccb"][0] + l * 4 + i
                    ceng = "dve"

                    fcl = [lambda e, i=i, o_w=o_w, o_b=o_b: e.tensor_scalar(out=CO[:, i, 0:T], in0=APAD[:, i, 30:30 + T],
                                                                            scalar1=PVt[:, o_w + 30:o_w + 31], scalar2=PVt[:, o_b:o_b + 1],
                                                                            op0=ALU.mult, op1=ALU.add)]
                    fcl += [lambda e, i=i, j=j, o_w=o_w: e.scalar_tensor_tensor(out=CO[:, i, 0:T], in0=APAD[:, i, j:j + T],
                                                                                scalar=PVt[:, o_w + j:o_w + j + 1], in1=CO[:, i, 0:T],
                                                                                op0=ALU.mult, op1=ALU.add) for j in range(30)]
                    chain(ceng, fcl, ["APAD", "PV"] + AF_R, [f"CO{i}"])
                    wv = PVt[:, o_w:o_w + 31]

                    chain("dve", [lambda e, i=i, wv=wv: e.tensor_tensor(out=SMX[:, 0:NS * 31].rearrange("p (r n) -> p r n", r=31), in0=SC31[:, i, :, :],
                                                                        in1=wv.unsqueeze(2).to_broadcast([128, 31, NS]), op=ALU.mult),
                                  lambda e, i=i: e.tensor_reduce(out=CO[:, i, T:TT], in_=SMX[:, 0:NS * 31].rearrange("p (r n) -> p n r", r=31),
                                                                 axis=AX.X, op=ALU.add),
                                  lambda e, i=i, o_b=o_b: e.tensor_scalar(out=CO[:, i, T:TT], in0=CO[:, i, T:TT], scalar1=PVt[:, o_b:o_b + 1],
                                                                          scalar2=None, op0=ALU.add)],
                          ["SCm", "PV"] + AF_R, [f"CO{i}s", "SMX"])
                    S.op("dve", lambda e, i=i: e.tensor_copy(out=KEEP[:, 30 * i:30 * i + 30], in_=CO[:, i, 0:30]),
                         reads=[f"CO{i}"], writes=[f"KEEPc{i}"])
                elif role == "ccv":
                    pend[("ccv", idx)] = slot
                elif role == "scc":
                    ew3("act", slot, lambda e, ps, t0, tw: e.activation(out=CT2[:, t0:t0 + tw], in_=ps, func=AF.Copy),
                        reads=AF_R, writes=["CT2"])
                elif role == "sch":
                    i = idx
                    regs = slot_regs(slot)

                    def fu(e, i=i, regs=regs):
                        ins = None
                        for (ps, a0, a1) in pieces(regs):
                            dst = UPAD[:, 2 + a0:2 + a1] if a0 < T else SS3[:, i, 2, :]
                            ins = e.tensor_tensor(out=dst, in0=ps, in1=CT2[:, a0:a1], op=ALU.mult)
                        return ins
                    S.op("dve", fu, reads=[f"slot{slot}", "CT2"] + AF_R, writes=["UPAD", "SSm", f"slot{slot}"])
                    o_w = PV["scw"][0] + (l * 4 + i) * 3

                    def fcs0(e, i=i, o_w=o_w):
                        e.tensor_scalar(out=CT3[:, 0:T], in0=UPAD[:, 2:2 + T], scalar1=PVt[:, o_w + 2:o_w + 3], scalar2=None, op0=ALU.mult)
                        e.tensor_tensor(out=SMX[:, 0:NS * 3].rearrange("p (r n) -> p r n", r=3), in0=SS3[:, i, :, :],
                                        in1=PVt[:, o_w:o_w + 3].unsqueeze(2).to_broadcast([128, 3, NS]), op=ALU.mult)
                        return e.tensor_copy(out=FIX[:, 2 * i:2 * i + 2], in_=UPAD[:, 1024:1026])

                    def fcs1(e, i=i, o_w=o_w):
                        e.tensor_reduce(out=CT3[:, T:TT], in_=SMX[:, 0:NS * 3].rearrange("p (r n) -> p n r", r=3), axis=AX.X, op=ALU.add)
                        return e.scalar_tensor_tensor(out=CT3[:, 0:T], in0=UPAD[:, 1:1 + T], scalar=PVt[:, o_w + 1:o_w + 2], in1=CT3[:, 0:T],
                                                      op0=ALU.mult, op1=ALU.add)
                    chain("dve", [fcs0, fcs1,
                                  lambda e, o_w=o_w: e.scalar_tensor_tensor(out=CT3[:, 0:T], in0=UPAD[:, 0:T], scalar=PVt[:, o_w:o_w + 1], in1=CT3[:, 0:T],
                                                                            op0=ALU.mult, op1=ALU.add),
                                  lambda e, i=i: e.tensor_copy(out=KEEP[:, 120 + 2 * i:122 + 2 * i], in_=CT3[:, 0:2])],
                          ["UPAD", "SSm", "PV"] + AF_R, ["CT3", "SMX", f"KEEPs{i}", f"FIXu{i}", "UPADr"])
                elif role == "scb":
                    i = idx
                    regs = slot_regs(slot)

                    def fy(e, i=i, regs=regs):
                        e.tensor_copy(out=KEEP[:, 128 + 2 * i:130 + 2 * i], in_=regs[0][:, 0:2])
                        ins = None
                        for (ps, a0, a1) in pieces(regs):
                            ins = e.tensor_tensor(out=mixT[:, 8 + i, a0:a1], in0=ps, in1=CT3[:, a0:a1], op=ALU.mult)
                        return ins
                    S.op("dve", fy, reads=[f"slot{slot}", "CT3"], writes=[f"mix{8 + i}", f"KEEPb{i}", f"slot{slot}", "UPAD"])
                elif role == "q":
                    if idx == 0:
                        ms_fence("sog")
                    head_q(l, idx, slot, AF_R)
                elif role == "f":
                    head_f(l, idx, slot, AF_R, LB, OML, NOML)
                elif role == "v":
                    head_v(l, idx, slot, AF_R)
                elif role == "og":
                    head_og(l, idx, slot, AF_R)
            if b == 4:
                conformer_ln(l, AF_R)
                store("sp", sc_o[l].rearrange("p (i q) -> p i q", i=4), SCm.rearrange("p (i q) -> p i q", i=4)[:, :, NS:31 * NS], ["SCm"], "st_sc")
                S.op("dve", lambda e: e.memset(FNC[:, 9:10], 0.0), reads=[], writes=["Afree", "SCm", "SSmX", "FNC9"])
                if DBG == 2:
                    dump("mixT", mixT[:, 8:16, :], [f"mix{m}" for m in range(8, 16)])
                    dump("PST", PST[:, 0:120], ["PSTa"])
                    dump("KEEP", KEEP, [f"KEEPc{i}" for i in range(4)] + [f"KEEPs{i}" for i in range(4)] + [f"KEEPb{i}" for i in range(4)])
                    dump("FIX", FIX[:, 0:8], [f"FIXu{i}" for i in range(4)])
                    store("sp", ss_o[l].rearrange("p (i q) -> p i q", i=4), SSm.rearrange("p (i q) -> p i q", i=4)[:, :, NS:3 * NS], ["SSm"], "st_ss")
                    raise _Stop()
        pop_steps(len(STEPS))
        if DBG == 3:
            dump("OL", OL, [f"OL{h}" for h in range(NH)])
            dump("SLE", SLE, [f"SLE{h}" for h in range(NH)])
            dump("OTS", OTS, [f"OTS{h}" for h in range(NH)])
            dump("QH", mixT[:, 0:8, 0:T], [f"mix{h}" for h in range(NH)])
            dump("SOG", SOG, [f"SOG{h}" for h in range(NH)])
            raise _Stop()
        tail_and_rest(l, AF_R, xold_src)
        if DBG == 6 or DBG == 4:
            raise _Stop()

    rmsnorm(4, lambda dc: xT[:, dc, :], "xT")
    store("sp", yT_o, xT, ["xT"], "st_y")
    return nc, S, out_streams


def emit(nc, S, out_streams):
    S.finalize()
    import contextlib
    with contextlib.ExitStack() as st:
        sems = {}
        for e in S.ops:
            sems[e] = st.enter_context(nc.semaphore(f"s_{e}"))
        for k in S.streams:
            sems[k] = st.enter_context(nc.semaphore(f"d_{k}"))
        block = st.enter_context(nc.Block())

        def wval(k, i):
            if k in S.streams:
                return (i + 1) if k.startswith("cc") else 16 * (i + 1)
            return S.sigcount[k][i]

        def run(engname, eng):
            for rec in S.ops[engname]:
                for (k, i) in rec["waits"]:
                    eng.wait_ge(sems[k], wval(k, i))
                ins = rec["fn"](eng)
                if rec["dma"] is not None:
                    ins.then_inc(sems[rec["dma"]], 1 if rec["dma"].startswith("cc") else 16)
                elif rec["sig"]:
                    ins.then_inc(sems[engname], 1)
            if engname == "sp":
                for k in out_streams:
                    eng.wait_ge(sems[k], 16 * S.streams[k])

        @block.sync
        def _(e):
            run("sp", e)

        @block.scalar
        def _(e):
            run("act", e)

        @block.vector
        def _(e):
            run("dve", e)

        @block.gpsimd
        def _(e):
            run("pool", e)

        @block.tensor
        def _(e):
            run("pe", e)


_CACHE = {}


def _pack_weights(w_in, w_out, w_up, w_down):
    wall = np.empty((DEPTH * NBLK, 128, WBLK), np.float32)
    cols = np.concatenate([np.arange(_COL0[r] + 128 * i, _COL0[r] + 128 * i + 128) for (r, i) in ROLES])

    def lhs_blocks(Wm, nblk):
        ncols = Wm.shape[1]
        t = Wm.reshape(KC, 128, ncols // 512, 4, 128)
        return np.ascontiguousarray(t.transpose(2, 1, 3, 0, 4)).reshape(ncols // 512, 128, WBLK)
    for l in range(DEPTH):
        base = l * NBLK
        wall[base:base + 13] = lhs_blocks(w_in[l][:, cols], 13)
        wall[base + 13:base + 17] = lhs_blocks(w_out[l], 4)
        wall[base + 17:base + 33] = lhs_blocks(w_up[l], 16)
        t = w_down[l].reshape(16, 4, 128, 16, 128)
        wall[base + 33:base + 49] = np.ascontiguousarray(t.transpose(0, 2, 3, 1, 4)).reshape(16, 128, WBLK)
    return wall


def kernel(x_prompt, x_sample, state_hgrn, state_sconv, state_cconv, g_mix, w_in, hgrn_lb, hgrn_norm_g, sconv_w,
           cconv_w, cconv_b, cconv_ln_g, cconv_ln_b, w_out, g_mlp, w_up, w_down, g_final):
    f = lambda a: np.asarray(a, dtype=np.float32)
    x_prompt, x_sample = f(x_prompt), f(x_sample)
    wall = _pack_weights(f(w_in), f(w_out), f(w_up), f(w_down))
    pv = np.zeros((128, NPV), np.float32)

    def put(name, arr):
        o, w = PV[name]
        pv[:, o:o + w] = arr.reshape(128, w)
    g5 = np.stack([f(g_mix)[0], f(g_mix)[1], f(g_mlp)[0], f(g_mlp)[1], f(g_final)])
    put("g", g5.reshape(5, 16, 128).transpose(2, 0, 1))
    put("lb", f(hgrn_lb).reshape(2, 8, 128).transpose(2, 0, 1))
    put("hg", f(hgrn_norm_g).reshape(2, 8, 128).transpose(2, 0, 1))
    put("scw", f(sconv_w).reshape(2, 3, 4, 128).transpose(3, 0, 2, 1))
    put("ccw", f(cconv_w).reshape(2, 31, 4, 128).transpose(3, 0, 2, 1))
    put("ccb", f(cconv_b).reshape(2, 4, 128).transpose(2, 0, 1))
    put("clg", f(cconv_ln_g).reshape(2, 4, 128).transpose(2, 0, 1))
    put("clb", f(cconv_ln_b).reshape(2, 4, 128).transpose(2, 0, 1))
    cv = np.zeros((128, NCV), np.float32)
    cv[:, CV["ident"][0]:CV["ident"][0] + 128] = np.eye(128, dtype=np.float32)
    caus = (np.arange(64)[:, None] <= np.arange(64)[None, :]).astype(np.float32)
    cv[:, CV["causal"][0]:CV["causal"][0] + 64] = np.concatenate([caus, caus], 0)
    sm = np.ones(1024, np.float32)
    sm[::64] = 0.0
    cv[:, CV["smask"][0]:CV["smask"][0] + 1024] = sm[None, :]

    sh, ssv, scv = f(state_hgrn), f(state_sconv), f(state_cconv)
    in_maps = []
    for c in range(NCORES):
        j, half = c // 2, c % 2
        xc = np.concatenate([x_prompt[j, half * T:(half + 1) * T], x_sample[NS * c:NS * (c + 1), 0]], 0)
        xTc = np.ascontiguousarray(xc.reshape(TT, KC, 128).transpose(2, 1, 0))
        pvc = pv.copy()
        pvc[:, PV["flag"][0]] = float(half)
        shc = np.ascontiguousarray(sh[:, NS * c:NS * (c + 1)].transpose(0, 2, 3, 1, 4)).reshape(DEPTH, NH, 128, NS * 128)
        ssc = np.ascontiguousarray(ssv[:, NS * c:NS * (c + 1)].reshape(DEPTH, NS, 2, 4, 128).transpose(0, 4, 3, 2, 1)).reshape(DEPTH, 128, 4 * NS * 2)
        scc = np.ascontiguousarray(scv[:, NS * c:NS * (c + 1)].reshape(DEPTH, NS, 30, 4, 128).transpose(0, 4, 3, 2, 1)).reshape(DEPTH, 128, 4 * NS * 30)
        in_maps.append({"xT": xTc, "wall": wall, "pv": pvc, "cv": cv, "sh": shc, "ss": ssc, "sc": scc})
    if DBG not in (5, 6, 99):
        wall = wall[:13]
        for m in in_maps:
            m["wall"] = wall
    if "nc" not in _CACHE:
        try:
            build_program()
        except _Stop:
            pass
        nc, S, outs = _CACHE["build"]
        emit(nc, S, outs)
        _CACHE["nc"] = nc
    nc = _CACHE["nc"]
    res = run_bass_kernel_spmd(nc, in_maps, core_ids=list(range(NCORES)))
    R = res.results
    _CACHE["results"] = R
    y_prompt = np.empty((4, 2048, D), np.float32)
    y_sample = np.empty((128, 1, D), np.float32)
    hp = np.empty((DEPTH, 4, NH, 128, 128), np.float32)
    sp = np.empty((DEPTH, 4, 2, 512), np.float32)
    cp = np.empty((DEPTH, 4, 30, 512), np.float32)
    hs = np.empty((DEPTH, 128, NH, 128, 128), np.float32)
    sso = np.empty((DEPTH, 128, 2, 512), np.float32)
    sco = np.empty((DEPTH, 128, 30, 512), np.float32)
    for c in range(NCORES):
        j, half = c // 2, c % 2
        yc = R[c]["yT"].transpose(2, 1, 0).reshape(TT, D)
        y_prompt[j, half * T:(half + 1) * T] = yc[:T]
        y_sample[NS * c:NS * (c + 1), 0] = yc[T:]
        hs[:, NS * c:NS * (c + 1)] = R[c]["hs"].reshape(DEPTH, NH, 128, NS, 128).transpose(0, 3, 1, 2, 4)
        sso[:, NS * c:NS * (c + 1)] = R[c]["sso"].reshape(DEPTH, 128, 4, 2, NS).transpose(0, 4, 3, 2, 1).reshape(DEPTH, NS, 2, 512)
        sco[:, NS * c:NS * (c + 1)] = R[c]["sco"].reshape(DEPTH, 128, 4, 30, NS).transpose(0, 4, 3, 2, 1).reshape(DEPTH, NS, 30, 512)
        if half == 1:
            hp[:, j] = R[c]["hp"].reshape(DEPTH, 128, NH, 128).transpose(0, 2, 1, 3)
            sp[:, j] = R[c]["sp"].reshape(DEPTH, 128, 4, 2).transpose(0, 3, 2, 1).reshape(DEPTH, 2, 512)
            cp[:, j] = R[c]["cp"].reshape(DEPTH, 128, 4, 30).transpose(0, 3, 2, 1).reshape(DEPTH, 30, 512)
    return (y_prompt, y_sample, hp, sp, cp, hs, sso, sco)
```

```python
import numpy as np
import concourse.bass as bass
import concourse.mybir as mybir
from concourse.bass_utils import run_bass_kernel_spmd

F32 = mybir.dt.float32
BF16 = mybir.dt.bfloat16
AF = mybir.ActivationFunctionType
ALU = mybir.AluOpType
AX = mybir.AxisListType

import os
NCORES = 8
DBG = int(os.environ.get("KDBG", "99"))


class _Stop(Exception):
    pass

D = 2048
T = 1024
NS = 16
TT = T + NS
KC = 16
DEPTH = 2
EPS = 1e-6
NH = 8
TILES = [(0, 352), (352, 352), (704, 336)]
NBLK = 49
WBLK = 8192

ROLES = []
for i in (0, 1):
    ROLES += [("ccv", 2 * i), ("ccg", 2 * i), ("ccv", 2 * i + 1), ("ccg", 2 * i + 1)]
_sc = []
for i in range(4):
    _sc += [("scc", i), ("sch", i), ("scb", i)]
ROLES += _sc
for h in range(NH):
    ROLES += [("f", h), ("q", h), ("v", h), ("og", h)]
assert len(ROLES) == 52
_COL0 = {"q": 0, "f": 1024, "v": 2048, "og": 3072, "scb": 4096, "scc": 4608, "sch": 5120, "ccv": 5632, "ccg": 6144}

PV = {}
_o = 0
for _n, _w in [("g", 5 * 16), ("lb", 2 * 8), ("hg", 2 * 8), ("scw", 2 * 4 * 3), ("ccw", 2 * 4 * 31),
               ("ccb", 2 * 4), ("clg", 2 * 4), ("clb", 2 * 4), ("flag", 1)]:
    PV[_n] = (_o, _w)
    _o += _w
NPV = _o
CV = {}
_o = 0
for _n, _w in [("ident", 128), ("causal", 64), ("smask", 1024)]:
    CV[_n] = (_o, _w)
    _o += _w
NCV = _o


class Sched:
    def __init__(self):
        self.ops = {e: [] for e in ("pe", "act", "dve", "pool", "sp")}
        self.streams = {}
        self.last_w = {}
        self.readers = {}
        self.known = {e: {} for e in self.ops}

    def op(self, eng, fn, reads=(), writes=(), dma=None, nowait_self=False):
        need = {}

        def add(ev):
            if ev is None:
                return
            k, i = ev
            if need.get(k, -1) < i:
                need[k] = i
        for r in reads:
            add(self.last_w.get(r))
        for w in writes:
            add(self.last_w.get(w))
            for ev in list(self.readers.get(w, {}).items()):
                add(ev)
        waits = []
        for k, i in need.items():
            if nowait_self and k == eng:
                continue
            if k in self.streams:
                i = max(i, self.streams[k] - 1)
            if self.known[eng].get(k, -1) >= i:
                continue
            self.known[eng][k] = i
            waits.append((k, i))
        rec = {"fn": fn, "waits": waits, "dma": dma, "sig": False}
        self.ops[eng].append(rec)
        if dma is not None:
            self.streams.setdefault(dma, 0)
            ev = (dma, self.streams[dma])
            self.streams[dma] += 1
        else:
            ev = (eng, len(self.ops[eng]) - 1)
        for r in reads:
            lst = self.readers.setdefault(r, {})
            if lst.get(ev[0], -1) < ev[1]:
                lst[ev[0]] = ev[1]
        for w in writes:
            self.last_w[w] = ev
            self.readers[w] = {}
        return ev

    def finalize(self):
        for e, lst in self.ops.items():
            for rec in lst:
                for (k, i) in rec["waits"]:
                    if k not in self.streams:
                        self.ops[k][i]["sig"] = True
        self.sigcount = {}
        for e, lst in self.ops.items():
            c = 0
            arr = []
            for rec in lst:
                if rec["sig"]:
                    c += 1
                arr.append(c)
            self.sigcount[e] = arr


def build_program():
    nc = bass.Bass("TRN2", target_bir_lowering=False)
    S = Sched()

    def din(name, shape, dt=F32):
        return nc.dram_tensor(name, shape, dt, kind="ExternalInput").ap()

    def dout(name, shape):
        return nc.dram_tensor(name, shape, F32, kind="ExternalOutput").ap()

    xT_d = din("xT", [128, KC, TT])
    NW = DEPTH * NBLK if DBG in (5, 6, 99) else 13
    wall = din("wall", [NW, 128, WBLK])
    pv_d = din("pv", [128, NPV])
    cv_d = din("cv", [128, NCV])
    sh_d = din("sh", [DEPTH, NH, 128, NS * 128])
    ss_d = din("ss", [DEPTH, 128, 4 * NS * 2])
    sc_d = din("sc", [DEPTH, 128, 4 * NS * 30])
    yT_o = dout("yT", [128, KC, TT])
    hp_o = dout("hp", [DEPTH, 128, NH * 128])
    sp_o = dout("sp", [DEPTH, 128, 8])
    cp_o = dout("cp", [DEPTH, 128, 120])
    hs_o = dout("hs", [DEPTH, NH, 128, NS * 128])
    ss_o = dout("sso", [DEPTH, 128, 4 * NS * 2])
    sc_o = dout("sco", [DEPTH, 128, 4 * NS * 30])
    xs_d = nc.dram_tensor("xspill", [128, KC, TT], F32).ap()
    PAYW = NH * 128 + 120 + 8
    pay_in = [nc.dram_tensor(f"pay_in{l}", [128, PAYW], F32) for l in range(DEPTH)]
    pay_out = [nc.dram_tensor(f"pay_out{l}", [256, PAYW], F32) for l in range(DEPTH)]

    def sb(name, shape, dt=F32):
        return nc.alloc_sbuf_tensor(name, shape, dt).ap()

    A = sb("A", [128, KC * TT])
    xT = A.rearrange("p (c t) -> p c t", c=KC)
    hT = sb("hT", [128, KC, TT], BF16)
    mixT = sb("mixT", [128, KC, TT], BF16)
    W = [sb(f"W{i}", [128, WBLK], BF16) for i in range(2)]
    MS = sb("MS", [128, 4352])
    PVt = sb("PVt", [128, NPV])
    CVt = sb("CVt", [128, NCV])
    identb = sb("identb", [128, 128], BF16)
    onesb = sb("onesb", [128, 128], BF16)
    causb = sb("causb", [128, 64], BF16)
    Gs = sb("Gs", [128, 5 * 16])
    LBt = sb("LBt", [128, 4 * 16])
    VT = sb("VT", [128, TT], BF16)
    VTOK2 = [sb(f"VTOK{i}", [128, 8 * 128], BF16) for i in range(2)]
    EBL2 = [sb(f"EBL{i}", [128, 16]) for i in range(2)]
    SMP2 = [sb(f"SMP{i}", [128, 4 * NS]) for i in range(2)]
    EBE = sb("EBE", [128, NH])
    SSm = sb("SSm", [128, 4 * NS * 3])
    FNC = sb("FNC", [128, 16])
    KEEP = sb("KEEP", [128, 4 * 30 + 4 * 2 + 4 * 2])
    FIX = sb("FIX", [128, 512])

    def carve(base, n, dt=F32):
        v = A[:, base:base + (n if dt == F32 else n // 2)]
        return v if dt == F32 else v.bitcast(BF16)
    _a = 0

    def take(n_f32):
        nonlocal _a
        r = (_a, n_f32)
        _a += n_f32
        return r
    r_T0, r_QS, r_FG, r_KK = take(TT), take(TT), take(TT), take(TT)
    r_BD, r_EE = take(T), take(T)
    r_QT, r_KT, r_KTOK = take(T // 2), take(T // 2), take(T // 2)
    r_QT2, r_KT2, r_KTOK2 = take(T // 2), take(T // 2), take(T // 2)
    r_OL = take(NH * T // 2)
    r_SST = take(NS * 128)
    assert _a <= KC * TT, _a
    T0 = A[:, r_T0[0]:r_T0[0] + TT]
    QS = A[:, r_QS[0]:r_QS[0] + TT]
    FG = A[:, r_FG[0]:r_FG[0] + TT]
    KK = A[:, r_KK[0]:r_KK[0] + TT]
    BD = A[:, r_BD[0]:r_BD[0] + T]
    EE = A[:, r_EE[0]:r_EE[0] + T]
    QTt2 = [A[:, r[0]:r[0] + T // 2].bitcast(BF16) for r in (r_QT, r_QT2)]
    KTt2 = [A[:, r[0]:r[0] + T // 2].bitcast(BF16) for r in (r_KT, r_KT2)]
    KTOK2 = [A[:, r[0]:r[0] + T // 2].bitcast(BF16) for r in (r_KTOK, r_KTOK2)]
    OL = A[:, r_OL[0]:r_OL[0] + NH * T // 2].bitcast(BF16).rearrange("p (h t) -> p h t", h=NH)
    Sst = A[:, r_SST[0]:r_SST[0] + NS * 128]
    APAD = A[:, 0:4 * 1054].rearrange("p (i t) -> p i t", i=4)
    CO = A[:, 4216:4216 + 4 * TT].rearrange("p (i t) -> p i t", i=4)
    CT1 = A[:, 8376:8376 + TT]
    CT2 = A[:, 9416:9416 + TT]
    CT3 = A[:, 10456:10456 + TT]
    UPAD = A[:, 11496:11496 + 1026]
    CMU = A[:, 11496:11496 + TT]
    CRS = A[:, 12536:12536 + TT]
    SMX = A[:, 12536:12536 + 512]
    CXB = A[:, 13576:13576 + TT // 2].bitcast(BF16)
    CXQ = A[:, 14096:14096 + TT // 2].bitcast(BF16)
    SCm = A[:, 14616:14616 + 4 * NS * 31]
    PAY = A[:, 0:PAYW]
    PIN = A[:, PAYW:2 * PAYW]
    SINB = A[:, 2 * PAYW:2 * PAYW + 64].bitcast(BF16)
    TSQ2 = [A[:, o:o + TT // 2].bitcast(BF16) for o in (2400, 3960)]
    TRS2 = [A[:, o:o + TT] for o in (2920, 4480)]
    SINB2 = [SINB, A[:, 5520:5520 + 64].bitcast(BF16)]
    Rr = MS[:, 0:TT]
    SQ = [MS[:, 1040 + i * 520:1040 + (i + 1) * 520].bitcast(BF16) for i in range(2)]
    SOG = MS[:, 0:4160].bitcast(BF16).rearrange("p (h t) -> p h t", h=NH)
    XO = [MS[:, i * TT:(i + 1) * TT] for i in range(2)]
    HID = [MS[:, i * 2080:(i + 1) * 2080].bitcast(BF16).rearrange("p (c t) -> p c t", c=4) for i in range(2)]

    def pvs(name, a, b=None):
        o, w = PV[name]
        return PVt[:, o + a:o + (a + 1 if b is None else b)]

    PS = [nc.alloc_psum_tensor(f"ps{i}", [128, 512], F32).ap() for i in range(8)]

    def slot_regs(s):
        return [PS[3 * s][:, 0:352], PS[3 * s + 1][:, 0:352], PS[3 * s + 2][:, 0:336]]

    def pieces(regs):
        return [(regs[0], 0, 352), (regs[1], 352, 704), (regs[2][:, 0:320], 704, 1024), (regs[2][:, 320:336], 1024, 1040)]
    B6, B7 = PS[6], PS[7]
    PTR = PS[7].bitcast(BF16)

    MSTATE = {"owner": None}

    def ms_fence(owner):
        if MSTATE["owner"] != owner:
            S.op("dve", lambda e: e.memset(FNC[:, 0:1], 0.0), reads=[], writes=["MS", "SMXf"])
            MSTATE["owner"] = owner

    wctr = {"n": 0}

    def load_w(l, b):
        i = wctr["n"] % 2
        wctr["n"] += 1
        src = wall[l * NBLK + b]
        S.op("pool", lambda e, i=i, src=src: e.dma_start(out=W[i], in_=src), reads=[], writes=[f"W{i}"], dma=f"w{i}")
        return i

    def mm_chunk(slot, lhs_fn, rhs_fn, nk, reads, extra_w=(), hook=None):
        regs = slot_regs(slot)

        def mk(k0, k1):
            def fn(e):
                ins = None
                for k in range(k0, k1):
                    for ti, (t0, tw) in enumerate(TILES):
                        ins = e.matmul(regs[ti][:, 0:tw], lhsT=lhs_fn(k), rhs=rhs_fn(k, t0, tw),
                                       start=(k == 0), stop=(k == nk - 1))
                return ins
            return fn
        if hook is None:
            S.op("pe", mk(0, nk), reads=reads, writes=[f"slot{slot}"] + list(extra_w))
        else:
            q = nk // 4
            for part in range(4):
                S.op("pe", mk(part * q, (part + 1) * q), reads=reads, writes=[f"slot{slot}"] + list(extra_w), nowait_self=(part > 0))
                hook()

    def ew3(eng, slot, fn3, reads, writes):
        regs = slot_regs(slot)

        def fn(e):
            ins = None
            for ti, (t0, tw) in enumerate(TILES):
                ins = fn3(e, regs[ti][:, 0:tw], t0, tw)
            return ins
        S.op(eng, fn, reads=[f"slot{slot}"] + list(reads), writes=list(writes) + [f"slot{slot}"])

    slotctr = {"n": 0}

    def next_slot():
        s = slotctr["n"] % 2
        slotctr["n"] += 1
        return s

    S.op("sp", lambda e: e.dma_start(out=xT, in_=xT_d), writes=["xT"], dma="ld_x")
    S.op("sp", lambda e: e.dma_start(out=PVt, in_=pv_d), writes=["PV"], dma="ld_p")
    S.op("sp", lambda e: e.dma_start(out=CVt, in_=cv_d), writes=["CV"], dma="ld_c")
    S.op("dve", lambda e: e.memset(onesb, 1.0), writes=["ones"])
    o_id, o_ca, o_sm = CV["ident"][0], CV["causal"][0], CV["smask"][0]
    identf = CVt[:, o_id:o_id + 128]
    smask = CVt[:, o_sm:o_sm + 1024]
    S.op("dve", lambda e: e.tensor_copy(out=identb, in_=identf), reads=["CV"], writes=["identb"])
    S.op("dve", lambda e: e.tensor_copy(out=causb, in_=CVt[:, o_ca:o_ca + 64]), reads=["CV"], writes=["causb"])
    S.op("dve", lambda e: e.tensor_scalar(out=Gs, in0=pvs("g", 0, 80), scalar1=float(np.sqrt(D)), scalar2=None,
                                          op0=ALU.mult), reads=["PV"], writes=["Gs"])
    lb0, lb1 = pvs("lb", 0, 8), pvs("lb", 8, 16)
    sc0, sc1, sc2 = LBt[:, 48:56], LBt[:, 56:64], LBt[:, 0:8]

    def lbsetup():
        S.op("dve", lambda e: e.tensor_tensor(out=sc2, in0=lb0, in1=lb1, op=ALU.max), reads=["PV"], writes=["LB"])
        S.op("dve", lambda e: e.tensor_tensor(out=sc0, in0=lb0, in1=sc2, op=ALU.subtract), reads=["LB", "PV"], writes=["LB"])
        S.op("dve", lambda e: e.tensor_tensor(out=sc1, in0=lb1, in1=sc2, op=ALU.subtract), reads=["LB", "PV"], writes=["LB"])
        S.op("act", lambda e: e.activation(out=sc0, in_=sc0, func=AF.Exp), reads=["LB"], writes=["LB"])
        S.op("act", lambda e: e.activation(out=sc1, in_=sc1, func=AF.Exp), reads=["LB"], writes=["LB"])
        S.op("dve", lambda e: e.tensor_tensor(out=sc2, in0=sc0, in1=sc1, op=ALU.add), reads=["LB"], writes=["LB"])
        S.op("dve", lambda e: e.reciprocal(out=sc2, in_=sc2), reads=["LB"], writes=["LB"])
        S.op("dve", lambda e: e.tensor_tensor(out=sc0, in0=sc0, in1=sc2, op=ALU.mult), reads=["LB"], writes=["LB"])
        S.op("dve", lambda e: e.tensor_tensor(out=sc1, in0=sc1, in1=sc2, op=ALU.mult), reads=["LB"], writes=["LB"])
        S.op("dve", lambda e: e.tensor_tensor(out=sc2, in0=sc0, in1=sc1, op=ALU.add), reads=["LB"], writes=["LB"])
        S.op("dve", lambda e: e.tensor_tensor(out=LBt[:, 8:16], in0=sc2, in1=sc0, op=ALU.subtract), reads=["LB"], writes=["LB"])
        S.op("dve", lambda e: e.tensor_tensor(out=LBt[:, 0:8], in0=sc0, in1=sc0, op=ALU.subtract), reads=["LB"], writes=["LB"])
        S.op("dve", lambda e: e.tensor_scalar(out=LBt[:, 16:32], in0=LBt[:, 0:16], scalar1=-1.0, scalar2=1.0,
                                              op0=ALU.mult, op1=ALU.add), reads=["LB"], writes=["LB"])
        S.op("dve", lambda e: e.tensor_scalar(out=LBt[:, 32:48], in0=LBt[:, 0:16], scalar1=-1.0, scalar2=None,
                                              op0=ALU.add), reads=["LB"], writes=["LB"])
    lbsetup()

    def chain(eng, fns, reads, writes):
        for f in fns:
            S.op(eng, f, reads=reads, writes=writes)

    def act_sigmoid_from_exp(buf, res, extra_reads=()):
        chain("act", [lambda e: e.activation(out=buf, in_=buf, func=AF.Ln, bias=1.0),
                      lambda e: e.activation(out=buf, in_=buf, func=AF.Exp, scale=-1.0)], [res] + list(extra_reads), [res])

    def act_rsqrt_inplace(buf, res, addc, extra_reads=()):
        chain("act", [lambda e: e.activation(out=buf, in_=buf, func=AF.Ln, bias=float(addc)),
                      lambda e: e.activation(out=buf, in_=buf, func=AF.Exp, scale=-0.5)], [res] + list(extra_reads), [res])

    def rmsnorm(gidx, dst_fn, dst_res):
        ms_fence("norm")
        slot = next_slot()
        regs = slot_regs(slot)
        for dc in range(KC):
            q = SQ[dc % 2]
            S.op("act", lambda e, dc=dc, q=q: e.activation(out=q, in_=xT[:, dc, :], func=AF.Square),
                 reads=["xT", "MS"], writes=[f"SQ{dc % 2}"])

            def fn(e, dc=dc, q=q):
                ins = None
                for ti, (t0, tw) in enumerate(TILES):
                    ins = e.matmul(regs[ti][:, 0:tw], lhsT=onesb, rhs=q[:, t0:t0 + tw], start=(dc == 0), stop=(dc == KC - 1))
                return ins
            S.op("pe", fn, reads=[f"SQ{dc % 2}", "ones"], writes=[f"slot{slot}"])
        ew3("act", slot, lambda e, ps, t0, tw: e.activation(out=Rr[:, t0:t0 + tw], in_=ps, func=AF.Ln, bias=float(D * EPS)),
            reads=["MS"], writes=["Rr"])
        S.op("act", lambda e: e.activation(out=Rr, in_=Rr, func=AF.Exp, scale=-0.5), reads=["Rr"], writes=["Rr"])
        for dc in range(KC):
            S.op("dve", lambda e, dc=dc: e.scalar_tensor_tensor(out=dst_fn(dc), in0=xT[:, dc, :],
                                                                scalar=Gs[:, gidx * 16 + dc:gidx * 16 + dc + 1], in1=Rr,
                                                                op0=ALU.mult, op1=ALU.mult),
                 reads=["xT", "Rr", "Gs"], writes=[dst_res])

    SLE = sb("SLE", [128, NH * 128])
    SL = sb("SL", [128, 128])
    SDB = [sb(f"SDB{i}", [128, 128], BF16) for i in range(2)]
    KVT = sb("KVT", [32, 256])
    AN = sb("AN", [32, 4 * 128])
    ATS = sb("ATS", [128, 128], BF16)
    BLt = sb("BLt", [128, 16])
    PFX = sb("PFX", [128, 16])
    PFE = sb("PFE", [128, 16])
    ONE16 = sb("ONE16", [128, 16])
    OTS = sb("OTS", [128, NH * NS])
    PST = sb("PST", [128, 128])
    HGs = sb("HGs", [128, 16])
    S.op("dve", lambda e: e.memset(ONE16, 1.0), writes=["ONE16"])
    S.op("dve", lambda e: e.tensor_scalar(out=HGs, in0=pvs("hg", 0, 16), scalar1=float(np.sqrt(128.0)), scalar2=None,
                                          op0=ALU.mult), reads=["PV"], writes=["HGs"])

    seq = []
    for l in range(DEPTH):
        order = list(range(13)) + list(range(13, 17))
        ml = [17, 18]
        for g in range(16):
            ml.append(33 + g)
            if g + 2 < 16:
                ml.append(17 + g + 2)
        order += ml
        assert len(order) == NBLK and len(set(order)) == NBLK
        seq += [(l, b) for b in order]
    wp = {"p": 0}

    def get_w(l, b):
        p = wp["p"]
        assert seq[p] == (l, b), (seq[p], l, b)
        if p == 0:
            load_w(*seq[0])
        if p + 1 < len(seq) and seq[p + 1][0] * NBLK + seq[p + 1][1] < NW:
            load_w(*seq[p + 1])
        wp["p"] += 1
        return p % 2

    def fenceA():
        S.op("dve", lambda e: e.memset(FNC[:, 2:3], 0.0), reads=[], writes=["Afree", "SMXf3"])

    def ln_silu(l, n0, n1, src_fn, dst_fn, tag, use_slots):
        pass

    def conformer_ln(l, AF_R):
        sa, sb_ = next_slot(), next_slot()
        ra, rb = slot_regs(sa), slot_regs(sb_)
        for i in range(4):
            S.op("act", lambda e, i=i: e.activation(out=CXB, in_=CO[:, i, :], func=AF.Copy),
                 reads=[f"CO{i}", f"CO{i}s"] + AF_R, writes=["CXB"])
            S.op("act", lambda e, i=i: e.activation(out=CXQ, in_=CO[:, i, :], func=AF.Square),
                 reads=[f"CO{i}", f"CO{i}s"] + AF_R, writes=["CXQ"])

            def fn(e, i=i):
                ins = None
                for ti, (t0, tw) in enumerate(TILES):
                    e.matmul(ra[ti][:, 0:tw], lhsT=onesb, rhs=CXB[:, t0:t0 + tw], start=(i == 0), stop=(i == 3))
                    ins = e.matmul(rb[ti][:, 0:tw], lhsT=onesb, rhs=CXQ[:, t0:t0 + tw], start=(i == 0), stop=(i == 3))
                return ins
            S.op("pe", fn, reads=["CXB", "CXQ", "ones"], writes=[f"slot{sa}", f"slot{sb_}"])
        ew3("dve", sa, lambda e, ps, t0, tw: e.tensor_scalar(out=CMU[:, t0:t0 + tw], in0=ps, scalar1=1.0 / 512, scalar2=None, op0=ALU.mult),
            reads=AF_R, writes=["CMU", "UPAD", "UPADr", f"slot{sa}"])
        ew3("dve", sb_, lambda e, ps, t0, tw: e.tensor_scalar(out=CRS[:, t0:t0 + tw], in0=ps, scalar1=1.0 / 512, scalar2=None, op0=ALU.mult),
            reads=AF_R, writes=["CRS", "SMX", f"slot{sb_}"])

        chain("dve", [lambda e: e.tensor_tensor(out=CT1, in0=CMU, in1=CMU, op=ALU.mult),
                      lambda e: e.tensor_tensor(out=CRS, in0=CRS, in1=CT1, op=ALU.subtract)],
              ["CMU", "CRS"] + AF_R, ["CRS", "CT1"])
        act_rsqrt_inplace(CRS, "CRS", EPS)
        for i in range(4):
            o_g = PV["clg"][0] + l * 4 + i
            o_b = PV["clb"][0] + l * 4 + i

            chain("dve", [lambda e, i=i: e.tensor_tensor(out=CT2, in0=CO[:, i, :], in1=CMU, op=ALU.subtract),
                          lambda e: e.tensor_tensor(out=CT2, in0=CT2, in1=CRS, op=ALU.mult),
                          lambda e, o_g=o_g, o_b=o_b: e.tensor_scalar(out=CT2, in0=CT2, scalar1=PVt[:, o_g:o_g + 1],
                                                                      scalar2=PVt[:, o_b:o_b + 1], op0=ALU.mult, op1=ALU.add)],
                  [f"CO{i}", f"CO{i}s", "CMU", "CRS", "PV"] + AF_R, ["CT2"])
            S.op("act", lambda e: e.activation(out=CT3, in_=CT2, func=AF.Exp, scale=-1.0), reads=["CT2"], writes=["CT3"])
            act_sigmoid_from_exp(CT3, "CT3")

            def f2(e, i=i):
                return e.tensor_tensor(out=mixT[:, 12 + i, :], in0=CT2, in1=CT3, op=ALU.mult)
            S.op("dve", f2, reads=["CT2", "CT3"], writes=["CT3", "CT2", f"mix{12 + i}"])
        S.op("dve", lambda e: e.tensor_copy(out=PST[:, 0:120].rearrange("p (i r) -> p i r", i=4), in_=APAD[:, :, 1024:1054]),
             reads=["APAD"] + AF_R, writes=["PSTa"])

    STEPS = []

    def pop_steps(n):
        for _ in range(n):
            if STEPS:
                STEPS.pop(0)()

    def head_q(l, h, sq, AF_R):
        R = AF_R
        hp = h % 2
        ew3("act", sq, lambda e, ps, t0, tw: e.activation(out=T0[:, t0:t0 + tw], in_=ps, func=AF.Exp, scale=-1.0), reads=R, writes=["T0"])
        act_sigmoid_from_exp(T0, "T0")
        ew3("dve", sq, lambda e, ps, t0, tw: e.tensor_tensor(out=QS[:, t0:t0 + tw], in0=ps, in1=T0[:, t0:t0 + tw], op=ALU.mult),
            reads=["T0"] + R, writes=["QS", f"slot{sq}"])
        S.op("dve", lambda e: e.tensor_copy(out=SMP2[hp][:, 2 * NS:3 * NS], in_=QS[:, T:TT]), reads=["QS"], writes=[f"SMPq{hp}"])
        QTt, KTt = QTt2[hp], KTt2[hp]
        S.op("dve", lambda e: e.tensor_tensor(out=mixT[:, h, 0:T], in0=QS[:, 0:T], in1=EE, op=ALU.mult), reads=["QS", "EE"], writes=[f"mix{h}"])
        S.op("act", lambda e: e.activation(out=EE, in_=BD, func=AF.Exp), reads=["BD", f"mix{h}"], writes=["EE"])
        S.op("dve", lambda e: e.scalar_tensor_tensor(out=QTt, in0=EE, scalar=1e35, in1=QS[:, 0:T], op0=ALU.min, op1=ALU.mult),
             reads=["EE", "QS"] + R, writes=[f"QT{hp}"])


    def head_f(l, h, sf, AF_R, LB, OML, NOML):
        R = AF_R
        hp = h % 2
        QTt, KTt, EBL = QTt2[hp], KTt2[hp], EBL2[hp]
        ew3("act", sf, lambda e, ps, t0, tw: e.activation(out=FG[:, t0:t0 + tw], in_=ps, func=AF.Exp, scale=-1.0), reads=R, writes=["FG", f"slot{sf}"])
        act_sigmoid_from_exp(FG, "FG")
        chain("dve", [lambda e: e.tensor_scalar(out=KK, in0=FG, scalar1=NOML[:, h:h + 1], scalar2=OML[:, h:h + 1], op0=ALU.mult, op1=ALU.add),
                      lambda e: e.tensor_scalar(out=FG, in0=FG, scalar1=OML[:, h:h + 1], scalar2=LB[:, h:h + 1], op0=ALU.mult, op1=ALU.add),
                      lambda e: e.tensor_scalar(out=FG, in0=FG, scalar1=1e-30, scalar2=None, op0=ALU.max)],
              ["FG", "LB"] + R, ["FG", "KK"])

        def fcp(e):
            e.tensor_copy(out=SMP2[hp][:, 0:NS], in_=FG[:, T:TT])
            return e.tensor_copy(out=SMP2[hp][:, 3 * NS:4 * NS], in_=KK[:, T:TT])
        S.op("dve", fcp, reads=["FG", "KK"], writes=[f"SMPf{hp}"])
        S.op("act", lambda e: e.activation(out=FG[:, 0:T], in_=FG[:, 0:T], func=AF.Ln), reads=["FG", f"SMPf{hp}"], writes=["FG"])
        S.op("dve", lambda e: e.tensor_tensor_scan(out=BD, data0=smask, data1=FG[:, 0:T], initial=0.0, op0=ALU.mult, op1=ALU.add),
             reads=["FG", "CV"] + R, writes=["BD"])
        BD3 = BD.rearrange("p (c t) -> p c t", t=64)
        EE3 = EE.rearrange("p (c t) -> p c t", t=64)
        chain("dve", [lambda e: e.tensor_copy(out=BLt, in_=BD3[:, :, 63]),
                      lambda e: e.tensor_tensor_scan(out=PFX, data0=ONE16, data1=BLt, initial=0.0, op0=ALU.mult, op1=ALU.add),
                      lambda e: e.tensor_tensor(out=PFE, in0=PFX, in1=BLt, op=ALU.subtract),
                      lambda e: e.tensor_tensor(out=EE3, in0=BD3, in1=PFE.unsqueeze(2).to_broadcast([128, 16, 64]), op=ALU.add),
                      lambda e: e.tensor_tensor(out=BD3, in0=BD3, in1=BLt.unsqueeze(2).to_broadcast([128, 16, 64]), op=ALU.subtract)],
              ["BD", "ONE16"] + R, ["BD", "EE", "BLt", "PFX"])

        def fe1(e):
            e.activation(out=EBL, in_=BLt, func=AF.Exp)
            e.activation(out=EBE[:, h:h + 1], in_=PFX[:, 15:16], func=AF.Exp)
            return e.activation(out=EE, in_=EE, func=AF.Exp)
        S.op("act", fe1, reads=["EE", "BLt", "PFX"], writes=["EE", f"EBL{hp}", f"EBE{h}"])
        S.op("act", lambda e: e.activation(out=FG[:, 0:T], in_=BD, func=AF.Exp, scale=-1.0), reads=["BD", "FG"], writes=["FG"])
        S.op("dve", lambda e: e.tensor_tensor(out=KTt, in0=FG[:, 0:T], in1=KK[:, 0:T], op=ALU.mult), reads=["FG", "KK"] + R, writes=[f"KT{hp}"])

    def head_v(l, h, sv, AF_R):
        R = AF_R
        hp = h % 2
        vr = slot_regs(sv)

        def fvv(e):
            ins = None
            for (ps, a0, a1) in pieces(vr):
                if a0 < T:
                    ins = e.activation(out=VT[:, a0:a1], in_=ps, func=AF.Copy)
                else:
                    ins = e.activation(out=SMP2[hp][:, NS:2 * NS], in_=ps, func=AF.Copy)
            return ins
        S.op("act", fvv, reads=[f"slot{sv}"], writes=["VT", f"SMPv{hp}", f"slot{sv}"])

    def head_og(l, h, so, AF_R):
        R = AF_R
        hp = h % 2
        QTt, KTt, KTOK, VTOK, EBL, SMP = QTt2[hp], KTt2[hp], KTOK2[hp], VTOK2[hp], EBL2[hp], SMP2[hp]
        Fs, VSs, QSs, KKs = SMP[:, 0:NS], SMP[:, NS:2 * NS], SMP[:, 2 * NS:3 * NS], SMP[:, 3 * NS:4 * NS]
        rQT, rKT, rKTOK, rVTOK, rEBL = f"QT{hp}", f"KT{hp}", f"KTOK{hp}", f"VTOK{hp}", f"EBL{hp}"
        rSMP = [f"SMPq{hp}", f"SMPf{hp}", f"SMPv{hp}"]
        ew3("act", so, lambda e, ps, t0, tw: e.activation(out=T0[:, t0:t0 + tw], in_=ps, func=AF.Exp, scale=-1.0), reads=["QS"] + R, writes=["T0"])
        act_sigmoid_from_exp(T0, "T0")
        ew3("dve", so, lambda e, ps, t0, tw: e.tensor_tensor(out=SOG[:, h, t0:t0 + tw], in0=ps, in1=T0[:, t0:t0 + tw], op=ALU.mult),
            reads=["T0", "MS"], writes=[f"SOG{h}", f"slot{so}"])
        def tr_step(which):
            src, dst, sr, dr = ((KTt, KTOK, rKT, rKTOK), (VT, VTOK, "VT", rVTOK))[which]

            def ft(e):
                ins = None
                for j in range(8):
                    ins = e.transpose(out=PTR[:, j * 128:(j + 1) * 128], in_=src[:, j * 128:(j + 1) * 128], identity=identb)
                return ins
            S.op("pe", ft, reads=[sr, "identb"], writes=["B7"])
            S.op("act", lambda e: e.activation(out=dst, in_=PTR, func=AF.Copy), reads=R, writes=[dr, "B7"])
        tr_step(1)
        STEPS.append(lambda: tr_step(0))

        def o_mm(g):
            for p in range(2):
                def fo(e, g=g, p=p):
                    c = 2 * g + p
                    oreg = B6[:, (c % 8) * 64:(c % 8) * 64 + 64]
                    ins = e.matmul(oreg, lhsT=VTOK[64 * p:64 * p + 64, g * 128:(g + 1) * 128], rhs=ATS[64 * p:64 * p + 64, 64 * p:64 * p + 64],
                                   start=True, stop=(c == 0))
                    if c > 0:
                        ins = e.matmul(oreg, lhsT=SDB[p], rhs=QTt[:, 64 * c:64 * c + 64], start=False, stop=True)
                    return ins
                S.op("pe", fo, reads=[rVTOK, "ATS", "SDB0", "SDB1", rQT], writes=["B6"])
            if g % 4 == 3:
                half = g // 4
                S.op("act", lambda e, half=half: e.activation(out=OL[:, h, 512 * half:512 * half + 512], in_=B6, func=AF.Copy),
                     reads=R, writes=[f"OL{h}", "B6"])

        def scan_step(g):
            if g > 0:
                o_mm(g - 1)
            for p in range(2):
                def fAU(e, g=g, p=p):
                    c = 2 * g + p
                    e.matmul(B7[:, 64 * p:64 * p + 64], lhsT=KTt[:, g * 128:(g + 1) * 128], rhs=QTt[:, 64 * c:64 * c + 64], start=True, stop=True)
                    return e.matmul(B7[:, 128 + 128 * p:256 + 128 * p], lhsT=KTOK[64 * p:64 * p + 64, g * 128:(g + 1) * 128],
                                    rhs=VTOK[64 * p:64 * p + 64, g * 128:(g + 1) * 128], start=True, stop=True)
                S.op("pe", fAU, reads=[rKT, rQT, rKTOK, rVTOK], writes=["B7"])
            S.op("dve", lambda e: e.tensor_tensor(out=ATS.rearrange("p (a t) -> p a t", a=2), in0=B7[:, 0:128].rearrange("p (a t) -> p a t", a=2),
                                                  in1=causb.unsqueeze(1).to_broadcast([128, 2, 64]), op=ALU.mult),
                 reads=["causb"], writes=["ATS", "B7"])
            for p in range(2):
                c = 2 * g + p
                if c > 0:
                    S.op("dve", lambda e, c=c, p=p: e.tensor_scalar(out=SDB[p], in0=SL, scalar1=EBL[:, c:c + 1], scalar2=None, op0=ALU.mult),
                         reads=["SL", rEBL], writes=[f"SDB{p}"])
                    S.op("dve", lambda e, c=c, p=p: e.scalar_tensor_tensor(out=SL, in0=SL, scalar=EBL[:, c:c + 1], in1=B7[:, 128 + 128 * p:256 + 128 * p],
                                                                            op0=ALU.mult, op1=ALU.add),
                         reads=["SL", rEBL], writes=["SL", "B7"])
                else:
                    S.op("dve", lambda e, p=p: e.tensor_copy(out=SL, in_=B7[:, 128 + 128 * p:256 + 128 * p]), reads=[], writes=["SL", "B7"])

        def scan_end():
            o_mm(7)
            S.op("dve", lambda e: e.tensor_copy(out=SLE[:, h * 128:(h + 1) * 128], in_=SL), reads=["SL"], writes=[f"SLE{h}"])
            S.op("sp", lambda e: e.dma_start(out=Sst, in_=sh_d[l, h]), reads=R, writes=["Sst"], dma="ld_sh")

            def ftr(e):
                e.matmul(B7[0:NS, 0:128], lhsT=KKs, rhs=identf, start=True, stop=True)
                return e.matmul(B7[0:NS, 128:256], lhsT=VSs, rhs=identf, start=True, stop=True)
            S.op("pe", ftr, reads=rSMP + ["CV"], writes=["B7"])
            S.op("dve", lambda e: e.tensor_copy(out=KVT[0:NS, :], in_=B7[0:NS, 0:256]), reads=[], writes=["KVT", "B7"])

        def samp_step(q4):
            bank, bres = (B6, "B6") if q4 % 2 == 0 else (B7, "B7")
            S.op("dve", lambda e: e.tensor_tensor(out=AN[0:NS, :].rearrange("p (n k) -> p n k", n=4),
                                                  in0=KVT[0:NS, 0:128].unsqueeze(1).to_broadcast([NS, 4, 128]),
                                                  in1=identf[0:NS, 4 * q4:4 * q4 + 4].unsqueeze(2).to_broadcast([NS, 4, 128]), op=ALU.mult),
                 reads=["KVT", "CV"], writes=["AN"])

            def fkv(e):
                ins = None
                for jn in range(4):
                    ins = e.matmul(bank[:, 128 * jn:128 * jn + 128], lhsT=AN[0:NS, jn * 128:(jn + 1) * 128], rhs=KVT[0:NS, 128:256],
                                   start=True, stop=True)
                return ins
            S.op("pe", fkv, reads=["AN", "KVT"], writes=[bres])

            def fsu(e):
                ins = None
                for jn in range(4):
                    n = 4 * q4 + jn
                    ins = e.scalar_tensor_tensor(out=Sst[:, n * 128:(n + 1) * 128], in0=Sst[:, n * 128:(n + 1) * 128], scalar=Fs[:, n:n + 1],
                                                 in1=bank[:, 128 * jn:128 * jn + 128], op0=ALU.mult, op1=ALU.add)
                return ins
            S.op("dve", fsu, reads=rSMP, writes=["Sst", bres])

        def samp_end():
            def fos(e):
                ins = None
                for n in range(NS):
                    ins = e.matmul(B6[:, n:n + 1], lhsT=Sst[:, n * 128:(n + 1) * 128], rhs=QSs[:, n:n + 1], start=True, stop=True)
                return ins
            S.op("pe", fos, reads=["Sst"] + rSMP, writes=["B6"])
            S.op("dve", lambda e: e.tensor_copy(out=OTS[:, h * NS:(h + 1) * NS], in_=B6[:, 0:NS]), reads=[], writes=[f"OTS{h}", "B6"])
            store("sp", hs_o[l, h], Sst, ["Sst"] + R, "st_hs")
        for g in range(8):
            STEPS.append(lambda g=g: scan_step(g))
        STEPS.append(scan_end)
        for q4 in range(4):
            STEPS.append(lambda q4=q4: samp_step(q4))
        STEPS.append(samp_end)

    def tail_and_rest(l, AF_R, xold_src):
        R = AF_R
        store("sp", ss_o[l].rearrange("p (i q) -> p i q", i=4), SSm.rearrange("p (i q) -> p i q", i=4)[:, :, NS:3 * NS], ["SSm"], "st_ss")
        S.op("dve", lambda e: e.tensor_copy(out=PST[:, 120:128], in_=FIX[:, 0:8]), reads=[f"FIXu{i}" for i in range(4)], writes=["PSTu"])
        fenceA()
        R2 = ["Afree"]

        def fpay(e):
            e.tensor_copy(out=PAY[:, 0:1024], in_=SLE)
            return e.tensor_copy(out=PAY[:, 1024:1152], in_=PST)
        S.op("dve", fpay, reads=[f"SLE{h}" for h in range(NH)] + ["PSTa", "PSTu"] + R2, writes=["PAY"])
        S.op("sp", lambda e: e.dma_start(out=pay_in[l].ap(), in_=PAY), reads=["PAY"] + R2, writes=["payin"], dma=f"st_pay{l}")
        S.op("pool", lambda e: e.collective_compute("AllGather", ALU.bypass, replica_groups=[[0, 1], [2, 3], [4, 5], [6, 7]],
                                                    ins=[pay_in[l].ap().opt()], outs=[pay_out[l].ap().opt()]),
             reads=["payin"], writes=["payout"], dma=f"cc{l}")
        S.op("sp", lambda e: e.dma_start(out=PIN, in_=pay_out[l].ap()[0:128, :]), reads=["payout"] + R2, writes=["PIN"], dma=f"ld_pin{l}")
        o_fl = PV["flag"][0]
        S.op("dve", lambda e: e.tensor_scalar(out=PIN, in0=PIN, scalar1=PVt[:, o_fl:o_fl + 1], scalar2=None, op0=ALU.mult),
             reads=["PIN", "PV"], writes=["PIN"])
        store("sp", sp_o[l], PST[:, 120:128], ["PSTu"], "st_sp")
        store("sp", cp_o[l], PST[:, 0:120], ["PSTa"], "st_cp")
        slots_h = {}

        def stA(h):
            hp = h % 2
            S.op("dve", lambda e: e.tensor_copy(out=SINB2[hp], in_=PIN[:, h * 128:(h + 1) * 128]), reads=["PIN"], writes=[f"SINB{hp}"])
            for half in range(2):
                S.op("pe", lambda e, half=half: e.matmul(B6, lhsT=SINB2[hp], rhs=mixT[:, h, 512 * half:512 * half + 512], start=True, stop=True),
                     reads=[f"SINB{hp}", f"mix{h}"], writes=["B6"])
                S.op("dve", lambda e, half=half: e.tensor_tensor(out=OL[:, h, 512 * half:512 * half + 512], in0=B6,
                                                                in1=OL[:, h, 512 * half:512 * half + 512], op=ALU.add),
                     reads=[f"OL{h}"], writes=[f"OL{h}", "B6"])
            S.op("dve", lambda e: e.scalar_tensor_tensor(out=SLE[:, h * 128:(h + 1) * 128], in0=PIN[:, h * 128:(h + 1) * 128],
                                                         scalar=EBE[:, h:h + 1], in1=SLE[:, h * 128:(h + 1) * 128], op0=ALU.mult, op1=ALU.add),
                 reads=["PIN", f"EBE{h}", "PAY"], writes=[f"SLE{h}"])

            def fsq(e):
                e.activation(out=TSQ2[hp][:, 0:T], in_=OL[:, h, :], func=AF.Square)
                return e.activation(out=TSQ2[hp][:, T:TT], in_=OTS[:, h * NS:(h + 1) * NS], func=AF.Square)
            S.op("act", fsq, reads=[f"OL{h}", f"OTS{h}"] + R2, writes=[f"TSQ{hp}"])
            slot = next_slot()
            slots_h[h] = slot
            mm_chunk(slot, lambda k: onesb, lambda k, t0, tw: TSQ2[hp][:, t0:t0 + tw], 1, reads=[f"TSQ{hp}", "ones"])

        def stB(h):
            hp = h % 2
            slot = slots_h[h]
            ew3("act", slot, lambda e, ps, t0, tw: e.activation(out=TRS2[hp][:, t0:t0 + tw], in_=ps, func=AF.Ln, bias=float(128 * EPS)),
                reads=R2, writes=[f"TRS{hp}", f"slot{slot}"])
            S.op("act", lambda e: e.activation(out=TRS2[hp], in_=TRS2[hp], func=AF.Exp, scale=-0.5), reads=[f"TRS{hp}"], writes=[f"TRS{hp}"])

        def stC(h):
            hp = h % 2

            def fn_a(e):
                e.tensor_tensor(out=OL[:, h, :], in0=OL[:, h, :], in1=TRS2[hp][:, 0:T], op=ALU.mult)
                return e.tensor_tensor(out=OTS[:, h * NS:(h + 1) * NS], in0=OTS[:, h * NS:(h + 1) * NS], in1=TRS2[hp][:, T:TT], op=ALU.mult)

            def fn_b(e):
                e.scalar_tensor_tensor(out=mixT[:, h, 0:T], in0=OL[:, h, :], scalar=HGs[:, l * 8 + h:l * 8 + h + 1], in1=SOG[:, h, 0:T],
                                       op0=ALU.mult, op1=ALU.mult)
                return e.scalar_tensor_tensor(out=mixT[:, h, T:TT], in0=OTS[:, h * NS:(h + 1) * NS], scalar=HGs[:, l * 8 + h:l * 8 + h + 1],
                                              in1=SOG[:, h, T:TT], op0=ALU.mult, op1=ALU.mult)
            chain("dve", [fn_a, fn_b], [f"TRS{hp}", f"OL{h}", f"OTS{h}", f"SOG{h}", "HGs"], [f"OL{h}", f"OTS{h}", f"mix{h}"])
        for i in range(NH + 2):
            if i < NH:
                stA(i)
            if 0 <= i - 1 < NH:
                stB(i - 1)
            if 0 <= i - 2 < NH:
                stC(i - 2)
        store("sp", hp_o[l], SLE, [f"SLE{h}" for h in range(NH)], "st_hp")
        HB = FIX[:, 0:60]
        C30 = FIX[:, 64:64 + 120].rearrange("p (i t) -> p i t", i=4)
        XB30 = FIX[:, 192:192 + 60].bitcast(BF16)
        XQ30 = FIX[:, 256:256 + 60].bitcast(BF16)
        M30, R30, Y30, E30 = FIX[:, 320:350], FIX[:, 352:382], FIX[:, 384:414], FIX[:, 416:446]
        for i in range(4):
            o_w = PV["ccw"][0] + (l * 4 + i) * 31

            def fx0(e, i=i):
                e.memset(HB[:, 30:60], 0.0)
                e.tensor_copy(out=HB[:, 0:30], in_=PIN[:, 1024 + 30 * i:1024 + 30 * i + 30])
                return e.tensor_copy(out=C30[:, i, :], in_=KEEP[:, 30 * i:30 * i + 30])
            fl_ = [fx0] + [lambda e, i=i, j=j, o_w=o_w: e.scalar_tensor_tensor(out=C30[:, i, :], in0=HB[:, j:j + 30], scalar=PVt[:, o_w + j:o_w + j + 1],
                                                                             in1=C30[:, i, :], op0=ALU.mult, op1=ALU.add) for j in range(30)]
            chain("dve", fl_, ["PIN", f"KEEPc{i}", "PV", "PSTu"], ["HB", f"C30_{i}"])
        S.op("act", lambda e: e.activation(out=XB30, in_=FIX[:, 64:184], func=AF.Copy), reads=[f"C30_{i}" for i in range(4)], writes=["XB30"])
        S.op("act", lambda e: e.activation(out=XQ30, in_=FIX[:, 64:184], func=AF.Square), reads=[f"C30_{i}" for i in range(4)], writes=["XQ30"])

        def fl(e):
            ins = None
            for i in range(4):
                e.matmul(PS[6][:, 64:94], lhsT=onesb, rhs=XB30[:, 30 * i:30 * i + 30], start=(i == 0), stop=(i == 3))
            for i in range(4):
                ins = e.matmul(PS[6][:, 128:158], lhsT=onesb, rhs=XQ30[:, 30 * i:30 * i + 30], start=(i == 0), stop=(i == 3))
            return ins
        S.op("pe", fl, reads=["XB30", "XQ30", "ones"], writes=["B6"])

        def fl2a(e):
            e.tensor_scalar(out=M30, in0=PS[6][:, 64:94], scalar1=1.0 / 512, scalar2=None, op0=ALU.mult)
            return e.tensor_scalar(out=R30, in0=PS[6][:, 128:158], scalar1=1.0 / 512, scalar2=None, op0=ALU.mult)
        chain("dve", [fl2a, lambda e: e.tensor_tensor(out=Y30, in0=M30, in1=M30, op=ALU.mult),
                      lambda e: e.tensor_tensor(out=R30, in0=R30, in1=Y30, op=ALU.subtract)],
              ["Y30"], ["M30", "B6", "Y30"])
        act_rsqrt_inplace(R30, "M30", EPS)
        for i in range(4):
            o_g = PV["clg"][0] + l * 4 + i
            o_b = PV["clb"][0] + l * 4 + i

            chain("dve", [lambda e, i=i: e.tensor_tensor(out=Y30, in0=C30[:, i, :], in1=M30, op=ALU.subtract),
                          lambda e: e.tensor_tensor(out=Y30, in0=Y30, in1=R30, op=ALU.mult),
                          lambda e, o_g=o_g, o_b=o_b: e.tensor_scalar(out=Y30, in0=Y30, scalar1=PVt[:, o_g:o_g + 1], scalar2=PVt[:, o_b:o_b + 1],
                                                                      op0=ALU.mult, op1=ALU.add)],
                  ["M30", f"C30_{i}", "PV"], ["Y30"])
            S.op("act", lambda e: e.activation(out=E30, in_=Y30, func=AF.Exp, scale=-1.0), reads=["Y30"], writes=["E30"])
            act_sigmoid_from_exp(E30, "E30")

            def f2(e, i=i):
                return e.tensor_tensor(out=mixT[:, 12 + i, 0:30], in0=Y30, in1=E30, op=ALU.mult)
            S.op("dve", f2, reads=["Y30", "E30"], writes=["E30", "Y30", f"mix{12 + i}"])
            o_w = PV["scw"][0] + (l * 4 + i) * 3
            h0 = PIN[:, 1144 + 2 * i:1144 + 2 * i + 1]
            h1 = PIN[:, 1144 + 2 * i + 1:1144 + 2 * i + 2]
            k0 = KEEP[:, 120 + 2 * i:121 + 2 * i]
            k1 = KEEP[:, 121 + 2 * i:122 + 2 * i]
            sc2 = FIX[:, 448 + 2 * i:450 + 2 * i]

            w0, w1 = PVt[:, o_w:o_w + 1], PVt[:, o_w + 1:o_w + 2]

            def f3a(e, h0=h0, h1=h1, k0=k0, k1=k1, sc2=sc2, w0=w0):
                e.scalar_tensor_tensor(out=sc2[:, 0:1], in0=h0, scalar=w0, in1=k0, op0=ALU.mult, op1=ALU.add)
                return e.scalar_tensor_tensor(out=sc2[:, 1:2], in0=h1, scalar=w0, in1=k1, op0=ALU.mult, op1=ALU.add)
            chain("dve", [f3a,
                          lambda e, h1=h1, sc2=sc2, w1=w1: e.scalar_tensor_tensor(out=sc2[:, 0:1], in0=h1, scalar=w1, in1=sc2[:, 0:1],
                                                                                  op0=ALU.mult, op1=ALU.add),
                          lambda e, i=i, sc2=sc2: e.tensor_tensor(out=mixT[:, 8 + i, 0:2], in0=sc2, in1=KEEP[:, 128 + 2 * i:130 + 2 * i], op=ALU.mult)],
                  ["PIN", f"KEEPs{i}", f"KEEPb{i}", "PV", "PSTu"], [f"mix{8 + i}", f"sc2_{i}"])
        if DBG == 4:
            return
        S.op("dve", lambda e: e.memset(FNC[:, 4:5], 0.0), reads=[], writes=["Afree", "xT", "MS", "SMXf5"])
        MSTATE["owner"] = "xold"
        mixres = [f"mix{m}" for m in range(16)]
        for b in range(4):
            wi = get_w(l, 13 + b)
            Wb = W[wi].rearrange("p (c k m) -> p c k m", c=4, k=KC)
            for ci in range(4):
                dc = 4 * b + ci
                slot = next_slot()
                xo = XO[dc % 2]
                S.op("sp", lambda e, dc=dc, xo=xo: e.dma_start(out=xo, in_=xold_src[:, dc, :]), reads=["xspill", "MS"], writes=[f"XO{dc % 2}"],
                     dma=f"ld_xo{dc % 2}")
                mm_chunk(slot, lambda k, Wb=Wb, ci=ci: Wb[:, ci, k, :], lambda k, t0, tw: mixT[:, k, t0:t0 + tw], KC, reads=mixres + [f"W{wi}"])
                ew3("dve", slot, lambda e, ps, t0, tw, dc=dc, xo=xo: e.tensor_tensor(out=xT[:, dc, t0:t0 + tw], in0=ps, in1=xo[:, t0:t0 + tw], op=ALU.add),
                    reads=[f"XO{dc % 2}", "Afree"], writes=[f"x{dc}", f"slot{slot}"])
        xres = [f"x{dc}" for dc in range(16)]
        S.op("dve", lambda e: e.memset(FNC[:, 5:6], 0.0), reads=xres, writes=["xT", "SMXf6"])
        rmsnorm(2 + l, lambda dc: hT[:, dc, :], "hT")
        S.op("dve", lambda e: e.memset(FNC[:, 6:7], 0.0), reads=[], writes=["MS", "SMXf7"])
        MSTATE["owner"] = "hid"

        def up(g):
            wi = get_w(l, 17 + g)
            Wb = W[wi].rearrange("p (c k m) -> p c k m", c=4, k=KC)
            hid = HID[g % 2]
            for ci in range(4):
                slot = next_slot()
                mm_chunk(slot, lambda k, Wb=Wb, ci=ci: Wb[:, ci, k, :], lambda k, t0, tw: hT[:, k, t0:t0 + tw], KC, reads=["hT", f"W{wi}"])
                ew3("act", slot, lambda e, ps, t0, tw, ci=ci, hid=hid: e.activation(out=hid[:, ci, t0:t0 + tw], in_=ps, func=AF.Relu),
                    reads=["MS"], writes=[f"HID{g % 2}"])
                ew3("dve", slot, lambda e, ps, t0, tw, ci=ci, hid=hid: e.tensor_tensor(out=hid[:, ci, t0:t0 + tw], in0=ps, in1=hid[:, ci, t0:t0 + tw],
                                                                                     op=ALU.mult),
                    reads=["MS", f"HID{g % 2}"], writes=[f"HID{g % 2}", f"slot{slot}"])

        def down(g):
            wi = get_w(l, 33 + g)
            Wd = W[wi].rearrange("p (d f m) -> p d f m", d=16, f=4)
            hid = HID[g % 2]
            for dc in range(16):
                slot = next_slot()
                mm_chunk(slot, lambda k, Wd=Wd, dc=dc: Wd[:, dc, k, :], lambda k, t0, tw, hid=hid: hid[:, k, t0:t0 + tw], 4,
                         reads=[f"HID{g % 2}", f"W{wi}"])
                ew3("dve", slot, lambda e, ps, t0, tw, dc=dc: e.tensor_tensor(out=xT[:, dc, t0:t0 + tw], in0=ps, in1=xT[:, dc, t0:t0 + tw], op=ALU.add),
                    reads=["xT"], writes=[f"x{dc}", f"slot{slot}"])
        up(0)
        up(1)
        for g in range(16):
            down(g)
            if g + 2 < 16:
                up(g + 2)
        S.op("dve", lambda e: e.memset(FNC[:, 7:8], 0.0), reads=xres, writes=["xT", "SMXf8"])

    out_streams = []
    _CACHE["build"] = (nc, S, out_streams)
    dumpctr = {"n": 0}

    def dump(name, ap, reads):
        shape = list(ap.shape)
        d = nc.dram_tensor("dbg_" + name, shape, F32, kind="ExternalOutput").ap()
        dumpctr["n"] += 1
        if ap.dtype == F32:
            store("sp", d, ap, reads, f"st_dbg{dumpctr['n']}")
        else:
            S.op("pool", lambda e: e.dma_start(out=d, in_=ap), reads=reads, writes=[], dma=f"st_dbg{dumpctr['n']}")
            out_streams.append(f"st_dbg{dumpctr['n']}")

    def store(eng, out, in_, reads, stream):
        S.op(eng, lambda e: e.dma_start(out=out, in_=in_), reads=reads, writes=[], dma=stream)
        if stream not in out_streams:
            out_streams.append(stream)

    for l in range(DEPTH if DBG in (5, 99) else 1):
        LB = LBt[:, 8 * l:8 * l + 8]
        OML = LBt[:, 16 + 8 * l:16 + 8 * l + 8]
        NOML = LBt[:, 32 + 8 * l:32 + 8 * l + 8]
        rmsnorm(l, lambda dc: hT[:, dc, :], "hT")
        if DBG == 1:
            dump("hT", hT, ["hT"])
            dump("LBt", LBt, ["LB"])
            dump("Gs", Gs, ["Gs"])
            raise _Stop()
        if l == 1:
            S.op("sp", lambda e: e.dma_start(out=xs_d, in_=xT), reads=["xT"], writes=["xspill"], dma="st_xs")
        xold_src = xT_d if l == 0 else xs_d
        S.op("dve", lambda e: e.memset(FNC[:, 1:2], 0.0), reads=[], writes=["xT", "Afree", "SMXf2"])
        AF_R = ["Afree"]

        S.op("sp", lambda e, l=l: e.dma_start(out=SSm.rearrange("p (i q) -> p i q", i=4)[:, :, 0:2 * NS],
                                              in_=ss_d[l].rearrange("p (i q) -> p i q", i=4)),
             writes=["SSm"], dma="ld_ss")
        S.op("sp", lambda e, l=l: e.dma_start(out=SCm.rearrange("p (i q) -> p i q", i=4)[:, :, 0:30 * NS],
                                              in_=sc_d[l].rearrange("p (i q) -> p i q", i=4)),
             reads=AF_R, writes=["SCm"], dma="ld_sc")
        SS3 = SSm.rearrange("p (i r n) -> p i r n", i=4, r=3)
        SC31 = SCm.rearrange("p (i r n) -> p i r n", i=4, r=31)

        S.op("dve", lambda e: e.memset(APAD[:, :, 0:30], 0.0), reads=AF_R, writes=["APAD"])
        S.op("dve", lambda e: e.memset(UPAD[:, 0:2], 0.0), reads=AF_R, writes=["UPAD"])

        pend = {}
        for b in range(13):
            wcur = get_w(l, b)
            Wb = W[wcur].rearrange("p (c k m) -> p c k m", c=4, k=KC)
            for ci in range(4):
                role, idx = ROLES[4 * b + ci]
                slot = next_slot()
                mm_chunk(slot, lambda k, Wb=Wb, ci=ci: Wb[:, ci, k, :], lambda k, t0, tw: hT[:, k, t0:t0 + tw], KC,
                         reads=["hT", f"W{wcur}"], hook=((lambda: pop_steps(1)) if b >= 5 else None))
                if role == "ccg":
                    ew3("act", slot, lambda e, ps, t0, tw: e.activation(out=CT1[:, t0:t0 + tw], in_=ps, func=AF.Exp, scale=-1.0),
                        reads=AF_R, writes=["CT1"])
                    act_sigmoid_from_exp(CT1, "CT1")
                    vslot = pend.pop(("ccv", idx))
                    i = idx
                    vregs = slot_regs(vslot)

                    def fa(e, i=i, vregs=vregs):
                        ins = None
                        for (ps, a0, a1) in pieces(vregs):
                            dst = APAD[:, i, 30 + a0:30 + a1] if a0 < T else SC31[:, i, 30, :]
                            ins = e.tensor_tensor(out=dst, in0=ps, in1=CT1[:, a0:a1], op=ALU.mult)
                        return ins
                    S.op("dve", fa, reads=[f"slot{vslot}", "CT1"] + AF_R, writes=["APAD", "SCm", f"slot{vslot}"])
                    o_w = PV["ccw"][0] + (l * 4 + i) * 31
                    o_b = PV["ccb"][0] + l * 4 + i
                    ceng = "dve"

                    fcl = [lambda e, i=i, o_w=o_w, o_b=o_b: e.tensor_scalar(out=CO[:, i, 0:T], in0=APAD[:, i, 30:30 + T],
                                                                            scalar1=PVt[:, o_w + 30:o_w + 31], scalar2=PVt[:, o_b:o_b + 1],
                                                                            op0=ALU.mult, op1=ALU.add)]
                    fcl += [lambda e, i=i, j=j, o_w=o_w: e.scalar_tensor_tensor(out=CO[:, i, 0:T], in0=APAD[:, i, j:j + T],
                                                                                scalar=PVt[:, o_w + j:o_w + j + 1], in1=CO[:, i, 0:T],
                                                                                op0=ALU.mult, op1=ALU.add) for j in range(30)]
                    chain(ceng, fcl, ["APAD", "PV"] + AF_R, [f"CO{i}"])
                    wv = PVt[:, o_w:o_w + 31]

                    chain("dve", [lambda e, i=i, wv=wv: e.tensor_tensor(out=SMX[:, 0:NS * 31].rearrange("p (r n) -> p r n", r=31), in0=SC31[:, i, :, :],
                                                                        in1=wv.unsqueeze(2).to_broadcast([128, 31, NS]), op=ALU.mult),
                                  lambda e, i=i: e.tensor_reduce(out=CO[:, i, T:TT], in_=SMX[:, 0:NS * 31].rearrange("p (r n) -> p n r", r=31),
                                                                 axis=AX.X, op=ALU.add),
                                  lambda e, i=i, o_b=o_b: e.tensor_scalar(out=CO[:, i, T:TT], in0=CO[:, i, T:TT], scalar1=PVt[:, o_b:o_b + 1],
                                                                          scalar2=None, op0=ALU.add)],
                          ["SCm", "PV"] + AF_R, [f"CO{i}s", "SMX"])
                    S.op("dve", lambda e, i=i: e.tensor_copy(out=KEEP[:, 30 * i:30 * i + 30], in_=CO[:, i, 0:30]),
                         reads=[f"CO{i}"], writes=[f"KEEPc{i}"])
                elif role == "ccv":
                    pend[("ccv", idx)] = slot
                elif role == "scc":
                    ew3("act", slot, lambda e, ps, t0, tw: e.activation(out=CT2[:, t0:t0 + tw], in_=ps, func=AF.Copy),
                        reads=AF_R, writes=["CT2"])
                elif role == "sch":
                    i = idx
                    regs = slot_regs(slot)

                    def fu(e, i=i, regs=regs):
                        ins = None
                        for (ps, a0, a1) in pieces(regs):
                            dst = UPAD[:, 2 + a0:2 + a1] if a0 < T else SS3[:, i, 2, :]
                            ins = e.tensor_tensor(out=dst, in0=ps, in1=CT2[:, a0:a1], op=ALU.mult)
                        return ins
                    S.op("dve", fu, reads=[f"slot{slot}", "CT2"] + AF_R, writes=["UPAD", "SSm", f"slot{slot}"])
                    o_w = PV["scw"][0] + (l * 4 + i) * 3

                    def fcs0(e, i=i, o_w=o_w):
                        e.tensor_scalar(out=CT3[:, 0:T], in0=UPAD[:, 2:2 + T], scalar1=PVt[:, o_w + 2:o_w + 3], scalar2=None, op0=ALU.mult)
                        e.tensor_tensor(out=SMX[:, 0:NS * 3].rearrange("p (r n) -> p r n", r=3), in0=SS3[:, i, :, :],
                                        in1=PVt[:, o_w:o_w + 3].unsqueeze(2).to_broadcast([128, 3, NS]), op=ALU.mult)
                        return e.tensor_copy(out=FIX[:, 2 * i:2 * i + 2], in_=UPAD[:, 1024:1026])

                    def fcs1(e, i=i, o_w=o_w):
                        e.tensor_reduce(out=CT3[:, T:TT], in_=SMX[:, 0:NS * 3].rearrange("p (r n) -> p n r", r=3), axis=AX.X, op=ALU.add)
                        return e.scalar_tensor_tensor(out=CT3[:, 0:T], in0=UPAD[:, 1:1 + T], scalar=PVt[:, o_w + 1:o_w + 2], in1=CT3[:, 0:T],
                                                      op0=ALU.mult, op1=ALU.add)
                    chain("dve", [fcs0, fcs1,
                                  lambda e, o_w=o_w: e.scalar_tensor_tensor(out=CT3[:, 0:T], in0=UPAD[:, 0:T], scalar=PVt[:, o_w:o_w + 1], in1=CT3[:, 0:T],
                                                                            op0=ALU.mult, op1=ALU.add),
                                  lambda e, i=i: e.tensor_copy(out=KEEP[:, 120 + 2 * i:122 + 2 * i], in_=CT3[:, 0:2])],
                          ["UPAD", "SSm", "PV"] + AF_R, ["CT3", "SMX", f"KEEPs{i}", f"FIXu{i}", "UPADr"])
                elif role == "scb":
                    i = idx
                    regs = slot_regs(slot)

                    def fy(e, i=i, regs=regs):
                        e.tensor_copy(out=KEEP[:, 128 + 2 * i:130 + 2 * i], in_=regs[0][:, 0:2])
                        ins = None
                        for (ps, a0, a1) in pieces(regs):
                            ins = e.tensor_tensor(out=mixT[:, 8 + i, a0:a1], in0=ps, in1=CT3[:, a0:a1], op=ALU.mult)
                        return ins
                    S.op("dve", fy, reads=[f"slot{slot}", "CT3"], writes=[f"mix{8 + i}", f"KEEPb{i}", f"slot{slot}", "UPAD"])
                elif role == "q":
                    if idx == 0:
                        ms_fence("sog")
                    head_q(l, idx, slot, AF_R)
                elif role == "f":
                    head_f(l, idx, slot, AF_R, LB, OML, NOML)
                elif role == "v":
                    head_v(l, idx, slot, AF_R)
                elif role == "og":
                    head_og(l, idx, slot, AF_R)
            if b == 4:
                conformer_ln(l, AF_R)
                store("sp", sc_o[l].rearrange("p (i q) -> p i q", i=4), SCm.rearrange("p (i q) -> p i q", i=4)[:, :, NS:31 * NS], ["SCm"], "st_sc")
                S.op("dve", lambda e: e.memset(FNC[:, 9:10], 0.0), reads=[], writes=["Afree", "SCm", "SSmX", "FNC9"])
                if DBG == 2:
                    dump("mixT", mixT[:, 8:16, :], [f"mix{m}" for m in range(8, 16)])
                    dump("PST", PST[:, 0:120], ["PSTa"])
                    dump("KEEP", KEEP, [f"KEEPc{i}" for i in range(4)] + [f"KEEPs{i}" for i in range(4)] + [f"KEEPb{i}" for i in range(4)])
                    dump("FIX", FIX[:, 0:8], [f"FIXu{i}" for i in range(4)])
                    store("sp", ss_o[l].rearrange("p (i q) -> p i q", i=4), SSm.rearrange("p (i q) -> p i q", i=4)[:, :, NS:3 * NS], ["SSm"], "st_ss")
                    raise _Stop()
        pop_steps(len(STEPS))
        if DBG == 3:
            dump("OL", OL, [f"OL{h}" for h in range(NH)])
            dump("SLE", SLE, [f"SLE{h}" for h in range(NH)])
            dump("OTS", OTS, [f"OTS{h}" for h in range(NH)])
            dump("QH", mixT[:, 0:8, 0:T], [f"mix{h}" for h in range(NH)])
            dump("SOG", SOG, [f"SOG{h}" for h in range(NH)])
            raise _Stop()
        tail_and_rest(l, AF_R, xold_src)
        if DBG == 6 or DBG == 4:
            raise _Stop()

    rmsnorm(4, lambda dc: xT[:, dc, :], "xT")
    store("sp", yT_o, xT, ["xT"], "st_y")
    return nc, S, out_streams


def emit(nc, S, out_streams):
    S.finalize()
    import contextlib
    with contextlib.ExitStack() as st:
        sems = {}
        for e in S.ops:
            sems[e] = st.enter_context(nc.semaphore(f"s_{e}"))
        for k in S.streams:
            sems[k] = st.enter_context(nc.semaphore(f"d_{k}"))
        block = st.enter_context(nc.Block())

        def wval(k, i):
            if k in S.streams:
                return (i + 1) if k.startswith("cc") else 16 * (i + 1)
            return S.sigcount[k][i]

        def run(engname, eng):
            for rec in S.ops[engname]:
                for (k, i) in rec["waits"]:
                    eng.wait_ge(sems[k], wval(k, i))
                ins = rec["fn"](eng)
                if rec["dma"] is not None:
                    ins.then_inc(sems[rec["dma"]], 1 if rec["dma"].startswith("cc") else 16)
                elif rec["sig"]:
                    ins.then_inc(sems[engname], 1)
            if engname == "sp":
                for k in out_streams:
                    eng.wait_ge(sems[k], 16 * S.streams[k])

        @block.sync
        def _(e):
            run("sp", e)

        @block.scalar
        def _(e):
            run("act", e)

        @block.vector
        def _(e):
            run("dve", e)

        @block.gpsimd
        def _(e):
            run("pool", e)

        @block.tensor
        def _(e):
            run("pe", e)


_CACHE = {}


def _pack_weights(w_in, w_out, w_up, w_down):
    wall = np.empty((DEPTH * NBLK, 128, WBLK), np.float32)
    cols = np.concatenate([np.arange(_COL0[r] + 128 * i, _COL0[r] + 128 * i + 128) for (r, i) in ROLES])

    def lhs_blocks(Wm, nblk):
        ncols = Wm.shape[1]
        t = Wm.reshape(KC, 128, ncols // 512, 4, 128)
        return np.ascontiguousarray(t.transpose(2, 1, 3, 0, 4)).reshape(ncols // 512, 128, WBLK)
    for l in range(DEPTH):
        base = l * NBLK
        wall[base:base + 13] = lhs_blocks(w_in[l][:, cols], 13)
        wall[base + 13:base + 17] = lhs_blocks(w_out[l], 4)
        wall[base + 17:base + 33] = lhs_blocks(w_up[l], 16)
        t = w_down[l].reshape(16, 4, 128, 16, 128)
        wall[base + 33:base + 49] = np.ascontiguousarray(t.transpose(0, 2, 3, 1, 4)).reshape(16, 128, WBLK)
    return wall


def kernel(x_prompt, x_sample, state_hgrn, state_sconv, state_cconv, g_mix, w_in, hgrn_lb, hgrn_norm_g, sconv_w,
           cconv_w, cconv_b, cconv_ln_g, cconv_ln_b, w_out, g_mlp, w_up, w_down, g_final):
    f = lambda a: np.asarray(a, dtype=np.float32)
    x_prompt, x_sample = f(x_prompt), f(x_sample)
    wall = _pack_weights(f(w_in), f(w_out), f(w_up), f(w_down))
    pv = np.zeros((128, NPV), np.float32)

    def put(name, arr):
        o, w = PV[name]
        pv[:, o:o + w] = arr.reshape(128, w)
    g5 = np.stack([f(g_mix)[0], f(g_mix)[1], f(g_mlp)[0], f(g_mlp)[1], f(g_final)])
    put("g", g5.reshape(5, 16, 128).transpose(2, 0, 1))
    put("lb", f(hgrn_lb).reshape(2, 8, 128).transpose(2, 0, 1))
    put("hg", f(hgrn_norm_g).reshape(2, 8, 128).transpose(2, 0, 1))
    put("scw", f(sconv_w).reshape(2, 3, 4, 128).transpose(3, 0, 2, 1))
    put("ccw", f(cconv_w).reshape(2, 31, 4, 128).transpose(3, 0, 2, 1))
    put("ccb", f(cconv_b).reshape(2, 4, 128).transpose(2, 0, 1))
    put("clg", f(cconv_ln_g).reshape(2, 4, 128).transpose(2, 0, 1))
    put("clb", f(cconv_ln_b).reshape(2, 4, 128).transpose(2, 0, 1))
    cv = np.zeros((128, NCV), np.float32)
    cv[:, CV["ident"][0]:CV["ident"][0] + 128] = np.eye(128, dtype=np.float32)
    caus = (np.arange(64)[:, None] <= np.arange(64)[None, :]).astype(np.float32)
    cv[:, CV["causal"][0]:CV["causal"][0] + 64] = np.concatenate([caus, caus], 0)
    sm = np.ones(1024, np.float32)
    sm[::64] = 0.0
    cv[:, CV["smask"][0]:CV["smask"][0] + 1024] = sm[None, :]

    sh, ssv, scv = f(state_hgrn), f(state_sconv), f(state_cconv)
    in_maps = []
    for c in range(NCORES):
        j, half = c // 2, c % 2
        xc = np.concatenate([x_prompt[j, half * T:(half + 1) * T], x_sample[NS * c:NS * (c + 1), 0]], 0)
        xTc = np.ascontiguousarray(xc.reshape(TT, KC, 128).transpose(2, 1, 0))
        pvc = pv.copy()
        pvc[:, PV["flag"][0]] = float(half)
        shc = np.ascontiguousarray(sh[:, NS * c:NS * (c + 1)].transpose(0, 2, 3, 1, 4)).reshape(DEPTH, NH, 128, NS * 128)
        ssc = np.ascontiguousarray(ssv[:, NS * c:NS * (c + 1)].reshape(DEPTH, NS, 2, 4, 128).transpose(0, 4, 3, 2, 1)).reshape(DEPTH, 128, 4 * NS * 2)
        scc = np.ascontiguousarray(scv[:, NS * c:NS * (c + 1)].reshape(DEPTH, NS, 30, 4, 128).transpose(0, 4, 3, 2, 1)).reshape(DEPTH, 128, 4 * NS * 30)
        in_maps.append({"xT": xTc, "wall": wall, "pv": pvc, "cv": cv, "sh": shc, "ss": ssc, "sc": scc})
    if DBG not in (5, 6, 99):
        wall = wall[:13]
        for m in in_maps:
            m["wall"] = wall
    if "nc" not in _CACHE:
        try:
            build_program()
        except _Stop:
            pass
        nc, S, outs = _CACHE["build"]
        emit(nc, S, outs)
        _CACHE["nc"] = nc
    nc = _CACHE["nc"]
    res = run_bass_kernel_spmd(nc, in_maps, core_ids=list(range(NCORES)))
    R = res.results
    _CACHE["results"] = R
    y_prompt = np.empty((4, 2048, D), np.float32)
    y_sample = np.empty((128, 1, D), np.float32)
    hp = np.empty((DEPTH, 4, NH, 128, 128), np.float32)
    sp = np.empty((DEPTH, 4, 2, 512), np.float32)
    cp = np.empty((DEPTH, 4, 30, 512), np.float32)
    hs = np.empty((DEPTH, 128, NH, 128, 128), np.float32)
    sso = np.empty((DEPTH, 128, 2, 512), np.float32)
    sco = np.empty((DEPTH, 128, 30, 512), np.float32)
    for c in range(NCORES):
        j, half = c // 2, c % 2
        yc = R[c]["yT"].transpose(2, 1, 0).reshape(TT, D)
        y_prompt[j, half * T:(half + 1) * T] = yc[:T]
        y_sample[NS * c:NS * (c + 1), 0] = yc[T:]
        hs[:, NS * c:NS * (c + 1)] = R[c]["hs"].reshape(DEPTH, NH, 128, NS, 128).transpose(0, 3, 1, 2, 4)
        sso[:, NS * c:NS * (c + 1)] = R[c]["sso"].reshape(DEPTH, 128, 4, 2, NS).transpose(0, 4, 3, 2, 1).reshape(DEPTH, NS, 2, 512)
        sco[:, NS * c:NS * (c + 1)] = R[c]["sco"].reshape(DEPTH, 128, 4, 30, NS).transpose(0, 4, 3, 2, 1).reshape(DEPTH, NS, 30, 512)
        if half == 1:
            hp[:, j] = R[c]["hp"].reshape(DEPTH, 128, NH, 128).transpose(0, 2, 1, 3)
            sp[:, j] = R[c]["sp"].reshape(DEPTH, 128, 4, 2).transpose(0, 3, 2, 1).reshape(DEPTH, 2, 512)
            cp[:, j] = R[c]["cp"].reshape(DEPTH, 128, 4, 30).transpose(0, 3, 2, 1).reshape(DEPTH, 30, 512)
    return (y_prompt, y_sample, hp, sp, cp, hs, sso, sco)
```
